# Optimizing a Trainium2 kernel written in Bass

```python
import math
import jax, jax.numpy as jnp
from jax import lax
import numpy as np

D_MODEL = 2048
BATCH = 1
SEQ = 8192
DEPTH = 1

CHUNK = 64
Q_BLOCK = 128
D_MIX = D_MODEL
D_ATTN = D_MIX // 2
D_SSM = D_MIX - D_ATTN
N_HEADS = 8
DV = D_ATTN // N_HEADS
DK = DV // 2
SSM_CH = 16
SSM_GROUPS = D_SSM // SSM_CH
SSM_STATE = 64
D_FF = 5632
CONV_W = 3
PLE_DIM = 256
LN_EPS = 1e-5
NEG_INF = -1e30
DEEPNORM_ALPHA = (2.0 * DEPTH) ** 0.25
DEEPNORM_BETA = (8.0 * DEPTH) ** -0.25
D_IN = 2 * D_ATTN + 2 * D_ATTN + D_ATTN + D_SSM

kernel_name = "hybrid_diffattn_s5_convffn_deepnorm"


def _q_width():
    return N_HEADS * 2 * DK


def _proj_width():
    return 2 * _q_width() + N_HEADS * DV + D_SSM


def layer_norm(x, g, b):
    xf = x.astype(jnp.float32)
    mu = jnp.mean(xf, axis=-1, keepdims=True)
    var = jnp.mean(jnp.square(xf - mu), axis=-1, keepdims=True)
    y = (xf - mu) * lax.rsqrt(var + LN_EPS)
    return (y * g.astype(jnp.float32) + b.astype(jnp.float32)).astype(x.dtype)


def lambda_init_fn(layer_idx):
    return 0.8 - 0.6 * math.exp(-0.3 * layer_idx)


def diff_attention(q, k, v, lam, lam_init, g_subln):
    b_, s_, h_ = q.shape[0], q.shape[1], q.shape[2]
    nb = s_ // Q_BLOCK
    scale = DK ** -0.5
    slopes = 2.0 ** (-8.0 * jnp.arange(1, h_ + 1, dtype=jnp.float32) / h_)
    k_pos = jnp.arange(s_)
    q_blocks = jnp.moveaxis(q.reshape(b_, nb, Q_BLOCK, h_, 2, DK), 1, 0)

    def one_block(args):
        q_blk, blk = args
        q_pos = blk * Q_BLOCK + jnp.arange(Q_BLOCK)
        sc = jnp.einsum('bqhmd,bkhmd->bhmqk', q_blk, k,
                        preferred_element_type=jnp.float32) * scale
        dist = jnp.abs(q_pos[:, None] - k_pos[None, :]).astype(jnp.float32)
        bias = -slopes[:, None, None, None] * dist[None, None]
        allowed = (k_pos // CHUNK)[None, :] <= (q_pos // CHUNK)[:, None]
        sc = jnp.where(allowed, sc + bias, NEG_INF)
        pr = jax.nn.softmax(sc, axis=-1)
        att = pr[:, :, 0] - lam * pr[:, :, 1]
        return jnp.einsum('bhqk,bkhd->bqhd', att.astype(v.dtype), v)

    out = lax.map(one_block, (q_blocks, jnp.arange(nb)))
    out = jnp.moveaxis(out, 0, 1).reshape(b_, s_, h_, DV)
    of = out.astype(jnp.float32)
    of = of * lax.rsqrt(jnp.mean(jnp.square(of), axis=-1, keepdims=True) + LN_EPS)
    of = of * g_subln.astype(jnp.float32) * (1.0 - lam_init)
    return of.reshape(b_, s_, h_ * DV).astype(v.dtype)


def s5_mixer(u, a_re, a_im, log_dt, b_re, b_im, c_re, c_im, d_skip, w_glu, b_glu):
    b_, s_ = u.shape[0], u.shape[1]
    ug = u.reshape(b_, s_, SSM_GROUPS, SSM_CH).astype(jnp.float32)
    ar, ai = a_re.astype(jnp.float32), a_im.astype(jnp.float32)
    dt = jnp.exp(log_dt.astype(jnp.float32))[:, None]
    mag = jnp.exp(dt * ar)
    lb_re, lb_im = mag * jnp.cos(dt * ai), mag * jnp.sin(dt * ai)
    den = ar * ar + ai * ai
    n_re, n_im = lb_re - 1.0, lb_im
    coef_re = (n_re * ar + n_im * ai) / den
    coef_im = (n_im * ar - n_re * ai) / den
    br, bi = b_re.astype(jnp.float32), b_im.astype(jnp.float32)
    bb_re = coef_re[..., None] * br - coef_im[..., None] * bi
    bb_im = coef_re[..., None] * bi + coef_im[..., None] * br
    bu_re = jnp.einsum('bsgc,gnc->bsgn', ug, bb_re)
    bu_im = jnp.einsum('bsgc,gnc->bsgn', ug, bb_im)
    shape = bu_re.shape
    la_re = jnp.broadcast_to(lb_re, shape)
    la_im = jnp.broadcast_to(lb_im, shape)

    def combine(e1, e2):
        a1r, a1i, b1r, b1i = e1
        a2r, a2i, b2r, b2i = e2
        return (a2r * a1r - a2i * a1i,
                a2r * a1i + a2i * a1r,
                a2r * b1r - a2i * b1i + b2r,
                a2r * b1i + a2i * b1r + b2i)

    _, _, xs_re, xs_im = lax.associative_scan(combine, (la_re, la_im, bu_re, bu_im), axis=1)
    y = (jnp.einsum('bsgn,gcn->bsgc', xs_re, c_re.astype(jnp.float32))
         - jnp.einsum('bsgn,gcn->bsgc', xs_im, c_im.astype(jnp.float32))
         + d_skip.astype(jnp.float32).reshape(SSM_GROUPS, SSM_CH) * ug)
    y = jax.nn.gelu(y.reshape(b_, s_, D_SSM)).astype(u.dtype)
    return y * jax.nn.sigmoid(y @ w_glu + b_glu)


def causal_depthwise_conv(h, w, b):
    s_ = h.shape[1]
    hp = jnp.pad(h, ((0, 0), (CONV_W - 1, 0), (0, 0)))
    out = b
    for j in range(CONV_W):
        out = out + hp[:, j:j + s_] * w[j]
    return out


def setup_inputs(seed: int = 0) -> dict:
    key = jax.random.key(seed)
    ks = iter(jax.random.split(key, 40))
    f32 = jnp.float32
    L = DEPTH

    def nrm(shape, scale):
        return jax.random.normal(next(ks), shape, f32) * scale

    def gain(shape):
        return 1.0 + nrm(shape, 0.02)

    n_idx = jnp.arange(SSM_STATE, dtype=f32)
    return {
        "x": nrm((BATCH, SEQ, D_MODEL), 1.0),
        "p": nrm((DEPTH, BATCH, SEQ, PLE_DIM), 1.0),
        "ln_in_g": gain((D_MODEL,)),
        "ln_in_b": nrm((D_MODEL,), 0.02),
        "w_in": nrm((L, D_MODEL, _proj_width()), D_MODEL ** -0.5),
        "lambda_q1": nrm((L, DK), 0.1),
        "lambda_k1": nrm((L, DK), 0.1),
        "lambda_q2": nrm((L, DK), 0.1),
        "lambda_k2": nrm((L, DK), 0.1),
        "g_subln": gain((L, DV)),
        "a_re": -0.5 + nrm((L, SSM_GROUPS, SSM_STATE), 0.01),
        "a_im": math.pi * n_idx + nrm((L, SSM_GROUPS, SSM_STATE), 0.01),
        "log_dt": jax.random.uniform(next(ks), (L, SSM_GROUPS), f32,
                                      math.log(1e-3), math.log(1e-1)),
        "b_re": nrm((L, SSM_GROUPS, SSM_STATE, SSM_CH), (2.0 * SSM_CH) ** -0.5),
        "b_im": nrm((L, SSM_GROUPS, SSM_STATE, SSM_CH), (2.0 * SSM_CH) ** -0.5),
        "c_re": nrm((L, SSM_GROUPS, SSM_CH, SSM_STATE), (2.0 * SSM_STATE) ** -0.5),
        "c_im": nrm((L, SSM_GROUPS, SSM_CH, SSM_STATE), (2.0 * SSM_STATE) ** -0.5),
        "d_skip": nrm((L, D_SSM), 1.0),
        "w_glu": nrm((L, D_SSM, D_SSM), D_SSM ** -0.5),
        "b_glu": nrm((L, D_SSM), 0.02),
        "w_o": nrm((L, D_ATTN + D_SSM, D_MODEL), DEEPNORM_BETA * (D_ATTN + D_SSM) ** -0.5),
        "ln1_g": gain((L, D_MODEL)),
        "ln1_b": nrm((L, D_MODEL), 0.02),
        "w_up": nrm((L, D_MODEL, 2 * D_FF), D_MODEL ** -0.5),
        "conv_w": nrm((L, CONV_W, 2 * D_FF), CONV_W ** -0.5),
        "conv_b": nrm((L, 2 * D_FF), 0.02),
        "w_down": nrm((L, D_FF, D_MODEL), DEEPNORM_BETA * D_FF ** -0.5),
        "w_ple": nrm((L, PLE_DIM, D_MODEL), PLE_DIM ** -0.5),
        "w_pg": nrm((L, D_MODEL, D_MODEL), D_MODEL ** -0.5),
        "b_pg": nrm((L, D_MODEL), 0.02),
        "ln2_g": gain((L, D_MODEL)),
        "ln2_b": nrm((L, D_MODEL), 0.02),
    }


def reference(x, p, ln_in_g, ln_in_b, w_in, lambda_q1, lambda_k1, lambda_q2, lambda_k2,
              g_subln, a_re, a_im, log_dt, b_re, b_im, c_re, c_im, d_skip, w_glu, b_glu,
              w_o, ln1_g, ln1_b, w_up, conv_w, conv_b, w_down, w_ple, w_pg, b_pg,
              ln2_g, ln2_b):
    b_, s_ = x.shape[0], x.shape[1]
    qw = _q_width()
    h = layer_norm(x, ln_in_g, ln_in_b)
    for i in range(DEPTH):
        lam_init = lambda_init_fn(i)
        z = h @ w_in[i]
        q = z[..., :qw].reshape(b_, s_, N_HEADS, 2, DK)
        k = z[..., qw:2 * qw].reshape(b_, s_, N_HEADS, 2, DK)
        v = z[..., 2 * qw:2 * qw + D_ATTN].reshape(b_, s_, N_HEADS, DV)
        u = z[..., 2 * qw + D_ATTN:]
        lam = (jnp.exp(jnp.sum(lambda_q1[i].astype(jnp.float32) * lambda_k1[i].astype(jnp.float32)))
               - jnp.exp(jnp.sum(lambda_q2[i].astype(jnp.float32) * lambda_k2[i].astype(jnp.float32)))
               + lam_init)
        attn_out = diff_attention(q, k, v, lam, lam_init, g_subln[i])
        ssm_out = s5_mixer(u, a_re[i], a_im[i], log_dt[i], b_re[i], b_im[i],
                           c_re[i], c_im[i], d_skip[i], w_glu[i], b_glu[i])
        mix = jnp.concatenate([attn_out, ssm_out], axis=-1) @ w_o[i]
        h = layer_norm(DEEPNORM_ALPHA * h + mix, ln1_g[i], ln1_b[i])
        hid = causal_depthwise_conv(h @ w_up[i], conv_w[i], conv_b[i])
        val, gate = hid[..., :D_FF], hid[..., D_FF:]
        ffn = (val * jax.nn.gelu(gate)) @ w_down[i]
        ple = (p[i] @ w_ple[i]) * jax.nn.sigmoid(h @ w_pg[i] + b_pg[i])
        h = layer_norm(DEEPNORM_ALPHA * h + ffn + ple, ln2_g[i], ln2_b[i])
    return h
```

```python
import math
from contextlib import ExitStack

import numpy as np
import concourse.bass as bass
import concourse.mybir as mybir
from concourse.bass_utils import run_bass_kernel_spmd

F32 = mybir.dt.float32
BF16 = mybir.dt.bfloat16
AF = mybir.ActivationFunctionType
ALU = mybir.AluOpType

NCORES = 8
S = 8192
D = 2048
OWN = 1024
NQ = 1040
HO = 16
QB = [(14, 16), (16, 528), (528, 1040)]
DFF = 5632
ALPHA = 2.0 ** 0.25
EPS = 1e-5
LAM_INIT = 0.8 - 0.6 * math.exp(0.0)
NEG = -1.0e30
DEBUG = False
PH1_BLOCKS = 16
NO_STORE = False
NO_HALO = False
STOP_AFTER = None


class Prog:
    ENG = ("pe", "act", "dve", "pool", "sp")

    def __init__(self, nc, stack, same_engine_sync=True):
        self.nc = nc
        self.stack = stack
        self.e = {"pe": nc.tensor, "act": nc.scalar, "dve": nc.vector, "pool": nc.gpsimd, "sp": nc.sync}
        self.sem = {k: stack.enter_context(nc.semaphore("s_" + k)) for k in ("pe", "act", "dve", "pool")}
        self.cnt = {k: 0 for k in self.sem}
        self.dsem = {}
        self.dcnt = {}
        self.seen = {}
        self.lastw = {}
        self.readers = {}
        self.ses = same_engine_sync
        self.nops = 0

    def _semof(self, tok):
        kind, src, val = tok
        return (self.sem[src] if kind == "eng" else self.dsem[src]), val

    def _wait(self, eng, tok):
        kind, src, val = tok
        if kind == "eng" and src == eng and (eng == "pe" or not self.ses):
            return
        key = (eng, kind, src)
        if self.seen.get(key, 0) >= val:
            return
        self.seen[key] = val
        s, v = self._semof(tok)
        self.e[eng].wait_ge(s, v)

    def _deps(self, eng, reads, writes):
        for b in reads:
            t = self.lastw.get(b)
            if t is not None:
                self._wait(eng, t)
        for b in writes:
            t = self.lastw.get(b)
            if t is not None:
                self._wait(eng, t)
            for t in self.readers.get(b, ()):
                self._wait(eng, t)

    def _commit(self, tok, reads, writes):
        for b in writes:
            self.lastw[b] = tok
            self.readers[b] = []
        for b in reads:
            r = self.readers.setdefault(b, [])
            r[:] = [t for t in r if (t[0], t[1]) != (tok[0], tok[1])]
            r.append(tok)

    def op(self, eng, fn, reads=(), writes=()):
        for k in reads:
            if isinstance(k, str) and k[:2] == "ps" and k[2:].isdigit():
                for t in self.readers.get(k, ()):
                    if t[1] != eng:
                        self._wait(eng, t)
        self._deps(eng, reads, writes)
        self.cnt[eng] += 1
        tok = ("eng", eng, self.cnt[eng])
        fn(self.e[eng]).then_inc(self.sem[eng], 1)
        self._commit(tok, reads, writes)
        self.nops += 1
        return tok

    def dma(self, slot, out, in_, reads=(), writes=(), q="sp", **kw):
        if slot not in self.dsem:
            self.dsem[slot] = self.stack.enter_context(self.nc.semaphore("d_" + slot))
            self.dcnt[slot] = 0
        self._deps(q, reads, writes)
        self.dcnt[slot] += 16
        tok = ("dma", slot, self.dcnt[slot])
        self.e[q].dma_start(out=out, in_=in_, **kw).then_inc(self.dsem[slot], 16)
        self._commit(tok, reads, writes)
        self.nops += 1
        return tok

    def barrier(self):
        toks = [("eng", k, self.cnt[k]) for k in self.sem if self.cnt[k] > 0]
        toks += [("dma", s, c) for s, c in self.dcnt.items()]
        for eng in self.ENG:
            for t in toks:
                self._wait(eng, t)
        self.lastw.clear()
        self.readers.clear()

    def finish(self):
        for k in self.sem:
            if self.cnt[k]:
                self._wait("sp", ("eng", k, self.cnt[k]))
        for s, c in self.dcnt.items():
            self._wait("sp", ("dma", s, c))


def build_program():
    nc = bass.Bass("TRN2", target_bir_lowering=False)

    def din(name, shape, dt=F32):
        return nc.dram_tensor(name, list(shape), dt, kind="ExternalInput").ap()

    def dscr(name, shape, dt):
        return nc.dram_tensor(name, list(shape), dt, kind="Internal").ap()

    xc = din("xc", [S, D])
    pc = din("pc", [OWN, 256])
    w_in = din("w_in", [D, 4096])
    w_glu = din("w_glu", [1024, 1024])
    w_o = din("w_o", [D, D])
    w_up = din("w_up", [D, 2 * DFF])
    w_down = din("w_down", [DFF, D])
    w_ple = din("w_ple", [256, D])
    w_pg = din("w_pg", [D, D])
    vecs = din("vecs", [128, 16 * 9])
    convp = din("convp", [128, 4 * 88])
    lamv = din("lamv", [128, 4 * 64])
    ident_d = din("ident", [128, 128])
    kbias_d = din("kbias", [128, 8 * 64 * 9])
    dt_d = din("dtile", [128, 8 * 128])
    valid_d = din("valid", [128, 17])
    ssmC = din("ssmC", [128, 5 * 512])
    ssmS = din("ssmS", [128, 7 * 512])
    out_d = nc.dram_tensor("out", [OWN, D], F32, kind="ExternalOutput").ap()
    if DEBUG:
        dbg_cat = nc.dram_tensor("dbg_cat", [16, 128, NQ], F32, kind="ExternalOutput").ap()

    KT_scr = dscr("KT_scr", [8, 128, S], BF16)
    V_scr = dscr("V_scr", [64, 128, 1024], BF16)
    UT_scr = dscr("UT_scr", [8, 128, S], BF16)
    H0_scr = dscr("H0_scr", [16, 128, NQ], F32)
    XT_scr = dscr("XT_scr", [16, 128, NQ], BF16)

    with ExitStack() as gs:
        P = Prog(nc, gs)

        def sb(st, name, shape, dt):
            return st.enter_context(nc.sbuf_tensor("sb_" + name, list(shape), dt))

        ps = [gs.enter_context(nc.psum_tensor(f"ps{i}", [128, 512], F32)) for i in range(8)]
        psk = [f"ps{i}" for i in range(8)]

        ident = sb(gs, "ident", [128, 128], F32)
        identb = sb(gs, "identb", [128, 128], BF16)
        onesb = sb(gs, "onesb", [128, 128], BF16)
        onesf = sb(gs, "onesf", [128, 128], F32)
        vec = sb(gs, "vec", [128, 144], F32)
        valid = sb(gs, "valid", [128, 17], F32)
        lamcol = sb(gs, "lamcol", [128, 4], F32)
        halo_f = sb(gs, "halo_f", [128, 16, 16], F32)
        halo_b = sb(gs, "halo_b", [128, 16, 16], BF16)
        P.dma("c0", ident[:], ident_d[:, :], writes=["ident"])
        P.dma("c1", vec[:], vecs[:, :], writes=["vec"])
        P.dma("c2", valid[:], valid_d[:, :], writes=["valid"])
        P.op("dve", lambda e: e.tensor_copy(out=identb[:], in_=ident[:]), reads=["ident"], writes=["identb"])
        P.op("dve", lambda e: e.memset(onesb[:], 1.0), writes=["onesb"])
        P.op("dve", lambda e: e.memset(onesf[:], 1.0), writes=["onesf"])
        G_IN, B_IN, G1, B1, G2, B2, BPG, MISC = [vec[:, i * 16:(i + 1) * 16] for i in range(8)]

        with ExitStack() as s0:
            lv = sb(s0, "lv", [128, 4, 64], F32)
            lt = sb(s0, "lt", [128, 64], F32)
            ld = sb(s0, "ld", [128, 2], F32)
            P.dma("c3", lv[:].rearrange("p a b -> p (a b)"), lamv[:, :], writes=["lv"])
            for i in range(2):
                P.op("dve", lambda e, i=i: e.tensor_tensor(out=lt[:], in0=lv[:, 2 * i, :], in1=lv[:, 2 * i + 1, :], op=ALU.mult),
                     reads=["lv"], writes=["lt"])
                P.op("dve", lambda e, i=i: e.tensor_reduce(out=ld[:, i:i + 1], in_=lt[:], axis=mybir.AxisListType.X, op=ALU.add),
                     reads=["lt"], writes=[("ld", i)])
            P.op("act", lambda e: e.activation(out=ld[:], in_=ld[:], func=AF.Exp), reads=[("ld", 0), ("ld", 1)], writes=["ld"])
            P.op("dve", lambda e: e.tensor_tensor(out=lamcol[:, 0:1], in0=ld[:, 1:2], in1=ld[:, 0:1], op=ALU.subtract),
                 reads=["ld"], writes=["lamcol"])
            P.op("dve", lambda e: e.tensor_scalar(out=lamcol[:, 0:1], in0=lamcol[:, 0:1], scalar1=-LAM_INIT, scalar2=None, op0=ALU.add),
                 reads=["lamcol"], writes=["lamcol"])
            P.barrier()

        rr = {"ps": 0, "ev": 0}

        def evac_eng():
            rr["ev"] += 1
            return "dve" if rr["ev"] % 2 else "act"

        def affine_evac(eng, out, in_, scale=None, bias=None, reads=(), writes=()):
            if eng == "act":
                kw = {}
                if scale is not None:
                    kw["scale"] = scale
                if bias is not None:
                    kw["bias"] = bias
                P.op("act", lambda e: e.activation(out=out, in_=in_, func=AF.Identity, **kw), reads=reads, writes=writes)
            else:
                if scale is None and bias is None:
                    P.op("dve", lambda e: e.tensor_copy(out=out, in_=in_), reads=reads, writes=writes)
                elif bias is None:
                    P.op("dve", lambda e: e.tensor_scalar(out=out, in0=in_, scalar1=scale, scalar2=None, op0=ALU.mult), reads=reads, writes=writes)
                elif scale is None:
                    P.op("dve", lambda e: e.tensor_scalar(out=out, in0=in_, scalar1=bias, scalar2=None, op0=ALU.add), reads=reads, writes=writes)
                else:
                    P.op("dve", lambda e: e.tensor_scalar(out=out, in0=in_, scalar1=scale, scalar2=bias, op0=ALU.mult, op1=ALU.add), reads=reads, writes=writes)

        with ExitStack() as s1:
            W = sb(s1, "W1", [128, 16, 3072], BF16)
            for kt in range(16):
                for cb in range(3):
                    P.dma(f"w1_{kt}", W[:, kt, cb * 1024:(cb + 1) * 1024],
                          w_in[kt * 128:(kt + 1) * 128, 1024 + cb * 1024:1024 + (cb + 1) * 1024],
                          writes=[("W1", kt)], q="pool")
            xq = [sb(s1, f"xq{i}", [128, D], F32) for i in range(2)]
            xh = [sb(s1, f"xh{i}", [128, D], F32) for i in range(4)]
            h0T = [sb(s1, "h0T0", [128, 16, 512], BF16)]
            stats = [sb(s1, f"stats{i}", [128, 4, 6], F32) for i in range(2)]
            mv = [sb(s1, f"mv{i}", [128, 2], F32) for i in range(2)]
            rstd = [sb(s1, f"rstd{i}", [128, 1], F32) for i in range(2)]
            nmr = [sb(s1, f"nmr{i}", [128, 1], F32) for i in range(2)]
            stg = [sb(s1, f"stg{i}", [128, 512], BF16) for i in range(6)]
            h0f = [sb(s1, f"h0f{i}", [128, 512], F32) for i in range(2)]
            nstg = 0
            nx = 0
            for tb in range(16 - PH1_BLOCKS, 16):
                hb = 0
                r0 = tb * 512
                for tt in range(4):
                    xi = nx % 2
                    nx += 1
                    P.dma(f"x{xi}", xq[xi][:], xc[r0 + tt * 128:r0 + (tt + 1) * 128, :], writes=[("xq", xi)])
                    for c in range(4):
                        P.op("dve", lambda e, c=c, xi=xi: e.bn_stats(out=stats[xi][:, c, :], in_=xq[xi][:, c * 512:(c + 1) * 512]),
                             reads=[("xq", xi)], writes=[("stats", xi, c)])
                    P.op("dve", lambda e, xi=xi: e.bn_aggr(out=mv[xi][:], in_=stats[xi][:].rearrange("p a b -> p (a b)")),
                         reads=[("stats", xi, c) for c in range(4)], writes=[("mv", xi)])
                    P.op("act", lambda e, xi=xi: e.activation(out=rstd[xi][:], in_=mv[xi][:, 1:2], func=AF.Sqrt, bias=EPS, scale=1.0),
                         reads=[("mv", xi)], writes=[("rstd", xi)])
                    P.op("dve", lambda e, xi=xi: e.reciprocal(out=rstd[xi][:], in_=rstd[xi][:]), reads=[("rstd", xi)], writes=[("rstd", xi)])
                    P.op("dve", lambda e, xi=xi: e.scalar_tensor_tensor(out=nmr[xi][:], in0=mv[xi][:, 0:1], scalar=-1.0, in1=rstd[xi][:],
                                                                        op0=ALU.mult, op1=ALU.mult),
                         reads=[("mv", xi), ("rstd", xi)], writes=[("nmr", xi)])
                    P.op("act", lambda e, xi=xi, tt=tt: e.activation(out=xh[tt][:], in_=xq[xi][:], func=AF.Identity, bias=nmr[xi][:], scale=rstd[xi][:]),
                         reads=[("xq", xi), ("nmr", xi), ("rstd", xi)], writes=[("xh", tt)])
                for kt in range(16):
                    b = kt % 2
                    for tt in range(4):
                        P.op("pe", lambda e, kt=kt, tt=tt, b=b: e.transpose(out=ps[b][:, tt * 128:(tt + 1) * 128],
                                                                          in_=xh[tt][:, kt * 128:(kt + 1) * 128], identity=ident[:]),
                             reads=[("xh", tt), "ident"], writes=[psk[b]])
                    affine_evac(evac_eng(), h0T[hb][:, kt, :], ps[b][:], scale=G_IN[:, kt:kt + 1], bias=B_IN[:, kt:kt + 1],
                                reads=[psk[b], "vec"], writes=[("h0T", hb, kt)])
                    if tb == 13:
                        affine_evac("dve", halo_f[:, kt, :], ps[b][:, 496:512], scale=G_IN[:, kt:kt + 1], bias=B_IN[:, kt:kt + 1],
                                    reads=[psk[b], "vec"], writes=[("halo_f", kt)])
                        P.op("dve", lambda e, kt=kt: e.tensor_copy(out=halo_b[:, kt, :], in_=h0T[hb][:, kt, 496:512]),
                             reads=[("h0T", hb, kt)], writes=[("halo_b", kt)])
                    if tb >= 14:
                        fi = kt % 2
                        affine_evac("dve", h0f[fi][:], ps[b][:], scale=G_IN[:, kt:kt + 1], bias=B_IN[:, kt:kt + 1],
                                    reads=[psk[b], "vec"], writes=[("h0f", fi)])
                        dc = HO + (tb - 14) * 512
                        P.dma(f"h0s{fi}", H0_scr[kt, :, dc:dc + 512], h0f[fi][:], reads=[("h0f", fi)], writes=[("H0", kt, tb)])
                        P.dma(f"xts{fi}", XT_scr[kt, :, dc:dc + 512], h0T[hb][:, kt, :], reads=[("h0T", hb, kt)], writes=[("XT", kt, tb)])
                hreads = [("h0T", hb, kt) for kt in range(16)]

                def proj_fm(col0, dst, dkey, scale=None):
                    nonlocal nstg
                    b = 2 + rr["ps"] % 6
                    rr["ps"] += 1
                    for kt in range(16):
                        P.op("pe", lambda e, kt=kt, b=b: e.matmul(ps[b][:], W[:, kt, col0:col0 + 128], h0T[hb][:, kt, :],
                                                                  start=(kt == 0), stop=(kt == 15)),
                             reads=[("W1", kt), ("h0T", hb, kt)], writes=[psk[b]])
                    si = nstg % 6
                    nstg += 1
                    affine_evac(evac_eng(), stg[si][:], ps[b][:], scale=scale, reads=[psk[b], "valid"], writes=[("stg", si)])
                    if not NO_STORE:
                        P.dma(f"stg{si}", dst, stg[si][:], reads=[("stg", si)], writes=[dkey])

                for h in range(8):
                    proj_fm(h * 128, KT_scr[h, :, r0:r0 + 512], ("KT", h, tb))
                for t in range(8):
                    proj_fm(2048 + t * 128, UT_scr[t, :, r0:r0 + 512], ("UT", t, tb), scale=valid[:, tb:tb + 1])
                for tt in range(4):
                    for fh in range(2):
                        b = 2 + rr["ps"] % 6
                        rr["ps"] += 1
                        for kt in range(16):
                            P.op("pe", lambda e, kt=kt, b=b, tt=tt, fh=fh: e.matmul(
                                ps[b][:], h0T[hb][:, kt, tt * 128:(tt + 1) * 128], W[:, kt, 1024 + fh * 512:1024 + (fh + 1) * 512],
                                start=(kt == 0), stop=(kt == 15)),
                                reads=[("W1", kt), ("h0T", hb, kt)], writes=[psk[b]])
                        si = nstg % 6
                        nstg += 1
                        affine_evac(evac_eng(), stg[si][:], ps[b][:], reads=[psk[b]], writes=[("stg", si)])
                        if not NO_STORE:
                            P.dma(f"stg{si}", V_scr[tb * 4 + tt, :, fh * 512:(fh + 1) * 512], stg[si][:], reads=[("stg", si)],
                                  writes=[("V", tb * 4 + tt, fh)])
            P.barrier()
        if STOP_AFTER == 1:
            P.finish()
            return nc

        catT = sb(gs, "catT", [128, 16, NQ], BF16)
        with ExitStack() as s2:
            qT = sb(s2, "qT", [128, 8, NQ], BF16)
            dtile = sb(s2, "dtile", [128, 8, 128], BF16)
            P.dma("dt", dtile[:].rearrange("p a b -> p (a b)"), dt_d[:, :], writes=["dtile"], q="pool")
            with ExitStack() as s2a:
                xT = sb(s2a, "xT", [128, 16, NQ], BF16)
                Wq = sb(s2a, "Wq", [128, 16, 1024], BF16)
                for kt in range(16):
                    P.dma(f"xt{kt % 4}", xT[:, kt, HO:NQ], XT_scr[kt, :, HO:NQ], writes=[("xT", kt)])
                    P.op("dve", lambda e, kt=kt: e.tensor_copy(out=xT[:, kt, 0:HO], in_=halo_b[:, kt, :]), reads=[("xT", kt)], writes=[("xT", kt)])
                    P.dma(f"wq{kt % 4}", Wq[:, kt, :], w_in[kt * 128:(kt + 1) * 128, 0:1024], writes=[("Wq", kt)], q="pool")
                for h in range(8):
                    for (c0, c1) in QB:
                        b = rr["ps"] % 8
                        rr["ps"] += 1
                        for kt in range(16):
                            P.op("pe", lambda e, kt=kt, b=b, h=h, c0=c0, c1=c1: e.matmul(
                                ps[b][:, 0:c1 - c0], Wq[:, kt, h * 128:(h + 1) * 128], xT[:, kt, c0:c1], start=(kt == 0), stop=(kt == 15)),
                                reads=[("Wq", kt), ("xT", kt)], writes=[psk[b]])
                        affine_evac(evac_eng(), qT[:, h, c0:c1], ps[b][:, 0:c1 - c0], reads=[psk[b]], writes=[("qT", h, c0)])
                P.barrier()

            kT = [sb(s2, f"kT{i}", [128, S], BF16) for i in range(2)]
            vh = [sb(s2, f"vh{i}", [128, 64, 128], BF16) for i in range(2)]
            kb = [sb(s2, f"kb{i}", [128, 64, 9], F32) for i in range(2)]
            pT = [[sb(s2, f"pT{m}_{i}", [128, 512], BF16) for i in range(3)] for m in range(2)]
            rl = [sb(s2, f"rl{m}", [128, 512], F32) for m in range(2)]
            oo = sb(s2, "oo", [128, 512], F32)
            o2 = sb(s2, "o2", [128, 512], F32)
            sq = sb(s2, "sq", [128, 512], F32)
            gs8 = sb(s2, "gs8", [128, 1], F32)
            P.op("dve", lambda e: e.tensor_scalar(out=gs8[:], in0=MISC[:, 8:9], scalar1=1.0 - LAM_INIT, scalar2=None, op0=ALU.mult),
                 reads=["vec"], writes=["gs8"])
            npt = 0
            for h in range(8):
                hb = h % 2
                for j in range(4):
                    P.dma(f"kt{hb}", kT[hb][:, j * 2048:(j + 1) * 2048], KT_scr[h, :, j * 2048:(j + 1) * 2048], writes=[("kT", hb)])
                for j in range(4):
                    P.dma(f"vh{hb}", vh[hb][:, j * 16:(j + 1) * 16, :],
                          V_scr[j * 16:(j + 1) * 16, :, h * 128:(h + 1) * 128].rearrange("k p d -> p k d"), writes=[("vh", hb)])
                P.dma(f"kb{hb}", kb[hb][:].rearrange("p a b -> p (a b)"), kbias_d[:, h * 576:(h + 1) * 576], writes=[("kb", hb)])
                for qi, (c0, c1) in enumerate(QB):
                    n = c1 - c0
                    if qi == 0:
                        subs = [(0, 2, 0, 55)]
                        klast = 55
                    else:
                        subs = [(i * 128, 128, 1 + (qi - 1) * 4 + i, 56 + (qi - 1) * 4 + i) for i in range(4)]
                        klast = 56 + (qi - 1) * 4 + 3
                    acc = [4, 5, 6, 7]
                    first = True
                    for kt in range(klast + 1):
                        act_subs = [s_ for s_ in subs if s_[3] >= kt]
                        lo = act_subs[0][0]
                        hi = act_subs[-1][0] + act_subs[-1][1]
                        pi = npt % 3
                        npt += 1
                        for m in range(2):
                            b = (npt % 2) * 2 + m
                            P.op("pe", lambda e, m=m, b=b, kt=kt, lo=lo, hi=hi: e.matmul(
                                ps[b][:, lo:hi], kT[hb][m * 64:(m + 1) * 64, kt * 128:(kt + 1) * 128],
                                qT[m * 64:(m + 1) * 64, h, c0 + lo:c0 + hi], start=True, stop=True),
                                reads=[("kT", hb)] + [("qT", h, c0)], writes=[psk[b]])
                            for (sl, sw, sid, dk) in act_subs:
                                if dk == kt:
                                    dsl = dtile[:, h, 128 - sw:128] if sw < 128 else dtile[:, h, :]
                                    P.op("pe", lambda e, b=b, sl=sl, sw=sw, dsl=dsl: e.matmul(
                                        ps[b][:, sl:sl + sw], identb[:], dsl, start=False, stop=True, skip_group_check=True),
                                        reads=["identb", "dtile"], writes=[psk[b]])
                            for (sl, sw, sid, dk) in act_subs:
                                P.op("act", lambda e, m=m, b=b, sl=sl, sw=sw, sid=sid, kt=kt, pi=pi: e.activation(
                                    out=pT[m][pi][:, sl:sl + sw], in_=ps[b][:, sl:sl + sw], func=AF.Exp,
                                    bias=kb[hb][:, kt, sid:sid + 1], scale=0.125),
                                    reads=[psk[b], ("kb", hb)], writes=[("pT", m, pi, sl)])
                        for m in range(2):
                            P.op("pe", lambda e, m=m, kt=kt, lo=lo, hi=hi, pi=pi, first=first: e.matmul(
                                ps[acc[m]][:, lo:hi], vh[hb][:, kt, :], pT[m][pi][:, lo:hi], start=first, stop=(kt == klast),
                                skip_group_check=True),
                                reads=[("vh", hb)] + [("pT", m, pi, s_[0]) for s_ in act_subs], writes=[psk[acc[m]]])
                            P.op("pe", lambda e, m=m, kt=kt, lo=lo, hi=hi, pi=pi, first=first: e.matmul(
                                ps[acc[2 + m]][:, lo:hi], onesb[:], pT[m][pi][:, lo:hi], start=first, stop=(kt == klast),
                                skip_group_check=True),
                                reads=["onesb"] + [("pT", m, pi, s_[0]) for s_ in act_subs], writes=[psk[acc[2 + m]]])
                        first = False
                    for m in range(2):
                        P.op("dve", lambda e, m=m: e.tensor_scalar(out=rl[m][:, 0:n], in0=ps[acc[2 + m]][:, 0:n], scalar1=1e-37, scalar2=None, op0=ALU.add),
                             reads=[psk[acc[2 + m]]], writes=[("rl", m)])
                        P.op("dve", lambda e, m=m: e.reciprocal(out=rl[m][:, 0:n], in_=rl[m][:, 0:n]), reads=[("rl", m)], writes=[("rl", m)])
                    P.op("dve", lambda e: e.tensor_tensor(out=oo[:, 0:n], in0=ps[acc[0]][:, 0:n], in1=rl[0][:, 0:n], op=ALU.mult),
                         reads=[psk[acc[0]], ("rl", 0)], writes=["oo"])
                    P.op("dve", lambda e: e.tensor_tensor(out=o2[:, 0:n], in0=ps[acc[1]][:, 0:n], in1=rl[1][:, 0:n], op=ALU.mult),
                         reads=[psk[acc[1]], ("rl", 1)], writes=["o2"])
                    P.op("dve", lambda e: e.scalar_tensor_tensor(out=oo[:, 0:n], in0=o2[:, 0:n], scalar=lamcol[:, 0:1], in1=oo[:, 0:n],
                                                                 op0=ALU.mult, op1=ALU.add),
                         reads=["o2", "oo", "lamcol"], writes=["oo"])
                    P.op("act", lambda e: e.activation(out=sq[:, 0:n], in_=oo[:, 0:n], func=AF.Square), reads=["oo"], writes=["sq"])
                    P.op("pe", lambda e: e.matmul(ps[0][:, 0:n], onesf[:], sq[:, 0:n], start=True, stop=True),
                         reads=["onesf", "sq"], writes=[psk[0]])
                    P.op("act", lambda e: e.activation(out=sq[:, 0:n], in_=ps[0][:, 0:n], func=AF.Sqrt, bias=EPS, scale=1.0 / 128.0),
                         reads=[psk[0]], writes=["sq"])
                    P.op("dve", lambda e: e.reciprocal(out=sq[:, 0:n], in_=sq[:, 0:n]), reads=["sq"], writes=["sq"])
                    P.op("dve", lambda e: e.tensor_tensor(out=oo[:, 0:n], in0=oo[:, 0:n], in1=sq[:, 0:n], op=ALU.mult),
                         reads=["oo", "sq"], writes=["oo"])
                    P.op("dve", lambda e, h=h, c0=c0, c1=c1: e.tensor_scalar(out=catT[:, h, c0:c1], in0=oo[:, 0:n], scalar1=gs8[:], scalar2=None, op0=ALU.mult),
                         reads=["oo", "gs8"], writes=[("catT", h, c0)])
            P.barrier()
        if STOP_AFTER == 2:
            P.finish()
            return nc

        build_rest(nc, P, gs, sb, ps, psk, rr, affine_evac, evac_eng, locals())
        P.finish()
    return nc


def build_rest(nc, P, gs, sb, ps, psk, rr, affine_evac, evac_eng, L):
    catT, vec, valid, ident, identb, onesb, onesf = (L[k] for k in ("catT", "vec", "valid", "ident", "identb", "onesb", "onesf"))
    G_IN, B_IN, G1, B1, G2, B2, BPG, MISC = (L[k] for k in ("G_IN", "B_IN", "G1", "B1", "G2", "B2", "BPG", "MISC"))
    UT_scr, H0_scr, ssmC, ssmS = (L[k] for k in ("UT_scr", "H0_scr", "ssmC", "ssmS"))
    w_glu, w_o, w_up, w_down, w_ple, w_pg, pc, convp, out_d = (L[k] for k in ("w_glu", "w_o", "w_up", "w_down", "w_ple", "w_pg", "pc", "convp", "out_d"))
    PI = math.pi
    LC = 8
    NCH = S // LC
    OWNCH = 130
    CTX = NCH - OWNCH

    def TT(o, a, b, op, key):
        P.op("dve", lambda e: e.tensor_tensor(out=o, in0=a, in1=b, op=op), reads=[key], writes=[key])

    def TS(o, a, s1, s2, op0, op1, key):
        if s2 is None:
            P.op("dve", lambda e: e.tensor_scalar(out=o, in0=a, scalar1=s1, scalar2=None, op0=op0), reads=[key], writes=[key])
        else:
            P.op("dve", lambda e: e.tensor_scalar(out=o, in0=a, scalar1=s1, scalar2=s2, op0=op0, op1=op1), reads=[key], writes=[key])

    def STT(o, a, sc, b, op0, op1, key, extra_r=()):
        P.op("dve", lambda e: e.scalar_tensor_tensor(out=o, in0=a, scalar=sc, in1=b, op0=op0, op1=op1), reads=[key] + list(extra_r), writes=[key])

    def ACT(o, a, func, key, **kw):
        P.op("act", lambda e: e.activation(out=o, in_=a, func=func, **kw), reads=[key], writes=[key])

    ygT = sb(gs, "ygT", [128, 8, NQ], BF16)

    with ExitStack() as s3:
        GFb = sb(s3, "GFb", [128, LC, 2, 512], BF16)
        HSb = sb(s3, "HSb", [128, LC + 1, 2, 512], BF16)
        BSb = sb(s3, "BSb", [128, 2, 512], BF16)
        DL = sb(s3, "DL", [128, 11, 2, 32], F32)
        LAMS = sb(s3, "LAMS", [128, 2, 32], F32)
        K = "ssm"

        def lam_calc(st, F, ar, ai, ldt, pre):
            t = {n: sb(st, pre + n, [128, F], F32) for n in ("dt", "mag", "ang", "m", "s", "c", "lr", "li", "den", "cr", "ci", "t1")}
            ACT(t["dt"][:], ldt, AF.Exp, K)
            TT(t["mag"][:], t["dt"][:], ar, ALU.mult, K)
            ACT(t["mag"][:], t["mag"][:], AF.Exp, K)
            TT(t["ang"][:], t["dt"][:], ai, ALU.mult, K)
            for j in range(7):
                TS(t["m"][:], t["ang"][:], (2 * j + 1) * PI, -2 * PI, ALU.is_ge, ALU.mult, K)
                if j == 0:
                    TT(t["s"][:], t["ang"][:], t["m"][:], ALU.add, K)
                else:
                    TT(t["s"][:], t["s"][:], t["m"][:], ALU.add, K)
            TS(t["c"][:], t["s"][:], PI / 2, None, ALU.add, None, K)
            TS(t["m"][:], t["c"][:], PI, -2 * PI, ALU.is_ge, ALU.mult, K)
            TT(t["c"][:], t["c"][:], t["m"][:], ALU.add, K)
            ACT(t["s"][:], t["s"][:], AF.Sin, K)
            ACT(t["c"][:], t["c"][:], AF.Sin, K)
            TT(t["lr"][:], t["mag"][:], t["c"][:], ALU.mult, K)
            TT(t["li"][:], t["mag"][:], t["s"][:], ALU.mult, K)
            TT(t["den"][:], ar, ar, ALU.mult, K)
            TT(t["t1"][:], ai, ai, ALU.mult, K)
            TT(t["den"][:], t["den"][:], t["t1"][:], ALU.add, K)
            P.op("dve", lambda e: e.reciprocal(out=t["den"][:], in_=t["den"][:]), reads=[K], writes=[K])
            TS(t["mag"][:], t["lr"][:], -1.0, None, ALU.add, None, K)
            TT(t["cr"][:], t["mag"][:], ar, ALU.mult, K)
            TT(t["t1"][:], t["li"][:], ai, ALU.mult, K)
            TT(t["cr"][:], t["cr"][:], t["t1"][:], ALU.add, K)
            TT(t["cr"][:], t["cr"][:], t["den"][:], ALU.mult, K)
            TT(t["ci"][:], t["li"][:], ar, ALU.mult, K)
            TT(t["t1"][:], t["mag"][:], ai, ALU.mult, K)
            TT(t["ci"][:], t["ci"][:], t["t1"][:], ALU.subtract, K)
            TT(t["ci"][:], t["ci"][:], t["den"][:], ALU.mult, K)
            return t["lr"], t["li"], t["cr"], t["ci"]

        def cmul(orr, oi, ar_, ai_, br_, bi_, t1, t2):
            TT(t1, ar_, br_, ALU.mult, K)
            TT(t2, ai_, bi_, ALU.mult, K)
            TT(t2, t1, t2, ALU.subtract, K)
            TT(t1, ar_, bi_, ALU.mult, K)
            TT(oi, ai_, br_, ALU.mult, K)
            TT(oi, oi, t1, ALU.add, K)
            P.op("dve", lambda e: e.tensor_copy(out=orr, in_=t2), reads=[K], writes=[K])

        with ExitStack() as sc:
            cs = sb(sc, "cs", [128, 5, 512], F32)
            P.dma("ssmc", cs[:].rearrange("p a b -> p (a b)"), ssmC[:, :], writes=[K])
            lr, li, cr, ci = lam_calc(sc, 512, cs[:, 0, :], cs[:, 1, :], cs[:, 4, :], "C_")
            g = [[sb(sc, f"g{i}{j}", [128, 512], F32) for j in range(2)] for i in range(2)]
            t1 = sb(sc, "ct1", [128, 512], F32)
            t2 = sb(sc, "ct2", [128, 512], F32)
            cmul(g[0][0][:], g[0][1][:], cr[:], ci[:], cs[:, 2, :], cs[:, 3, :], t1[:], t2[:])
            for e_ in range(LC):
                cur, nxt = g[e_ % 2], g[(e_ + 1) % 2]
                for part in range(2):
                    P.op("dve", lambda e, e_=e_, part=part, cur=cur: e.tensor_copy(out=GFb[:, e_, part, :], in_=cur[part][:]), reads=[K], writes=[K])
                if e_ < LC - 1:
                    cmul(nxt[0][:], nxt[1][:], lr[:], li[:], cur[0][:], cur[1][:], t1[:], t2[:])
            ss = sb(sc, "ss", [128, 7, 512], F32)
            P.dma("ssms", ss[:].rearrange("p a b -> p (a b)"), ssmS[:, :], writes=[K])
            lrs, lis, crs, cis = lam_calc(sc, 512, ss[:, 0, :], ss[:, 1, :], ss[:, 2, :], "S_")
            cmul(g[0][0][:], g[0][1][:], crs[:], cis[:], ss[:, 5, :], ss[:, 6, :], t1[:], t2[:])
            for part in range(2):
                P.op("dve", lambda e, part=part: e.tensor_copy(out=BSb[:, part, :], in_=g[0][part][:]), reads=[K], writes=[K])
            pw = [[sb(sc, f"pw{i}{j}", [128, 512], F32) for j in range(2)] for i in range(2)]
            P.op("dve", lambda e: e.memset(pw[0][0][:], 1.0), reads=[K], writes=[K])
            P.op("dve", lambda e: e.memset(pw[0][1][:], 0.0), reads=[K], writes=[K])
            for e_ in range(LC + 1):
                cur, nxt = pw[e_ % 2], pw[(e_ + 1) % 2]
                cmul(g[1][0][:], g[1][1][:], ss[:, 3, :], ss[:, 4, :], cur[0][:], cur[1][:], t1[:], t2[:])
                P.op("dve", lambda e, e_=e_: e.tensor_copy(out=HSb[:, e_, 0, :], in_=g[1][0][:]), reads=[K], writes=[K])
                TS(HSb[:, e_, 1, :], g[1][1][:], -1.0, None, ALU.mult, None, K)
                if e_ < LC:
                    cmul(nxt[0][:], nxt[1][:], lrs[:], lis[:], cur[0][:], cur[1][:], t1[:], t2[:])
            lamL = pw[LC % 2]
            v32 = lambda tl: tl[:].rearrange("p (q c) -> p q c", c=16)[:, :, 0]
            for part in range(2):
                P.op("dve", lambda e, part=part: e.tensor_copy(out=LAMS[:, part, :], in_=v32(lamL[part])), reads=[K], writes=[K])
                P.op("dve", lambda e, part=part: e.tensor_copy(out=DL[:, 0, part, :], in_=v32(lamL[part])), reads=[K], writes=[K])
            d1 = sb(sc, "d1", [128, 32], F32)
            d2 = sb(sc, "d2", [128, 32], F32)
            for k in range(10):
                cmul(DL[:, k + 1, 0, :], DL[:, k + 1, 1, :], DL[:, k, 0, :], DL[:, k, 1, :], DL[:, k, 0, :], DL[:, k, 1, :], d1[:], d2[:])
            P.barrier()

        HZ = sb(s3, "HZ", [128, 4, 2, LC + 1, 128], BF16)
        BZ = sb(s3, "BZ", [128, 4, 2, 128], BF16)
        KTb = sb(s3, "KTb", [128, LC, 128], BF16)
        UZ = [sb(s3, f"UZ{i}", [128, S], BF16) for i in range(2)]
        uown = sb(s3, "uown", [128, OWNCH * LC], BF16)
        Wr = sb(s3, "Wr", [128, 2, CTX], F32)
        junk = sb(s3, "junk", [128, CTX], F32)
        acc4 = sb(s3, "acc4", [128, 4], F32)
        xs = sb(s3, "xs", [128, 2], F32)
        scn = [sb(s3, f"scn{i}", [128, 2, OWNCH], F32) for i in range(2)]
        xpv = sb(s3, "xpv", [128, 4, 2, OWNCH], BF16)
        dm = sb(s3, "dm", [128, 128], F32)
        tcol = sb(s3, "tcol", [128, 512], F32)
        P.op("pool", lambda e: e.memset(HZ[:].rearrange("p a b c d -> p (a b c d)"), 0.0), writes=[K])
        P.op("pool", lambda e: e.memset(BZ[:].rearrange("p a b d -> p (a b d)"), 0.0), reads=[K], writes=[K])

        for t in range(8):
            for ql in range(4):
                pr = 4 * t + ql
                for gl in range(2):
                    r0, r1 = gl * 64, (gl + 1) * 64
                    c0 = 32 * ql + 16 * gl
                    for part in range(2):
                        P.op("dve", lambda e, ql=ql, part=part, r0=r0, r1=r1, c0=c0, pr=pr: e.tensor_copy(
                            out=HZ[r0:r1, ql, part, :, c0:c0 + 16], in_=HSb[r0:r1, :, part, pr * 16:(pr + 1) * 16]), reads=[K], writes=[K])
                        P.op("dve", lambda e, ql=ql, part=part, r0=r0, r1=r1, c0=c0, pr=pr: e.tensor_copy(
                            out=BZ[r0:r1, ql, part, c0:c0 + 16], in_=BSb[r0:r1, part, pr * 16:(pr + 1) * 16]), reads=[K], writes=[K])
            P.op("dve", lambda e, t=t: e.tensor_scalar(out=dm[:], in0=ident[:], scalar1=vec[:, 128 + t:129 + t], scalar2=None, op0=ALU.mult),
                 reads=[K, "vec", "ident"], writes=[K])
            for d in range(LC):
                b = d % 2
                n = 0
                for ql in range(4):
                    for part in range(2):
                        P.op("pe", lambda e, ql=ql, part=part, d=d, b=b, n=n: e.matmul(
                            ps[b][:, 0:128], BZ[:, ql, part, :], HZ[:, ql, part, d, :], start=(n == 0), stop=(n == 7)),
                            reads=[K], writes=[psk[b]])
                        n += 1
                if d == 0:
                    P.op("dve", lambda e, b=b: e.tensor_tensor(out=KTb[:, 0, :], in0=ps[b][:, 0:128], in1=dm[:], op=ALU.add), reads=[K, psk[b]], writes=[K])
                else:
                    P.op("dve", lambda e, b=b, d=d: e.tensor_copy(out=KTb[:, d, :], in_=ps[b][:, 0:128]), reads=[K, psk[b]], writes=[K])
            P.dma("uown", uown[:], UT_scr[t, :, S - OWNCH * LC:S], writes=["uown"])
            for ql in range(4):
                pr = 4 * t + ql
                for gl in range(2):
                    gp = 2 * ql + gl
                    P.op("pool", lambda e, gl=gl: e.memset(UZ[gl][:], 0.0), reads=[("UZ", gl)], writes=[("UZ", gl)])
                    P.dma(f"uz{gl}", UZ[gl][gp * 16:(gp + 1) * 16, :], UT_scr[t, gp * 16:(gp + 1) * 16, :], reads=[], writes=[("UZ", gl)])
                for part in range(2):
                    for gl in range(2):
                        uzv = UZ[gl][:].rearrange("p (m l) -> p m l", l=LC)
                        for half in range(2):
                            b = 4 + part * 2 + half
                            for e_ in range(LC):
                                P.op("pe", lambda e, part=part, gl=gl, half=half, b=b, e_=e_, uzv=uzv: e.matmul(
                                    ps[b][gl * 64:(gl + 1) * 64, :], GFb[:, e_, part, t * 64:(t + 1) * 64],
                                    uzv[:, half * 512:(half + 1) * 512, LC - 1 - e_], start=(e_ == 0), stop=(e_ == LC - 1), skip_group_check=True),
                                    reads=[K, ("UZ", gl)], writes=[psk[b]])
                P.op("dve", lambda e: e.memset(Wr[:, 0, CTX - 1:CTX], 1.0), reads=[K], writes=[K])
                P.op("dve", lambda e: e.memset(Wr[:, 1, CTX - 1:CTX], 0.0), reads=[K], writes=[K])
                have = 1
                k = 0
                while have < CTX:
                    nn = min(have, CTX - have)
                    src_lo = CTX - nn
                    dst_lo = CTX - have - nn
                    pr_c = DL[:, k, 0, pr:pr + 1]
                    pi_c = DL[:, k, 1, pr:pr + 1]
                    sr = Wr[:, 0, src_lo:src_lo + nn]
                    si = Wr[:, 1, src_lo:src_lo + nn]
                    TS(tcol[:, 0:nn], si, pi_c, None, ALU.mult, None, K)
                    STT(Wr[:, 0, dst_lo:dst_lo + nn], sr, pr_c, tcol[:, 0:nn], ALU.mult, ALU.subtract, K)
                    TS(tcol[:, 0:nn], si, pr_c, None, ALU.mult, None, K)
                    STT(Wr[:, 1, dst_lo:dst_lo + nn], sr, pi_c, tcol[:, 0:nn], ALU.mult, ALU.add, K)
                    have += nn
                    k += 1
                def vsl(part, lo, hi):
                    return ps[4 + part * 2 + lo // 512][:, lo % 512:(hi - 1) % 512 + 1]
                segs = [(0, 512), (512, CTX)]
                first = True
                for (a, bnd) in segs:
                    for idx, (wp, vp) in enumerate([(0, 0), (1, 1), (0, 1), (1, 0)]):
                        col = acc4[:, idx:idx + 1] if first else xs[:, 0:1]
                        P.op("dve", lambda e, a=a, bnd=bnd, wp=wp, vp=vp: e.tensor_tensor(
                            out=junk[:, 0:bnd - a], in0=Wr[:, wp, a:bnd], in1=vsl(vp, a, bnd), op=ALU.mult),
                            reads=[K] + psk[4:8], writes=[K])
                        P.op("dve", lambda e, a=a, bnd=bnd, col=col: e.tensor_reduce(
                            out=col, in_=junk[:, 0:bnd - a], axis=mybir.AxisListType.X, op=ALU.add), reads=[K], writes=[K])
                        if not first:
                            TT(acc4[:, idx:idx + 1], acc4[:, idx:idx + 1], xs[:, 0:1], ALU.add, K)
                    first = False
                TT(xs[:, 0:1], acc4[:, 0:1], acc4[:, 1:2], ALU.subtract, K)
                TT(xs[:, 1:2], acc4[:, 2:3], acc4[:, 3:4], ALU.add, K)
                for part in range(2):
                    P.op("dve", lambda e, part=part: e.tensor_copy(out=scn[0][:, part, 0:512 - (CTX - 512)], in_=ps[5 + part * 2][:, CTX - 512:512]),
                         reads=[K] + psk[4:8], writes=[K])
                lr_c = LAMS[:, 0, pr:pr + 1]
                li_c = LAMS[:, 1, pr:pr + 1]
                TS(tcol[:, 0:1], xs[:, 1:2], li_c, None, ALU.mult, None, K)
                STT(tcol[:, 1:2], xs[:, 0:1], lr_c, tcol[:, 0:1], ALU.mult, ALU.subtract, K)
                TT(scn[0][:, 0, 0:1], scn[0][:, 0, 0:1], tcol[:, 1:2], ALU.add, K)
                TS(tcol[:, 0:1], xs[:, 1:2], lr_c, None, ALU.mult, None, K)
                STT(tcol[:, 1:2], xs[:, 0:1], li_c, tcol[:, 0:1], ALU.mult, ALU.add, K)
                TT(scn[0][:, 1, 0:1], scn[0][:, 1, 0:1], tcol[:, 1:2], ALU.add, K)
                cur = 0
                sh = 1
                k = 0
                while sh < OWNCH:
                    A, B = scn[cur], scn[1 - cur]
                    nn = OWNCH - sh
                    pr_c = DL[:, k, 0, pr:pr + 1]
                    pi_c = DL[:, k, 1, pr:pr + 1]
                    for part in range(2):
                        P.op("dve", lambda e, part=part, A=A, B=B, sh=sh: e.tensor_copy(out=B[:, part, 0:sh], in_=A[:, part, 0:sh]), reads=[K], writes=[K])
                    STT(tcol[:, 0:nn], A[:, 0, 0:nn], pr_c, A[:, 0, sh:OWNCH], ALU.mult, ALU.add, K)
                    TS(tcol[:, 256:256 + nn], A[:, 1, 0:nn], pi_c, None, ALU.mult, None, K)
                    TT(B[:, 0, sh:OWNCH], tcol[:, 0:nn], tcol[:, 256:256 + nn], ALU.subtract, K)
                    STT(tcol[:, 0:nn], A[:, 1, 0:nn], pr_c, A[:, 1, sh:OWNCH], ALU.mult, ALU.add, K)
                    TS(tcol[:, 256:256 + nn], A[:, 0, 0:nn], pi_c, None, ALU.mult, None, K)
                    TT(B[:, 1, sh:OWNCH], tcol[:, 0:nn], tcol[:, 256:256 + nn], ALU.add, K)
                    cur = 1 - cur
                    sh *= 2
                    k += 1
                fin = scn[cur]
                for part in range(2):
                    P.op("dve", lambda e, part=part, ql=ql: e.tensor_copy(out=xpv[:, ql, part, 0:1], in_=xs[:, part:part + 1]), reads=[K], writes=[K])
                    P.op("dve", lambda e, part=part, ql=ql, fin=fin: e.tensor_copy(out=xpv[:, ql, part, 1:OWNCH], in_=fin[:, part, 0:OWNCH - 1]), reads=[K], writes=[K])
            uov = uown[:].rearrange("p (m l) -> p m l", l=LC)
            ygv = None
            for j in range(LC):
                b = j % 4
                nmm = 8 + j + 1
                n = 0
                for ql in range(4):
                    for part in range(2):
                        P.op("pe", lambda e, ql=ql, part=part, j=j, b=b, n=n, nmm=nmm: e.matmul(
                            ps[b][:, 0:OWNCH], HZ[:, ql, part, j + 1, :], xpv[:, ql, part, :], start=(n == 0), stop=(n == nmm - 1)),
                            reads=[K], writes=[psk[b]])
                        n += 1
                for d in range(j + 1):
                    P.op("pe", lambda e, d=d, j=j, b=b, n=n, nmm=nmm: e.matmul(
                        ps[b][:, 0:OWNCH], KTb[:, d, :], uov[:, :, j - d], start=(n == 0), stop=(n == nmm - 1)),
                        reads=[K, "uown"], writes=[psk[b]])
                    n += 1
                P.op("act", lambda e, b=b, t=t, j=j: e.activation(
                    out=ygT[:, t, j:j + (OWNCH - 1) * LC + 1:LC], in_=ps[b][:, 0:OWNCH], func=AF.Gelu_apprx_tanh),
                    reads=[psk[b]], writes=[("ygT", t)])
        P.barrier()
    if STOP_AFTER == 3:
        return
    build_phase4(nc, P, gs, sb, ps, psk, rr, affine_evac, evac_eng, L, ygT)


def build_phase4(nc, P, gs, sb, ps, psk, rr, affine_evac, evac_eng, L, ygT):
    catT, vec, valid, ident, identb, onesb, onesf = (L[k] for k in ("catT", "vec", "valid", "ident", "identb", "onesb", "onesf"))
    G_IN, B_IN, G1, B1, G2, B2, BPG, MISC = (L[k] for k in ("G_IN", "B_IN", "G1", "B1", "G2", "B2", "BPG", "MISC"))
    H0_scr = L["H0_scr"]
    w_glu, w_o, w_up, w_down, w_ple, w_pg, pc, convp, out_d = (L[k] for k in ("w_glu", "w_o", "w_up", "w_down", "w_ple", "w_pg", "pc", "convp", "out_d"))
    OB = QB[1:]

    def nb():
        b = rr["ps"] % 8
        rr["ps"] += 1
        return b

    r1 = sb(gs, "r1", [128, 16, NQ], F32)
    h1b = catT
    mean_t = sb(gs, "mean_t", [128, NQ], F32)
    rstd_t = sb(gs, "rstd_t", [128, NQ], F32)
    sqt = [sb(gs, f"sqt{i}", [128, 512], F32) for i in range(2)]

    def layer_norm_fm(blocks, gcols, bcols, key):
        for (c0, c1) in blocks:
            n = c1 - c0
            bs, bq = nb(), nb()
            for kt in range(16):
                P.op("pe", lambda e, kt=kt, bs=bs, c0=c0, c1=c1, n=n: e.matmul(ps[bs][:, 0:n], onesf[:], r1[:, kt, c0:c1], start=(kt == 0), stop=(kt == 15)),
                     reads=[(key, kt, c0), "onesf"], writes=[psk[bs]])
            for kt in range(16):
                si = kt % 2
                P.op("act", lambda e, kt=kt, si=si, c0=c0, c1=c1, n=n: e.activation(out=sqt[si][:, 0:n], in_=r1[:, kt, c0:c1], func=AF.Square),
                     reads=[(key, kt, c0)], writes=[("sqt", si)])
                P.op("pe", lambda e, kt=kt, si=si, bq=bq, n=n: e.matmul(ps[bq][:, 0:n], onesf[:], sqt[si][:, 0:n], start=(kt == 0), stop=(kt == 15)),
                     reads=[("sqt", si), "onesf"], writes=[psk[bq]])
            P.op("dve", lambda e, bs=bs, c0=c0, c1=c1, n=n: e.tensor_scalar(out=mean_t[:, c0:c1], in0=ps[bs][:, 0:n], scalar1=1.0 / D, scalar2=None, op0=ALU.mult),
                 reads=[psk[bs]], writes=[("mean", c0)])
            P.op("dve", lambda e, c0=c0, c1=c1: e.tensor_tensor(out=rstd_t[:, c0:c1], in0=mean_t[:, c0:c1], in1=mean_t[:, c0:c1], op=ALU.mult),
                 reads=[("mean", c0)], writes=[("rstd", c0)])
            P.op("dve", lambda e, bq=bq, c0=c0, c1=c1, n=n: e.scalar_tensor_tensor(out=rstd_t[:, c0:c1], in0=ps[bq][:, 0:n], scalar=1.0 / D, in1=rstd_t[:, c0:c1],
                                                                                  op0=ALU.mult, op1=ALU.subtract),
                 reads=[psk[bq], ("rstd", c0)], writes=[("rstd", c0)])
            P.op("act", lambda e, c0=c0, c1=c1: e.activation(out=rstd_t[:, c0:c1], in_=rstd_t[:, c0:c1], func=AF.Sqrt, bias=EPS, scale=1.0),
                 reads=[("rstd", c0)], writes=[("rstd", c0)])
            P.op("dve", lambda e, c0=c0, c1=c1: e.reciprocal(out=rstd_t[:, c0:c1], in_=rstd_t[:, c0:c1]), reads=[("rstd", c0)], writes=[("rstd", c0)])
            for kt in range(16):
                P.op("dve", lambda e, kt=kt, c0=c0, c1=c1: e.tensor_tensor(out=r1[:, kt, c0:c1], in0=r1[:, kt, c0:c1], in1=mean_t[:, c0:c1], op=ALU.subtract),
                     reads=[(key, kt, c0), ("mean", c0)], writes=[(key, kt, c0)])
                P.op("dve", lambda e, kt=kt, c0=c0, c1=c1: e.tensor_tensor(out=r1[:, kt, c0:c1], in0=r1[:, kt, c0:c1], in1=rstd_t[:, c0:c1], op=ALU.mult),
                     reads=[(key, kt, c0), ("rstd", c0)], writes=[(key, kt, c0)])
                P.op("dve", lambda e, kt=kt, c0=c0, c1=c1: e.tensor_scalar(out=r1[:, kt, c0:c1], in0=r1[:, kt, c0:c1], scalar1=gcols[:, kt:kt + 1], scalar2=bcols[:, kt:kt + 1],
                                                                          op0=ALU.mult, op1=ALU.add),
                     reads=[(key, kt, c0), "vec"], writes=[(key, kt, c0)])

    def wtile(st, name, n):
        return [sb(st, f"{name}{i}", [128, n, 128], BF16) for i in range(2)]

    with ExitStack() as s4:
      with ExitStack() as s4g:
        Wg = sb(s4g, "Wg", [128, 8, 1024], BF16)
        for kt in range(8):
            P.dma(f"wg{kt % 4}", Wg[:, kt, :], w_glu[kt * 128:(kt + 1) * 128, :], writes=[("Wg", kt)], q="pool")
        sg = [sb(s4g, f"sg{i}", [128, 512], F32) for i in range(2)]
        nsg = 0
        for mt in range(8):
            for (c0, c1) in QB:
                n = c1 - c0
                b = nb()
                for kt in range(8):
                    P.op("pe", lambda e, kt=kt, b=b, mt=mt, c0=c0, c1=c1, n=n: e.matmul(ps[b][:, 0:n], Wg[:, kt, mt * 128:(mt + 1) * 128], ygT[:, kt, c0:c1],
                                                                                         start=(kt == 0), stop=(kt == 7)),
                         reads=[("Wg", kt), ("ygT", kt)], writes=[psk[b]])
                si = nsg % 2
                nsg += 1
                P.op("act", lambda e, b=b, si=si, mt=mt, n=n: e.activation(out=sg[si][:, 0:n], in_=ps[b][:, 0:n], func=AF.Sigmoid, bias=MISC[:, mt:mt + 1], scale=1.0),
                     reads=[psk[b], "vec"], writes=[("sg", si)])
                P.op("dve", lambda e, si=si, mt=mt, c0=c0, c1=c1, n=n: e.tensor_tensor(out=catT[:, 8 + mt, c0:c1], in0=ygT[:, mt, c0:c1], in1=sg[si][:, 0:n], op=ALU.mult),
                     reads=[("sg", si), ("ygT", mt)], writes=[("catT", 8 + mt, c0)])
        P.barrier()
        Wo = [sb(s4g, f"Wo{i}", [128, 16, 512], BF16) for i in range(2)]
        h0m = [sb(s4g, f"h0m{i}", [128, NQ], F32) for i in range(2)]
        wo_v = w_o.rearrange("(kt p) m -> p kt m", p=128)
        for mt in range(16):
            wg_, mi = (mt // 4) % 2, mt % 4
            wi = mt % 2
            if mi == 0:
                for kq in range(4):
                    P.dma(f"wo{wg_}", Wo[wg_][:, kq * 4:(kq + 1) * 4, :], wo_v[:, kq * 4:(kq + 1) * 4, (mt // 4) * 512:(mt // 4 + 1) * 512],
                          writes=[("Wo", wg_)], q="pool")
            P.dma(f"h0m{wi}", h0m[wi][:, HO:NQ], H0_scr[mt, :, HO:NQ], writes=[("h0m", wi)])
            P.op("dve", lambda e, wi=wi, mt=mt: e.tensor_copy(out=h0m[wi][:, 0:HO], in_=L["halo_f"][:, mt, :]), reads=[("h0m", wi)], writes=[("h0m", wi)])
            for (c0, c1) in QB:
                n = c1 - c0
                b = nb()
                for kt in range(16):
                    P.op("pe", lambda e, kt=kt, b=b, wg_=wg_, mi=mi, c0=c0, c1=c1, n=n: e.matmul(ps[b][:, 0:n], Wo[wg_][:, kt, mi * 128:(mi + 1) * 128], catT[:, kt, c0:c1], start=(kt == 0), stop=(kt == 15)),
                         reads=[("Wo", wg_), ("catT", kt, c0)], writes=[psk[b]])
                P.op("dve", lambda e, b=b, wi=wi, mt=mt, c0=c0, c1=c1, n=n: e.scalar_tensor_tensor(out=r1[:, mt, c0:c1], in0=h0m[wi][:, c0:c1], scalar=ALPHA, in1=ps[b][:, 0:n],
                                                                                                  op0=ALU.mult, op1=ALU.add),
                     reads=[psk[b], ("h0m", wi)], writes=[("r1", mt, c0)])
        layer_norm_fm(QB, G1, B1, "r1")
        for kt in range(16):
            P.op("act", lambda e, kt=kt: e.activation(out=h1b[:, kt, :], in_=r1[:, kt, :], func=AF.Identity),
                 reads=[("r1", kt, c0) for (c0, c1) in QB], writes=[("h1b", kt)])
            P.op("dve", lambda e, kt=kt: e.tensor_scalar(out=h1b[:, kt, HO - 2:HO], in0=h1b[:, kt, HO - 2:HO], scalar1=valid[:, 16:17], scalar2=None, op0=ALU.mult),
                 reads=[("h1b", kt), "valid"], writes=[("h1b", kt)])
        P.barrier()

    with ExitStack() as s5:
        pTb = sb(s5, "pTb", [128, 2, OWN], BF16)
        pt = [sb(s5, f"pt{i}", [128, 256], F32) for i in range(2)]
        for tt in range(8):
            pi_ = tt % 2
            P.dma(f"pt{pi_}", pt[pi_][:], pc[tt * 128:(tt + 1) * 128, :], writes=[("pt", pi_)])
            b = nb()
            for k2 in range(2):
                P.op("pe", lambda e, k2=k2, b=b, pi_=pi_: e.transpose(out=ps[b][:, k2 * 128:(k2 + 1) * 128], in_=pt[pi_][:, k2 * 128:(k2 + 1) * 128], identity=ident[:]),
                     reads=[("pt", pi_), "ident"], writes=[psk[b]])
            for k2 in range(2):
                P.op("dve", lambda e, k2=k2, b=b, tt=tt: e.tensor_copy(out=pTb[:, k2, tt * 128:(tt + 1) * 128], in_=ps[b][:, k2 * 128:(k2 + 1) * 128]),
                     reads=[psk[b]], writes=[("pTb", tt)])
        Wpg = [sb(s5, f"Wpg{i}", [128, 16, 512], BF16) for i in range(2)]
        Wpl = [sb(s5, f"Wpl{i}", [128, 2, 512], BF16) for i in range(2)]
        sg = [sb(s5, f"sgp{i}", [128, 512], F32) for i in range(2)]
        wpg_v = w_pg.rearrange("(kt p) m -> p kt m", p=128)
        wpl_v = w_ple.rearrange("(kt p) m -> p kt m", p=128)
        nsg = 0
        for mt in range(16):
            wi, mi = (mt // 4) % 2, mt % 4
            if mi == 0:
                for kq in range(4):
                    P.dma(f"wpg{wi}", Wpg[wi][:, kq * 4:(kq + 1) * 4, :], wpg_v[:, kq * 4:(kq + 1) * 4, (mt // 4) * 512:(mt // 4 + 1) * 512],
                          writes=[("Wpg", wi)], q="pool")
                P.dma(f"wpl{wi}", Wpl[wi][:], wpl_v[:, :, (mt // 4) * 512:(mt // 4 + 1) * 512], writes=[("Wpl", wi)], q="pool")
            for (c0, c1) in OB:
                bg, bp = nb(), nb()
                for kt in range(16):
                    P.op("pe", lambda e, kt=kt, bg=bg, wi=wi, mi=mi, c0=c0, c1=c1: e.matmul(ps[bg][:], Wpg[wi][:, kt, mi * 128:(mi + 1) * 128], h1b[:, kt, c0:c1], start=(kt == 0), stop=(kt == 15)),
                         reads=[("Wpg", wi), ("h1b", kt)], writes=[psk[bg]])
                for k2 in range(2):
                    P.op("pe", lambda e, k2=k2, bp=bp, wi=wi, mi=mi, c0=c0, c1=c1: e.matmul(ps[bp][:], Wpl[wi][:, k2, mi * 128:(mi + 1) * 128], pTb[:, k2, c0 - HO:c1 - HO], start=(k2 == 0), stop=(k2 == 1)),
                         reads=[("Wpl", wi)] + [("pTb", tt) for tt in range(8)], writes=[psk[bp]])
                si = nsg % 2
                nsg += 1
                P.op("act", lambda e, bg=bg, si=si, mt=mt: e.activation(out=sg[si][:], in_=ps[bg][:], func=AF.Sigmoid, bias=BPG[:, mt:mt + 1], scale=1.0),
                     reads=[psk[bg], "vec"], writes=[("sgp", si)])
                P.op("dve", lambda e, bp=bp, si=si: e.tensor_tensor(out=sg[si][:], in0=ps[bp][:], in1=sg[si][:], op=ALU.mult),
                     reads=[psk[bp], ("sgp", si)], writes=[("sgp", si)])
                P.op("dve", lambda e, si=si, mt=mt, c0=c0, c1=c1: e.scalar_tensor_tensor(out=r1[:, mt, c0:c1], in0=r1[:, mt, c0:c1], scalar=ALPHA, in1=sg[si][:],
                                                                                        op0=ALU.mult, op1=ALU.add),
                     reads=[("sgp", si), ("r1", mt, c0)], writes=[("r1", mt, c0)])
        P.barrier()

    with ExitStack() as s6:
        cvp = sb(s6, "cvp", [128, 4, 88], F32)
        P.dma("cvp", cvp[:].rearrange("p a b -> p (a b)"), convp[:, :], writes=["cvp"])
        Wu = [sb(s6, f"Wu{i}", [128, 16, 512], BF16) for i in range(2)]
        Wd = sb(s6, "Wd", [128, 4, D], BF16)
        actb = sb(s6, "actb", [128, 4, OWN], BF16)
        hv = [sb(s6, f"hv{i}", [128, NQ], F32) for i in range(2)]
        cv = [sb(s6, f"cv{i}", [128, OWN], F32) for i in range(2)]
        wu_v = w_up.rearrange("(kt p) m -> p kt m", p=128)
        for jg in range(11):
            for jj in range(4):
                P.dma("wd", Wd[:, jj, :], w_down[(jg * 4 + jj) * 128:(jg * 4 + jj + 1) * 128, :], writes=[("Wd", jj)], q="pool")
            for vg in range(2):
                for kq in range(4):
                    P.dma(f"wu{vg}", Wu[vg][:, kq * 4:(kq + 1) * 4, :], wu_v[:, kq * 4:(kq + 1) * 4, vg * DFF + jg * 512:vg * DFF + (jg + 1) * 512],
                          writes=[("Wu", vg)], q="pool")
            for jj in range(4):
                j = jg * 4 + jj
                for vg in range(2):
                    ft = vg * 44 + j
                    for (c0, c1) in QB:
                        n = c1 - c0
                        b = nb()
                        for kt in range(16):
                            P.op("pe", lambda e, kt=kt, b=b, vg=vg, jj=jj, c0=c0, c1=c1, n=n: e.matmul(ps[b][:, 0:n], Wu[vg][:, kt, jj * 128:(jj + 1) * 128], h1b[:, kt, c0:c1], start=(kt == 0), stop=(kt == 15)),
                                 reads=[("Wu", vg), ("h1b", kt)], writes=[psk[b]])
                        affine_evac(evac_eng(), hv[vg][:, c0:c1], ps[b][:, 0:n], reads=[psk[b]], writes=[("hv", vg)])
                    P.op("dve", lambda e, vg=vg, ft=ft: e.tensor_scalar(out=cv[vg][:], in0=hv[vg][:, HO - 2:HO - 2 + OWN], scalar1=cvp[:, 0, ft:ft + 1], scalar2=cvp[:, 3, ft:ft + 1],
                                                                       op0=ALU.mult, op1=ALU.add),
                         reads=[("hv", vg), "cvp"], writes=[("cv", vg)])
                    for tap in (1, 2):
                        P.op("dve", lambda e, vg=vg, ft=ft, tap=tap: e.scalar_tensor_tensor(out=cv[vg][:], in0=hv[vg][:, HO - 2 + tap:HO - 2 + tap + OWN], scalar=cvp[:, tap, ft:ft + 1], in1=cv[vg][:],
                                                                                             op0=ALU.mult, op1=ALU.add),
                             reads=[("hv", vg), ("cv", vg), "cvp"], writes=[("cv", vg)])
                P.op("act", lambda e: e.activation(out=cv[1][:], in_=cv[1][:], func=AF.Gelu_apprx_tanh), reads=[("cv", 1)], writes=[("cv", 1)])
                P.op("dve", lambda e, jj=jj: e.tensor_tensor(out=actb[:, jj, :], in0=cv[0][:], in1=cv[1][:], op=ALU.mult),
                     reads=[("cv", 0), ("cv", 1)], writes=[("actb", jj)])
            for mt in range(16):
                for (c0, c1) in OB:
                    b = nb()
                    for jj in range(4):
                        P.op("pe", lambda e, jj=jj, b=b, mt=mt, c0=c0, c1=c1: e.matmul(ps[b][:], Wd[:, jj, mt * 128:(mt + 1) * 128], actb[:, jj, c0 - HO:c1 - HO], start=(jj == 0), stop=(jj == 3)),
                             reads=[("Wd", jj), ("actb", jj)], writes=[psk[b]])
                    P.op("dve", lambda e, b=b, mt=mt, c0=c0, c1=c1: e.tensor_tensor(out=r1[:, mt, c0:c1], in0=r1[:, mt, c0:c1], in1=ps[b][:], op=ALU.add),
                         reads=[psk[b], ("r1", mt, c0)], writes=[("r1", mt, c0)])
        P.barrier()

    layer_norm_fm(OB, G2, B2, "r1")
    with ExitStack() as s7:
        ot = [sb(s7, f"ot{i}", [128, D], F32) for i in range(2)]
        for tt in range(8):
            oi = tt % 2
            for k4 in range(4):
                b = nb()
                for j in range(4):
                    kt = k4 * 4 + j
                    P.op("pe", lambda e, kt=kt, j=j, b=b, tt=tt: e.transpose(out=ps[b][:, j * 128:(j + 1) * 128], in_=r1[:, kt, HO + tt * 128:HO + (tt + 1) * 128], identity=ident[:]),
                         reads=[("r1", kt, c0) for (c0, c1) in OB] + ["ident"], writes=[psk[b]])
                affine_evac(evac_eng(), ot[oi][:, k4 * 512:(k4 + 1) * 512], ps[b][:], reads=[psk[b]], writes=[("ot", oi)])
            P.dma(f"ot{oi}", out_d[tt * 128:(tt + 1) * 128, :], ot[oi][:], reads=[("ot", oi)], writes=[("out", tt)])
        P.barrier()


_CACHE = {}


def _cols(v, n):
    return np.ascontiguousarray(np.asarray(v, np.float32).reshape(n, 128).T)


def kernel(x, p, ln_in_g, ln_in_b, w_in, lambda_q1, lambda_k1, lambda_q2, lambda_k2, g_subln, a_re, a_im, log_dt,
           b_re, b_im, c_re, c_im, d_skip, w_glu, b_glu, w_o, ln1_g, ln1_b, w_up, conv_w, conv_b, w_down, w_ple,
           w_pg, b_pg, ln2_g, ln2_b):
    f = lambda a: np.asarray(a, np.float32)
    x = f(x)[0]
    pp = f(p)[0, 0]
    if "nc" not in _CACHE:
        _CACHE["nc"] = build_program()
    nc = _CACHE["nc"]
    vecs = np.zeros((128, 144), np.float32)
    for i, v in enumerate([ln_in_g, ln_in_b, f(ln1_g)[0], f(ln1_b)[0], f(ln2_g)[0], f(ln2_b)[0], f(b_pg)[0]]):
        vecs[:, i * 16:(i + 1) * 16] = _cols(v, 16)
    vecs[:, 112:120] = _cols(f(b_glu)[0], 8)
    vecs[:, 120] = f(g_subln)[0]
    vecs[:, 128:136] = _cols(f(d_skip)[0], 8)
    convp = np.zeros((128, 4, 88), np.float32)
    for t in range(3):
        convp[:, t, :] = _cols(f(conv_w)[0, t], 88)
    convp[:, 3, :] = _cols(f(conv_b)[0], 88)
    lamv = np.stack([np.broadcast_to(f(v)[0], (128, 64)) for v in (lambda_q1, lambda_k1, lambda_q2, lambda_k2)], axis=1)
    ar, ai, ldt = f(a_re)[0], f(a_im)[0], f(log_dt)[0]
    br, bi, cr, ci = f(b_re)[0], f(b_im)[0], f(c_re)[0], f(c_im)[0]
    def clay(a_gn):
        a = a_gn.reshape(8, 8, 1, 64)
        a = np.broadcast_to(a, (8, 8, 16, 64))
        return a.transpose(1, 2, 0, 3).reshape(128, 512)
    def clay_b(b_gnc):
        a = b_gnc.reshape(8, 8, 64, 16).transpose(1, 3, 0, 2)
        return a.reshape(128, 512)
    ssmC = np.stack([clay(ar), clay(ai), clay_b(br), clay_b(bi), clay(np.broadcast_to(ldt[:, None], (64, 64)))], axis=1)
    def slay(a_gn):
        a = a_gn.reshape(32, 2, 64, 1)
        a = np.broadcast_to(a, (32, 2, 64, 16))
        return a.transpose(1, 2, 0, 3).reshape(128, 512)
    def slay_c(c_gcn):
        a = c_gcn.reshape(32, 2, 16, 64).transpose(1, 3, 0, 2)
        return a.reshape(128, 512)
    def slay_b(b_gnc):
        a = b_gnc.reshape(32, 2, 64, 16).transpose(1, 2, 0, 3)
        return a.reshape(128, 512)
    ssmS = np.stack([slay(ar), slay(ai), slay(np.broadcast_to(ldt[:, None], (64, 64))), slay_c(cr), slay_c(ci), slay_b(br), slay_b(bi)], axis=1)
    ident = np.eye(128, dtype=np.float32)
    slopes = 2.0 ** (-np.arange(1, 9, dtype=np.float64))
    kk = np.arange(128)[:, None]
    qq = np.arange(128)[None, :]
    dtile = np.zeros((128, 8, 128), np.float32)
    for h in range(8):
        dmat = 8.0 * (-slopes[h] * np.abs(qq - kk) + slopes[h] * (qq - 127))
        dmat = np.where((kk // 64) <= (qq // 64), dmat, NEG)
        dtile[:, h, :] = dmat
    shared = {
        "w_in": f(w_in)[0], "w_glu": f(w_glu)[0], "w_o": f(w_o)[0], "w_up": f(w_up)[0], "w_down": f(w_down)[0],
        "w_ple": f(w_ple)[0], "w_pg": f(w_pg)[0], "vecs": vecs, "convp": convp.reshape(128, -1),
        "lamv": np.ascontiguousarray(lamv.reshape(128, -1)), "ident": ident, "dtile": dtile.reshape(128, -1),
        "ssmC": np.ascontiguousarray(ssmC.reshape(128, -1)), "ssmS": np.ascontiguousarray(ssmS.reshape(128, -1)),
    }
    in_maps = []
    origins = [7167] + [7168 + 128 * i + 127 for i in range(8)]
    diagt = [55] + [56 + i for i in range(8)]
    for c in range(NCORES):
        start = S - OWN * (c + 1)
        xcx = np.zeros((S, D), np.float32)
        xcx[start:] = x[:OWN * (c + 1)]
        kb = np.zeros((128, 8, 64, 9), np.float32)
        kpos = (np.arange(64)[None, :] * 128 + np.arange(128)[:, None]).astype(np.float64)
        vmask = kpos >= start
        for h in range(8):
            for s_ in range(9):
                val = slopes[h] * (kpos - origins[s_])
                val[:, diagt[s_]] = 0.0
                val = np.where(vmask, np.minimum(val, 0.0), NEG)
                kb[:, h, :, s_] = val
        vd = np.zeros((128, 17), np.float32)
        for tb in range(16):
            vd[:, tb] = 1.0 if tb * 512 >= start else 0.0
        vd[:, 16] = 1.0 if c > 0 else 0.0
        m = dict(shared)
        m.update({"xc": xcx, "pc": np.ascontiguousarray(pp[c * OWN:(c + 1) * OWN]), "kbias": kb.reshape(128, -1), "valid": vd})
        in_maps.append(m)
    res = run_bass_kernel_spmd(nc, in_maps, core_ids=list(range(NCORES)))
    out = np.concatenate([np.asarray(res.results[c]["out"], np.float32) for c in range(NCORES)], axis=0)
    return out[None].astype(np.float32)
```

```python
import math
from contextlib import ExitStack

import numpy as np
import concourse.bass as bass
import concourse.mybir as mybir
from concourse.bass_utils import run_bass_kernel_spmd

F32 = mybir.dt.float32
BF16 = mybir.dt.bfloat16
AF = mybir.ActivationFunctionType
ALU = mybir.AluOpType

NCORES = 8
S = 8192
D = 2048
OWN = 1024
NQ = 1040
HO = 16
QB = [(14, 16), (16, 528), (528, 1040)]
DFF = 5632
ALPHA = 2.0 ** 0.25
EPS = 1e-5
LAM_INIT = 0.8 - 0.6 * math.exp(0.0)
NEG = -1.0e30
GW = [128, 256, 512, 512, 512, 512, 512, 512]
DEBUG = False
PH1_BLOCKS = 16
NO_STORE = False
NO_HALO = False
STOP_AFTER = None


class Prog:
    ENG = ("pe", "act", "dve", "pool", "sp")

    def __init__(self, nc, stack, same_engine_sync=True):
        self.nc = nc
        self.stack = stack
        self.e = {"pe": nc.tensor, "act": nc.scalar, "dve": nc.vector, "pool": nc.gpsimd, "sp": nc.sync}
        self.sem = {k: stack.enter_context(nc.semaphore("s_" + k)) for k in ("pe", "act", "dve", "pool")}
        self.cnt = {k: 0 for k in self.sem}
        self.dsem = {}
        self.dcnt = {}
        self.seen = {}
        self.lastw = {}
        self.readers = {}
        self.ses = same_engine_sync
        self.nops = 0

    def _semof(self, tok):
        kind, src, val = tok
        return (self.sem[src] if kind == "eng" else self.dsem[src]), val

    def _wait(self, eng, tok):
        kind, src, val = tok
        if kind == "eng" and src == eng and (eng == "pe" or not self.ses):
            return
        key = (eng, kind, src)
        if self.seen.get(key, 0) >= val:
            return
        self.seen[key] = val
        s, v = self._semof(tok)
        self.e[eng].wait_ge(s, v)

    def _deps(self, eng, reads, writes):
        for b in reads:
            t = self.lastw.get(b)
            if t is not None:
                self._wait(eng, t)
        for b in writes:
            t = self.lastw.get(b)
            if t is not None:
                self._wait(eng, t)
            for t in self.readers.get(b, ()):
                self._wait(eng, t)

    def _commit(self, tok, reads, writes):
        for b in writes:
            self.lastw[b] = tok
            self.readers[b] = []
        for b in reads:
            r = self.readers.setdefault(b, [])
            r[:] = [t for t in r if (t[0], t[1]) != (tok[0], tok[1])]
            r.append(tok)

    def op(self, eng, fn, reads=(), writes=()):
        for k in reads:
            if isinstance(k, str) and k[:2] == "ps" and k[2:].isdigit():
                for t in self.readers.get(k, ()):
                    if t[1] != eng:
                        self._wait(eng, t)
        self._deps(eng, reads, writes)
        self.cnt[eng] += 1
        tok = ("eng", eng, self.cnt[eng])
        fn(self.e[eng]).then_inc(self.sem[eng], 1)
        self._commit(tok, reads, writes)
        self.nops += 1
        return tok

    def dma(self, slot, out, in_, reads=(), writes=(), q="sp", **kw):
        if slot not in self.dsem:
            self.dsem[slot] = self.stack.enter_context(self.nc.semaphore("d_" + slot))
            self.dcnt[slot] = 0
        self._deps(q, reads, writes)
        self.dcnt[slot] += 16
        tok = ("dma", slot, self.dcnt[slot])
        self.e[q].dma_start(out=out, in_=in_, **kw).then_inc(self.dsem[slot], 16)
        self._commit(tok, reads, writes)
        self.nops += 1
        return tok

    def barrier(self):
        toks = [("eng", k, self.cnt[k]) for k in self.sem if self.cnt[k] > 0]
        toks += [("dma", s, c) for s, c in self.dcnt.items()]
        for eng in self.ENG:
            for t in toks:
                self._wait(eng, t)
        self.lastw.clear()
        self.readers.clear()

    def finish(self):
        for k in self.sem:
            if self.cnt[k]:
                self._wait("sp", ("eng", k, self.cnt[k]))
        for s, c in self.dcnt.items():
            self._wait("sp", ("dma", s, c))


def build_program():
    nc = bass.Bass("TRN2", target_bir_lowering=False)

    def din(name, shape, dt=F32):
        return nc.dram_tensor(name, list(shape), dt, kind="ExternalInput").ap()

    def dscr(name, shape, dt):
        return nc.dram_tensor(name, list(shape), dt, kind="Internal").ap()

    xc = din("xc", [S, D])
    pc = din("pc", [OWN, 256])
    w_in = din("w_in", [D, 4096])
    w_glu = din("w_glu", [1024, 1024])
    w_o = din("w_o", [D, D])
    w_up = din("w_up", [D, 2 * DFF])
    w_down = din("w_down", [DFF, D])
    w_ple = din("w_ple", [256, D])
    w_pg = din("w_pg", [D, D])
    vecs = din("vecs", [128, 16 * 9])
    convp = din("convp", [128, 4 * 88])
    lamv = din("lamv", [128, 4 * 64])
    ident_d = din("ident", [128, 128])
    kbias_d = din("kbias", [128, 8 * 64 * 9])
    dt_d = din("dtile", [128, 8 * 128])
    valid_d = din("valid", [128, 17])
    ssmC = din("ssmC", [128, 5 * 512])
    ssmS = din("ssmS", [128, 7 * 512])
    out_d = nc.dram_tensor("out", [OWN, D], F32, kind="ExternalOutput").ap()
    if DEBUG:
        dbg_cat = nc.dram_tensor("dbg_cat", [16, 128, NQ], F32, kind="ExternalOutput").ap()

    KT_scr = dscr("KT_scr", [8, 128, S], BF16)
    V_scr = dscr("V_scr", [64, 128, 1024], BF16)
    UT_scr = dscr("UT_scr", [8, 128, S], BF16)
    H0_scr = dscr("H0_scr", [16, 128, NQ], F32)
    XT_scr = dscr("XT_scr", [16, 128, NQ], BF16)

    with ExitStack() as gs:
        P = Prog(nc, gs)

        def sb(st, name, shape, dt):
            return st.enter_context(nc.sbuf_tensor("sb_" + name, list(shape), dt))

        ps = [gs.enter_context(nc.psum_tensor(f"ps{i}", [128, 512], F32)) for i in range(8)]
        psk = [f"ps{i}" for i in range(8)]

        ident = sb(gs, "ident", [128, 128], F32)
        identb = sb(gs, "identb", [128, 128], BF16)
        onesb = sb(gs, "onesb", [128, 128], BF16)
        onesf = sb(gs, "onesf", [128, 128], F32)
        vec = sb(gs, "vec", [128, 144], F32)
        valid = sb(gs, "valid", [128, 17], F32)
        lamcol = sb(gs, "lamcol", [128, 4], F32)
        halo_f = sb(gs, "halo_f", [128, 16, 16], F32)
        halo_b = sb(gs, "halo_b", [128, 16, 16], BF16)
        P.dma("c0", ident[:], ident_d[:, :], writes=["ident"])
        P.dma("c1", vec[:], vecs[:, :], writes=["vec"])
        P.dma("c2", valid[:], valid_d[:, :], writes=["valid"])
        P.op("dve", lambda e: e.tensor_copy(out=identb[:], in_=ident[:]), reads=["ident"], writes=["identb"])
        P.op("dve", lambda e: e.memset(onesb[:], 1.0), writes=["onesb"])
        P.op("dve", lambda e: e.memset(onesf[:], 1.0), writes=["onesf"])
        G_IN, B_IN, G1, B1, G2, B2, BPG, MISC = [vec[:, i * 16:(i + 1) * 16] for i in range(8)]

        with ExitStack() as s0:
            lv = sb(s0, "lv", [128, 4, 64], F32)
            lt = sb(s0, "lt", [128, 64], F32)
            ld = sb(s0, "ld", [128, 2], F32)
            P.dma("c3", lv[:].rearrange("p a b -> p (a b)"), lamv[:, :], writes=["lv"])
            for i in range(2):
                P.op("dve", lambda e, i=i: e.tensor_tensor(out=lt[:], in0=lv[:, 2 * i, :], in1=lv[:, 2 * i + 1, :], op=ALU.mult),
                     reads=["lv"], writes=["lt"])
                P.op("dve", lambda e, i=i: e.tensor_reduce(out=ld[:, i:i + 1], in_=lt[:], axis=mybir.AxisListType.X, op=ALU.add),
                     reads=["lt"], writes=[("ld", i)])
            P.op("act", lambda e: e.activation(out=ld[:], in_=ld[:], func=AF.Exp), reads=[("ld", 0), ("ld", 1)], writes=["ld"])
            P.op("dve", lambda e: e.tensor_tensor(out=lamcol[:, 0:1], in0=ld[:, 1:2], in1=ld[:, 0:1], op=ALU.subtract),
                 reads=["ld"], writes=["lamcol"])
            P.op("dve", lambda e: e.tensor_scalar(out=lamcol[:, 0:1], in0=lamcol[:, 0:1], scalar1=-LAM_INIT, scalar2=None, op0=ALU.add),
                 reads=["lamcol"], writes=["lamcol"])
            P.barrier()

        rr = {"ps": 0, "ev": 0}

        def evac_eng():
            rr["ev"] += 1
            return "dve" if rr["ev"] % 2 else "act"

        def affine_evac(eng, out, in_, scale=None, bias=None, reads=(), writes=()):
            if eng == "act":
                kw = {}
                if scale is not None:
                    kw["scale"] = scale
                if bias is not None:
                    kw["bias"] = bias
                P.op("act", lambda e: e.activation(out=out, in_=in_, func=AF.Identity, **kw), reads=reads, writes=writes)
            else:
                if scale is None and bias is None:
                    P.op("dve", lambda e: e.tensor_copy(out=out, in_=in_), reads=reads, writes=writes)
                elif bias is None:
                    P.op("dve", lambda e: e.tensor_scalar(out=out, in0=in_, scalar1=scale, scalar2=None, op0=ALU.mult), reads=reads, writes=writes)
                elif scale is None:
                    P.op("dve", lambda e: e.tensor_scalar(out=out, in0=in_, scalar1=bias, scalar2=None, op0=ALU.add), reads=reads, writes=writes)
                else:
                    P.op("dve", lambda e: e.tensor_scalar(out=out, in0=in_, scalar1=scale, scalar2=bias, op0=ALU.mult, op1=ALU.add), reads=reads, writes=writes)

        with ExitStack() as s1:
            W = sb(s1, "W1", [128, 16, 3072], BF16)
            for kt in range(16):
                for cb in range(3):
                    P.dma(f"w1_{kt}", W[:, kt, cb * 1024:(cb + 1) * 1024],
                          w_in[kt * 128:(kt + 1) * 128, 1024 + cb * 1024:1024 + (cb + 1) * 1024],
                          writes=[("W1", kt)], q="pool")
            xq = [sb(s1, f"xq{i}", [128, D], F32) for i in range(2)]
            xh = [sb(s1, f"xh{i}", [128, D], F32) for i in range(4)]
            h0T = [sb(s1, "h0T0", [128, 16, 512], BF16)]
            stats = [sb(s1, f"stats{i}", [128, 4, 6], F32) for i in range(2)]
            mv = [sb(s1, f"mv{i}", [128, 2], F32) for i in range(2)]
            rstd = [sb(s1, f"rstd{i}", [128, 1], F32) for i in range(2)]
            nmr = [sb(s1, f"nmr{i}", [128, 1], F32) for i in range(2)]
            stg = [sb(s1, f"stg{i}", [128, 512], BF16) for i in range(6)]
            h0f = [sb(s1, f"h0f{i}", [128, 512], F32) for i in range(2)]
            nstg = 0
            nx = 0
            for tb in range(16 - PH1_BLOCKS, 16):
                hb = 0
                r0 = tb * 512
                for tt in range(4):
                    xi = nx % 2
                    nx += 1
                    P.dma(f"x{xi}", xq[xi][:], xc[r0 + tt * 128:r0 + (tt + 1) * 128, :], writes=[("xq", xi)])
                    for c in range(4):
                        P.op("dve", lambda e, c=c, xi=xi: e.bn_stats(out=stats[xi][:, c, :], in_=xq[xi][:, c * 512:(c + 1) * 512]),
                             reads=[("xq", xi)], writes=[("stats", xi, c)])
                    P.op("dve", lambda e, xi=xi: e.bn_aggr(out=mv[xi][:], in_=stats[xi][:].rearrange("p a b -> p (a b)")),
                         reads=[("stats", xi, c) for c in range(4)], writes=[("mv", xi)])
                    P.op("act", lambda e, xi=xi: e.activation(out=rstd[xi][:], in_=mv[xi][:, 1:2], func=AF.Sqrt, bias=EPS, scale=1.0),
                         reads=[("mv", xi)], writes=[("rstd", xi)])
                    P.op("dve", lambda e, xi=xi: e.reciprocal(out=rstd[xi][:], in_=rstd[xi][:]), reads=[("rstd", xi)], writes=[("rstd", xi)])
                    P.op("dve", lambda e, xi=xi: e.scalar_tensor_tensor(out=nmr[xi][:], in0=mv[xi][:, 0:1], scalar=-1.0, in1=rstd[xi][:],
                                                                        op0=ALU.mult, op1=ALU.mult),
                         reads=[("mv", xi), ("rstd", xi)], writes=[("nmr", xi)])
                    P.op("act", lambda e, xi=xi, tt=tt: e.activation(out=xh[tt][:], in_=xq[xi][:], func=AF.Identity, bias=nmr[xi][:], scale=rstd[xi][:]),
                         reads=[("xq", xi), ("nmr", xi), ("rstd", xi)], writes=[("xh", tt)])
                for kt in range(16):
                    b = kt % 2
                    for tt in range(4):
                        P.op("pe", lambda e, kt=kt, tt=tt, b=b: e.transpose(out=ps[b][:, tt * 128:(tt + 1) * 128],
                                                                          in_=xh[tt][:, kt * 128:(kt + 1) * 128], identity=ident[:]),
                             reads=[("xh", tt), "ident"], writes=[psk[b]])
                    affine_evac(evac_eng(), h0T[hb][:, kt, :], ps[b][:], scale=G_IN[:, kt:kt + 1], bias=B_IN[:, kt:kt + 1],
                                reads=[psk[b], "vec"], writes=[("h0T", hb, kt)])
                    if tb == 13:
                        affine_evac("dve", halo_f[:, kt, :], ps[b][:, 496:512], scale=G_IN[:, kt:kt + 1], bias=B_IN[:, kt:kt + 1],
                                    reads=[psk[b], "vec"], writes=[("halo_f", kt)])
                        P.op("dve", lambda e, kt=kt: e.tensor_copy(out=halo_b[:, kt, :], in_=h0T[hb][:, kt, 496:512]),
                             reads=[("h0T", hb, kt)], writes=[("halo_b", kt)])
                    if tb >= 14:
                        fi = kt % 2
                        affine_evac("dve", h0f[fi][:], ps[b][:], scale=G_IN[:, kt:kt + 1], bias=B_IN[:, kt:kt + 1],
                                    reads=[psk[b], "vec"], writes=[("h0f", fi)])
                        dc = HO + (tb - 14) * 512
                        P.dma(f"h0s{fi}", H0_scr[kt, :, dc:dc + 512], h0f[fi][:], reads=[("h0f", fi)], writes=[("H0", kt, tb)])
                        P.dma(f"xts{fi}", XT_scr[kt, :, dc:dc + 512], h0T[hb][:, kt, :], reads=[("h0T", hb, kt)], writes=[("XT", kt, tb)])
                hreads = [("h0T", hb, kt) for kt in range(16)]

                def proj_fm(col0, dst, dkey, scale=None):
                    nonlocal nstg
                    b = 2 + rr["ps"] % 6
                    rr["ps"] += 1
                    for kt in range(16):
                        P.op("pe", lambda e, kt=kt, b=b: e.matmul(ps[b][:], W[:, kt, col0:col0 + 128], h0T[hb][:, kt, :],
                                                                  start=(kt == 0), stop=(kt == 15)),
                             reads=[("W1", kt), ("h0T", hb, kt)], writes=[psk[b]])
                    si = nstg % 6
                    nstg += 1
                    affine_evac(evac_eng(), stg[si][:], ps[b][:], scale=scale, reads=[psk[b], "valid"], writes=[("stg", si)])
                    if not NO_STORE:
                        P.dma(f"stg{si}", dst, stg[si][:], reads=[("stg", si)], writes=[dkey])

                for h in range(8):
                    proj_fm(h * 128, KT_scr[h, :, r0:r0 + 512], ("KT", h, tb))
                for t in range(8):
                    proj_fm(2048 + t * 128, UT_scr[t, :, r0:r0 + 512], ("UT", t, tb), scale=valid[:, tb:tb + 1])
                for tt in range(4):
                    for fh in range(2):
                        b = 2 + rr["ps"] % 6
                        rr["ps"] += 1
                        for kt in range(16):
                            P.op("pe", lambda e, kt=kt, b=b, tt=tt, fh=fh: e.matmul(
                                ps[b][:], h0T[hb][:, kt, tt * 128:(tt + 1) * 128], W[:, kt, 1024 + fh * 512:1024 + (fh + 1) * 512],
                                start=(kt == 0), stop=(kt == 15)),
                                reads=[("W1", kt), ("h0T", hb, kt)], writes=[psk[b]])
                        si = nstg % 6
                        nstg += 1
                        affine_evac(evac_eng(), stg[si][:], ps[b][:], reads=[psk[b]], writes=[("stg", si)])
                        if not NO_STORE:
                            P.dma(f"stg{si}", V_scr[tb * 4 + tt, :, fh * 512:(fh + 1) * 512], stg[si][:], reads=[("stg", si)],
                                  writes=[("V", tb * 4 + tt, fh)])
            P.barrier()
        if STOP_AFTER == 1:
            P.finish()
            return nc

        catT = sb(gs, "catT", [128, 16, NQ], BF16)
        with ExitStack() as s2:
            qT = sb(s2, "qT", [128, 8, NQ], BF16)
            dtile = sb(s2, "dtile", [128, 8, 128], BF16)
            P.dma("dt", dtile[:].rearrange("p a b -> p (a b)"), dt_d[:, :], writes=["dtile"], q="pool")
            with ExitStack() as s2a:
                xT = sb(s2a, "xT", [128, 16, NQ], BF16)
                Wq = sb(s2a, "Wq", [128, 16, 1024], BF16)
                for kt in range(16):
                    P.dma(f"xt{kt % 4}", xT[:, kt, HO:NQ], XT_scr[kt, :, HO:NQ], writes=[("xT", kt)])
                    P.op("dve", lambda e, kt=kt: e.tensor_copy(out=xT[:, kt, 0:HO], in_=halo_b[:, kt, :]), reads=[("xT", kt)], writes=[("xT", kt)])
                    P.dma(f"wq{kt % 4}", Wq[:, kt, :], w_in[kt * 128:(kt + 1) * 128, 0:1024], writes=[("Wq", kt)], q="pool")
                for h in range(8):
                    for (c0, c1) in QB:
                        b = rr["ps"] % 8
                        rr["ps"] += 1
                        for kt in range(16):
                            P.op("pe", lambda e, kt=kt, b=b, h=h, c0=c0, c1=c1: e.matmul(
                                ps[b][:, 0:c1 - c0], Wq[:, kt, h * 128:(h + 1) * 128], xT[:, kt, c0:c1], start=(kt == 0), stop=(kt == 15)),
                                reads=[("Wq", kt), ("xT", kt)], writes=[psk[b]])
                        affine_evac(evac_eng(), qT[:, h, c0:c1], ps[b][:, 0:c1 - c0], reads=[psk[b]], writes=[("qT", h, c0)])
                P.barrier()

            kT = [sb(s2, f"kT{i}", [128, S], BF16) for i in range(2)]
            vh = [sb(s2, f"vh{i}", [128, 64, 128], BF16) for i in range(2)]
            kb = [sb(s2, f"kb{i}", [128, 64, 9], F32) for i in range(2)]
            pT = [[sb(s2, f"pT{m}_{i}", [128, 512], BF16) for i in range(3)] for m in range(2)]
            rl = [sb(s2, f"rl{m}", [128, 512], F32) for m in range(2)]
            oo = sb(s2, "oo", [128, 512], F32)
            o2 = sb(s2, "o2", [128, 512], F32)
            sq = sb(s2, "sq", [128, 512], F32)
            gs8 = sb(s2, "gs8", [128, 1], F32)
            P.op("dve", lambda e: e.tensor_scalar(out=gs8[:], in0=MISC[:, 8:9], scalar1=1.0 - LAM_INIT, scalar2=None, op0=ALU.mult),
                 reads=["vec"], writes=["gs8"])
            npt = 0
            for h in range(8):
                hb = h % 2
                for j in range(4):
                    P.dma(f"kt{hb}", kT[hb][:, j * 2048:(j + 1) * 2048], KT_scr[h, :, j * 2048:(j + 1) * 2048], writes=[("kT", hb)])
                for j in range(4):
                    P.dma(f"vh{hb}", vh[hb][:, j * 16:(j + 1) * 16, :],
                          V_scr[j * 16:(j + 1) * 16, :, h * 128:(h + 1) * 128].rearrange("k p d -> p k d"), writes=[("vh", hb)])
                P.dma(f"kb{hb}", kb[hb][:].rearrange("p a b -> p (a b)"), kbias_d[:, h * 576:(h + 1) * 576], writes=[("kb", hb)])
                for qi, (c0, c1) in enumerate(QB):
                    n = c1 - c0
                    if qi == 0:
                        subs = [(0, 2, 0, 55)]
                        klast = 55
                    else:
                        subs = [(i * 128, 128, 1 + (qi - 1) * 4 + i, 56 + (qi - 1) * 4 + i) for i in range(4)]
                        klast = 56 + (qi - 1) * 4 + 3
                    acc = [4, 5, 6, 7]
                    per = GW[h] // 128
                    steps = []
                    for kt in range(klast + 1):
                        act_subs = [s_ for s_ in subs if s_[3] >= kt]
                        groups = []
                        if qi == 0:
                            groups = [(sl, sw, sid) for (sl, sw, sid, dk) in act_subs]
                        else:
                            for g0 in range(0, 4, per):
                                grp = [s_ for s_ in subs[g0:g0 + per] if s_[3] >= kt]
                                if len(grp) == per and all(s_[3] > kt for s_ in grp):
                                    groups.append((grp[0][0], sum(x[1] for x in grp), grp[-1][2]))
                                else:
                                    groups += [(sl, sw, sid) for (sl, sw, sid, dk) in grp]
                        steps.append((kt, act_subs, groups))

                    def emit_S(i):
                        kt, act_subs, groups = steps[i]
                        lo = act_subs[0][0]
                        hi = act_subs[-1][0] + act_subs[-1][1]
                        pi = i % 3
                        for m in range(2):
                            b = (i % 2) * 2 + m
                            P.op("pe", lambda e, m=m, b=b, kt=kt, lo=lo, hi=hi: e.matmul(
                                ps[b][:, lo:hi], kT[hb][m * 64:(m + 1) * 64, kt * 128:(kt + 1) * 128],
                                qT[m * 64:(m + 1) * 64, h, c0 + lo:c0 + hi], start=True, stop=True),
                                reads=[("kT", hb)] + [("qT", h, c0)], writes=[psk[b]])
                            for (sl, sw, sid, dk) in act_subs:
                                if dk == kt:
                                    dsl = dtile[:, h, 128 - sw:128] if sw < 128 else dtile[:, h, :]
                                    P.op("pe", lambda e, b=b, sl=sl, sw=sw, dsl=dsl: e.matmul(
                                        ps[b][:, sl:sl + sw], identb[:], dsl, start=False, stop=True, skip_group_check=True),
                                        reads=["identb", "dtile"], writes=[psk[b]])
                            for (sl, sw, sid) in groups:
                                P.op("act", lambda e, m=m, b=b, sl=sl, sw=sw, sid=sid, kt=kt, pi=pi: e.activation(
                                    out=pT[m][pi][:, sl:sl + sw], in_=ps[b][:, sl:sl + sw], func=AF.Exp,
                                    bias=kb[hb][:, kt, sid:sid + 1], scale=0.125),
                                    reads=[psk[b], ("kb", hb)], writes=[("pT", m, pi, sl)])

                    def emit_PV(i):
                        kt, act_subs, groups = steps[i]
                        lo = act_subs[0][0]
                        hi = act_subs[-1][0] + act_subs[-1][1]
                        pi = i % 3
                        first = (i == 0)
                        for m in range(2):
                            P.op("pe", lambda e, m=m, kt=kt, lo=lo, hi=hi, pi=pi, first=first: e.matmul(
                                ps[acc[m]][:, lo:hi], vh[hb][:, kt, :], pT[m][pi][:, lo:hi], start=first, stop=(kt == klast),
                                skip_group_check=True),
                                reads=[("vh", hb)] + [("pT", m, pi, g_[0]) for g_ in groups], writes=[psk[acc[m]]])
                            P.op("pe", lambda e, m=m, kt=kt, lo=lo, hi=hi, pi=pi, first=first: e.matmul(
                                ps[acc[2 + m]][:, lo:hi], onesb[:], pT[m][pi][:, lo:hi], start=first, stop=(kt == klast),
                                skip_group_check=True),
                                reads=["onesb"] + [("pT", m, pi, g_[0]) for g_ in groups], writes=[psk[acc[2 + m]]])

                    emit_S(0)
                    for i in range(len(steps)):
                        if i + 1 < len(steps):
                            emit_S(i + 1)
                        emit_PV(i)
                    for m in range(2):
                        P.op("dve", lambda e, m=m: e.tensor_scalar(out=rl[m][:, 0:n], in0=ps[acc[2 + m]][:, 0:n], scalar1=1e-37, scalar2=None, op0=ALU.add),
                             reads=[psk[acc[2 + m]]], writes=[("rl", m)])
                        P.op("dve", lambda e, m=m: e.reciprocal(out=rl[m][:, 0:n], in_=rl[m][:, 0:n]), reads=[("rl", m)], writes=[("rl", m)])
                    P.op("dve", lambda e: e.tensor_tensor(out=oo[:, 0:n], in0=ps[acc[0]][:, 0:n], in1=rl[0][:, 0:n], op=ALU.mult),
                         reads=[psk[acc[0]], ("rl", 0)], writes=["oo"])
                    P.op("dve", lambda e: e.tensor_tensor(out=o2[:, 0:n], in0=ps[acc[1]][:, 0:n], in1=rl[1][:, 0:n], op=ALU.mult),
                         reads=[psk[acc[1]], ("rl", 1)], writes=["o2"])
                    P.op("dve", lambda e: e.scalar_tensor_tensor(out=oo[:, 0:n], in0=o2[:, 0:n], scalar=lamcol[:, 0:1], in1=oo[:, 0:n],
                                                                 op0=ALU.mult, op1=ALU.add),
                         reads=["o2", "oo", "lamcol"], writes=["oo"])
                    P.op("act", lambda e: e.activation(out=sq[:, 0:n], in_=oo[:, 0:n], func=AF.Square), reads=["oo"], writes=["sq"])
                    P.op("pe", lambda e: e.matmul(ps[0][:, 0:n], onesf[:], sq[:, 0:n], start=True, stop=True),
                         reads=["onesf", "sq"], writes=[psk[0]])
                    P.op("act", lambda e: e.activation(out=sq[:, 0:n], in_=ps[0][:, 0:n], func=AF.Sqrt, bias=EPS, scale=1.0 / 128.0),
                         reads=[psk[0]], writes=["sq"])
                    P.op("dve", lambda e: e.reciprocal(out=sq[:, 0:n], in_=sq[:, 0:n]), reads=["sq"], writes=["sq"])
                    P.op("dve", lambda e: e.tensor_tensor(out=oo[:, 0:n], in0=oo[:, 0:n], in1=sq[:, 0:n], op=ALU.mult),
                         reads=["oo", "sq"], writes=["oo"])
                    P.op("dve", lambda e, h=h, c0=c0, c1=c1: e.tensor_scalar(out=catT[:, h, c0:c1], in0=oo[:, 0:n], scalar1=gs8[:], scalar2=None, op0=ALU.mult),
                         reads=["oo", "gs8"], writes=[("catT", h, c0)])
            P.barrier()
        if STOP_AFTER == 2:
            P.finish()
            return nc

        build_rest(nc, P, gs, sb, ps, psk, rr, affine_evac, evac_eng, locals())
        P.finish()
    return nc


def build_rest(nc, P, gs, sb, ps, psk, rr, affine_evac, evac_eng, L):
    catT, vec, valid, ident, identb, onesb, onesf = (L[k] for k in ("catT", "vec", "valid", "ident", "identb", "onesb", "onesf"))
    G_IN, B_IN, G1, B1, G2, B2, BPG, MISC = (L[k] for k in ("G_IN", "B_IN", "G1", "B1", "G2", "B2", "BPG", "MISC"))
    UT_scr, H0_scr, ssmC, ssmS = (L[k] for k in ("UT_scr", "H0_scr", "ssmC", "ssmS"))
    w_glu, w_o, w_up, w_down, w_ple, w_pg, pc, convp, out_d = (L[k] for k in ("w_glu", "w_o", "w_up", "w_down", "w_ple", "w_pg", "pc", "convp", "out_d"))
    PI = math.pi
    LC = 8
    NCH = S // LC
    OWNCH = 130
    CTX = NCH - OWNCH

    def TT(o, a, b, op, key):
        P.op("dve", lambda e: e.tensor_tensor(out=o, in0=a, in1=b, op=op), reads=[key], writes=[key])

    def TS(o, a, s1, s2, op0, op1, key):
        if s2 is None:
            P.op("dve", lambda e: e.tensor_scalar(out=o, in0=a, scalar1=s1, scalar2=None, op0=op0), reads=[key], writes=[key])
        else:
            P.op("dve", lambda e: e.tensor_scalar(out=o, in0=a, scalar1=s1, scalar2=s2, op0=op0, op1=op1), reads=[key], writes=[key])

    def STT(o, a, sc, b, op0, op1, key, extra_r=()):
        P.op("dve", lambda e: e.scalar_tensor_tensor(out=o, in0=a, scalar=sc, in1=b, op0=op0, op1=op1), reads=[key] + list(extra_r), writes=[key])

    def ACT(o, a, func, key, **kw):
        P.op("act", lambda e: e.activation(out=o, in_=a, func=func, **kw), reads=[key], writes=[key])

    ygT = sb(gs, "ygT", [128, 8, NQ], BF16)

    with ExitStack() as s3:
        GFb = sb(s3, "GFb", [128, LC, 2, 512], BF16)
        HSb = sb(s3, "HSb", [128, LC + 1, 2, 512], BF16)
        BSb = sb(s3, "BSb", [128, 2, 512], BF16)
        DL = sb(s3, "DL", [128, 11, 2, 32], F32)
        LAMS = sb(s3, "LAMS", [128, 2, 32], F32)
        K = "ssm"

        def lam_calc(st, F, ar, ai, ldt, pre):
            t = {n: sb(st, pre + n, [128, F], F32) for n in ("dt", "mag", "ang", "m", "s", "c", "lr", "li", "den", "cr", "ci", "t1")}
            ACT(t["dt"][:], ldt, AF.Exp, K)
            TT(t["mag"][:], t["dt"][:], ar, ALU.mult, K)
            ACT(t["mag"][:], t["mag"][:], AF.Exp, K)
            TT(t["ang"][:], t["dt"][:], ai, ALU.mult, K)
            for j in range(7):
                TS(t["m"][:], t["ang"][:], (2 * j + 1) * PI, -2 * PI, ALU.is_ge, ALU.mult, K)
                if j == 0:
                    TT(t["s"][:], t["ang"][:], t["m"][:], ALU.add, K)
                else:
                    TT(t["s"][:], t["s"][:], t["m"][:], ALU.add, K)
            TS(t["c"][:], t["s"][:], PI / 2, None, ALU.add, None, K)
            TS(t["m"][:], t["c"][:], PI, -2 * PI, ALU.is_ge, ALU.mult, K)
            TT(t["c"][:], t["c"][:], t["m"][:], ALU.add, K)
            ACT(t["s"][:], t["s"][:], AF.Sin, K)
            ACT(t["c"][:], t["c"][:], AF.Sin, K)
            TT(t["lr"][:], t["mag"][:], t["c"][:], ALU.mult, K)
            TT(t["li"][:], t["mag"][:], t["s"][:], ALU.mult, K)
            TT(t["den"][:], ar, ar, ALU.mult, K)
            TT(t["t1"][:], ai, ai, ALU.mult, K)
            TT(t["den"][:], t["den"][:], t["t1"][:], ALU.add, K)
            P.op("dve", lambda e: e.reciprocal(out=t["den"][:], in_=t["den"][:]), reads=[K], writes=[K])
            TS(t["mag"][:], t["lr"][:], -1.0, None, ALU.add, None, K)
            TT(t["cr"][:], t["mag"][:], ar, ALU.mult, K)
            TT(t["t1"][:], t["li"][:], ai, ALU.mult, K)
            TT(t["cr"][:], t["cr"][:], t["t1"][:], ALU.add, K)
            TT(t["cr"][:], t["cr"][:], t["den"][:], ALU.mult, K)
            TT(t["ci"][:], t["li"][:], ar, ALU.mult, K)
            TT(t["t1"][:], t["mag"][:], ai, ALU.mult, K)
            TT(t["ci"][:], t["ci"][:], t["t1"][:], ALU.subtract, K)
            TT(t["ci"][:], t["ci"][:], t["den"][:], ALU.mult, K)
            return t["lr"], t["li"], t["cr"], t["ci"]

        def cmul(orr, oi, ar_, ai_, br_, bi_, t1, t2):
            TT(t1, ar_, br_, ALU.mult, K)
            TT(t2, ai_, bi_, ALU.mult, K)
            TT(t2, t1, t2, ALU.subtract, K)
            TT(t1, ar_, bi_, ALU.mult, K)
            TT(oi, ai_, br_, ALU.mult, K)
            TT(oi, oi, t1, ALU.add, K)
            P.op("dve", lambda e: e.tensor_copy(out=orr, in_=t2), reads=[K], writes=[K])

        with ExitStack() as sc:
            cs = sb(sc, "cs", [128, 5, 512], F32)
            P.dma("ssmc", cs[:].rearrange("p a b -> p (a b)"), ssmC[:, :], writes=[K])
            lr, li, cr, ci = lam_calc(sc, 512, cs[:, 0, :], cs[:, 1, :], cs[:, 4, :], "C_")
            g = [[sb(sc, f"g{i}{j}", [128, 512], F32) for j in range(2)] for i in range(2)]
            t1 = sb(sc, "ct1", [128, 512], F32)
            t2 = sb(sc, "ct2", [128, 512], F32)
            cmul(g[0][0][:], g[0][1][:], cr[:], ci[:], cs[:, 2, :], cs[:, 3, :], t1[:], t2[:])
            for e_ in range(LC):
                cur, nxt = g[e_ % 2], g[(e_ + 1) % 2]
                for part in range(2):
                    P.op("dve", lambda e, e_=e_, part=part, cur=cur: e.tensor_copy(out=GFb[:, e_, part, :], in_=cur[part][:]), reads=[K], writes=[K])
                if e_ < LC - 1:
                    cmul(nxt[0][:], nxt[1][:], lr[:], li[:], cur[0][:], cur[1][:], t1[:], t2[:])
            ss = sb(sc, "ss", [128, 7, 512], F32)
            P.dma("ssms", ss[:].rearrange("p a b -> p (a b)"), ssmS[:, :], writes=[K])
            lrs, lis, crs, cis = lam_calc(sc, 512, ss[:, 0, :], ss[:, 1, :], ss[:, 2, :], "S_")
            cmul(g[0][0][:], g[0][1][:], crs[:], cis[:], ss[:, 5, :], ss[:, 6, :], t1[:], t2[:])
            for part in range(2):
                P.op("dve", lambda e, part=part: e.tensor_copy(out=BSb[:, part, :], in_=g[0][part][:]), reads=[K], writes=[K])
            pw = [[sb(sc, f"pw{i}{j}", [128, 512], F32) for j in range(2)] for i in range(2)]
            P.op("dve", lambda e: e.memset(pw[0][0][:], 1.0), reads=[K], writes=[K])
            P.op("dve", lambda e: e.memset(pw[0][1][:], 0.0), reads=[K], writes=[K])
            for e_ in range(LC + 1):
                cur, nxt = pw[e_ % 2], pw[(e_ + 1) % 2]
                cmul(g[1][0][:], g[1][1][:], ss[:, 3, :], ss[:, 4, :], cur[0][:], cur[1][:], t1[:], t2[:])
                P.op("dve", lambda e, e_=e_: e.tensor_copy(out=HSb[:, e_, 0, :], in_=g[1][0][:]), reads=[K], writes=[K])
                TS(HSb[:, e_, 1, :], g[1][1][:], -1.0, None, ALU.mult, None, K)
                if e_ < LC:
                    cmul(nxt[0][:], nxt[1][:], lrs[:], lis[:], cur[0][:], cur[1][:], t1[:], t2[:])
            lamL = pw[LC % 2]
            v32 = lambda tl: tl[:].rearrange("p (q c) -> p q c", c=16)[:, :, 0]
            for part in range(2):
                P.op("dve", lambda e, part=part: e.tensor_copy(out=LAMS[:, part, :], in_=v32(lamL[part])), reads=[K], writes=[K])
                P.op("dve", lambda e, part=part: e.tensor_copy(out=DL[:, 0, part, :], in_=v32(lamL[part])), reads=[K], writes=[K])
            d1 = sb(sc, "d1", [128, 32], F32)
            d2 = sb(sc, "d2", [128, 32], F32)
            for k in range(10):
                cmul(DL[:, k + 1, 0, :], DL[:, k + 1, 1, :], DL[:, k, 0, :], DL[:, k, 1, :], DL[:, k, 0, :], DL[:, k, 1, :], d1[:], d2[:])
            P.barrier()

        HZ = sb(s3, "HZ", [128, 4, 2, LC + 1, 128], BF16)
        BZ = sb(s3, "BZ", [128, 4, 2, 128], BF16)
        KTb = sb(s3, "KTb", [128, LC, 128], BF16)
        uown = sb(s3, "uown", [128, OWNCH * LC], BF16)
        xpv = sb(s3, "xpv", [128, 4, 2, OWNCH], BF16)
        dm = sb(s3, "dm", [128, 128], F32)
        UZ = [sb(s3, f"UZ{i}", [128, S], BF16) for i in range(2)]
        Wr = [sb(s3, f"Wr{i}", [128, 2, CTX], F32) for i in range(2)]
        junk = [sb(s3, f"junk{i}", [128, CTX], F32) for i in range(2)]
        acc8 = [sb(s3, f"acc8{i}", [128, 8], F32) for i in range(2)]
        tcol2 = [sb(s3, f"tcol2{i}", [128, 130], F32) for i in range(2)]
        xs = [sb(s3, f"xs{i}", [128, 2], F32) for i in range(2)]
        scn = [[sb(s3, f"scn{c}{i}", [128, 2, OWNCH], F32) for i in range(2)] for c in range(2)]
        tcol = [sb(s3, f"tcol{i}", [128, 512], F32) for i in range(2)]
        TC = "tilec"
        P.op("pool", lambda e: e.memset(HZ[:].rearrange("p a b c d -> p (a b c d)"), 0.0), writes=[TC])
        P.op("pool", lambda e: e.memset(BZ[:].rearrange("p a b d -> p (a b d)"), 0.0), reads=[TC], writes=[TC])

        def pair_chain(t, ql, c):
            pr = 4 * t + ql
            vb = 4 if c == 0 else 0
            Wr_, junk_, acc_, xs_, scn_, tcol_, UZ_ = Wr[c], junk[c], acc8[c], xs[c], scn[c], tcol[c], UZ[c]
            WRr, WRi = ("wr", c, 0), ("wr", c, 1)
            TCk = [("tc", c, i) for i in range(4)]
            tcs = [tcol_[:, i * 130:(i + 1) * 130] for i in range(3)] + [tcol_[:, 390:512]]

            def dve(fn, r, w):
                P.op("dve", fn, reads=r, writes=w)

            def act_mul(o, a_, col, r, w):
                P.op("act", lambda e: e.activation(out=o, in_=a_, func=AF.Identity, scale=col), reads=r, writes=w)

            dve(lambda e: e.memset(Wr_[:, 0, CTX - 1:CTX], 1.0), [], [WRr])
            yield
            dve(lambda e: e.memset(Wr_[:, 1, CTX - 1:CTX], 0.0), [], [WRi])
            yield
            have = 1
            k = 0
            while have < CTX:
                nn = min(have, CTX - have)
                src_lo = CTX - nn
                dst_lo = CTX - have - nn
                pr_c = DL[:, k, 0, pr:pr + 1]
                pi_c = DL[:, k, 1, pr:pr + 1]
                sr = Wr_[:, 0, src_lo:src_lo + nn]
                si = Wr_[:, 1, src_lo:src_lo + nn]
                t0 = junk_[:, 0:nn] if nn > 122 else tcs[0][:, 0:nn]
                t1 = junk_[:, 447:447 + nn] if nn > 122 else tcs[1][:, 0:nn]
                k0 = ("jk", c, 0) if nn > 122 else TCk[0]
                k1 = ("jk", c, 1) if nn > 122 else TCk[1]
                act_mul(t0, si, pi_c, [WRi], [k0])
                yield
                act_mul(t1, si, pr_c, [WRi], [k1])
                yield
                dve(lambda e, dst_lo=dst_lo, nn=nn, sr=sr, pr_c=pr_c, t0=t0: e.scalar_tensor_tensor(
                    out=Wr_[:, 0, dst_lo:dst_lo + nn], in0=sr, scalar=pr_c, in1=t0, op0=ALU.mult, op1=ALU.subtract), [WRr, k0], [WRr])
                yield
                dve(lambda e, dst_lo=dst_lo, nn=nn, sr=sr, pi_c=pi_c, t1=t1: e.scalar_tensor_tensor(
                    out=Wr_[:, 1, dst_lo:dst_lo + nn], in0=sr, scalar=pi_c, in1=t1, op0=ALU.mult, op1=ALU.add), [WRr, k1], [WRi])
                yield
                have += nn
                k += 1
            for gl in range(2):
                gp = 2 * ql + gl
                P.op("pool", lambda e: e.memset(UZ_[:], 0.0), reads=[("UZ", c)], writes=[("UZ", c)])
                P.dma(f"uz{c}", UZ_[gp * 16:(gp + 1) * 16, :], UT_scr[t, gp * 16:(gp + 1) * 16, :], reads=[], writes=[("UZ", c)])
                uzv = UZ_[:].rearrange("p (m l) -> p m l", l=LC)
                for part in range(2):
                    for half in range(2):
                        b = vb + part * 2 + half
                        for e_ in range(LC):
                            P.op("pe", lambda e, part=part, gl=gl, half=half, b=b, e_=e_, uzv=uzv: e.matmul(
                                ps[b][gl * 64:(gl + 1) * 64, :], GFb[:, e_, part, t * 64:(t + 1) * 64],
                                uzv[:, half * 512:(half + 1) * 512, LC - 1 - e_], start=(e_ == 0), stop=(e_ == LC - 1), skip_group_check=True),
                                reads=[("UZ", c)], writes=[psk[b]])
                yield
            vbanks = psk[vb:vb + 4]

            def vsl(part, lo, hi):
                return ps[vb + part * 2 + lo // 512][:, lo % 512:(hi - 1) % 512 + 1]
            segs = [(0, 512, 0), (512, CTX, 447)]
            WRk = [WRr, WRi]
            for idx, (wp, vp) in enumerate([(0, 0), (1, 1), (0, 1), (1, 0)]):
                for si_, (a_, bnd, joff) in enumerate(segs):
                    jk = ("jk", c, si_)
                    dve(lambda e, a_=a_, bnd=bnd, wp=wp, vp=vp, joff=joff: e.tensor_tensor(
                        out=junk_[:, joff:joff + bnd - a_], in0=Wr_[:, wp, a_:bnd], in1=vsl(vp, a_, bnd), op=ALU.mult),
                        [WRk[wp]] + vbanks, [jk])
                    yield
                    dve(lambda e, a_=a_, bnd=bnd, joff=joff, idx=idx, si_=si_: e.tensor_reduce(
                        out=acc_[:, 4 * si_ + idx:4 * si_ + idx + 1], in_=junk_[:, joff:joff + bnd - a_], axis=mybir.AxisListType.X, op=ALU.add),
                        [jk], [("acc", c, si_, idx)])
                    yield
            akeys = [("acc", c, si_, idx) for si_ in range(2) for idx in range(4)]
            XS = ("xs", c)
            dve(lambda e: e.tensor_tensor(out=acc_[:, 0:4], in0=acc_[:, 0:4], in1=acc_[:, 4:8], op=ALU.add), akeys, [("acc", c, 0, 0)])
            yield
            dve(lambda e: e.tensor_tensor(out=xs_[:, 0:1], in0=acc_[:, 0:1], in1=acc_[:, 1:2], op=ALU.subtract), [("acc", c, 0, 0)], [XS])
            yield
            dve(lambda e: e.tensor_tensor(out=xs_[:, 1:2], in0=acc_[:, 2:3], in1=acc_[:, 3:4], op=ALU.add), [("acc", c, 0, 0), XS], [XS])
            yield
            SK = lambda bi, part: ("scn", c, bi, part)
            for part in range(2):
                dve(lambda e, part=part: e.tensor_copy(out=scn_[0][:, part, 0:512 - (CTX - 512)], in_=ps[vb + 1 + part * 2][:, CTX - 512:512]),
                    vbanks, [SK(0, part)])
                yield
            lr_c = LAMS[:, 0, pr:pr + 1]
            li_c = LAMS[:, 1, pr:pr + 1]
            I0, I1 = ("ini", c, 0), ("ini", c, 1)
            ini = tcs[3]
            act_mul(ini[:, 0:1], xs_[:, 1:2], li_c, [XS], [I0])
            yield
            act_mul(ini[:, 2:3], xs_[:, 1:2], lr_c, [XS], [I1])
            yield
            dve(lambda e: e.scalar_tensor_tensor(out=ini[:, 1:2], in0=xs_[:, 0:1], scalar=lr_c, in1=ini[:, 0:1], op0=ALU.mult, op1=ALU.subtract), [XS, I0], [I0])
            yield
            dve(lambda e: e.scalar_tensor_tensor(out=ini[:, 3:4], in0=xs_[:, 0:1], scalar=li_c, in1=ini[:, 2:3], op0=ALU.mult, op1=ALU.add), [XS, I1], [I1])
            yield
            dve(lambda e: e.tensor_tensor(out=scn_[0][:, 0, 0:1], in0=scn_[0][:, 0, 0:1], in1=ini[:, 1:2], op=ALU.add), [I0, SK(0, 0)], [SK(0, 0)])
            yield
            dve(lambda e: e.tensor_tensor(out=scn_[0][:, 1, 0:1], in0=scn_[0][:, 1, 0:1], in1=ini[:, 3:4], op=ALU.add), [I1, SK(0, 1)], [SK(0, 1)])
            yield
            cur = 0
            sh = 1
            k = 0
            while sh < OWNCH:
                A, B = scn_[cur], scn_[1 - cur]
                ai, bi = cur, 1 - cur
                nn = OWNCH - sh
                pr_c = DL[:, k, 0, pr:pr + 1]
                pi_c = DL[:, k, 1, pr:pr + 1]
                for part in range(2):
                    P.op("act", lambda e, part=part, A=A, B=B, sh=sh: e.activation(out=B[:, part, 0:sh], in_=A[:, part, 0:sh], func=AF.Identity),
                         reads=[SK(ai, part)], writes=[SK(bi, part)])
                    yield
                act_mul(tcs[1][:, 0:nn], A[:, 1, 0:nn], pi_c, [SK(ai, 1)], [TCk[1]])
                yield
                act_mul(tcs[2][:, 0:nn], A[:, 0, 0:nn], pi_c, [SK(ai, 0)], [TCk[2]])
                yield
                dve(lambda e, A=A, nn=nn, sh=sh, pr_c=pr_c: e.scalar_tensor_tensor(out=tcs[0][:, 0:nn], in0=A[:, 0, 0:nn], scalar=pr_c, in1=A[:, 0, sh:OWNCH],
                                                                                   op0=ALU.mult, op1=ALU.add), [SK(ai, 0)], [TCk[0]])
                yield
                dve(lambda e, A=A, nn=nn, sh=sh, pr_c=pr_c: e.scalar_tensor_tensor(out=ini[:, 0:nn] if False else tcol2[c][:, 0:nn], in0=A[:, 1, 0:nn], scalar=pr_c, in1=A[:, 1, sh:OWNCH],
                                                                                   op0=ALU.mult, op1=ALU.add), [SK(ai, 1)], [TCk[3]])
                yield
                dve(lambda e, B=B, nn=nn, sh=sh: e.tensor_tensor(out=B[:, 0, sh:OWNCH], in0=tcs[0][:, 0:nn], in1=tcs[1][:, 0:nn], op=ALU.subtract),
                    [TCk[0], TCk[1]], [SK(bi, 0)])
                yield
                dve(lambda e, B=B, nn=nn, sh=sh: e.tensor_tensor(out=B[:, 1, sh:OWNCH], in0=tcol2[c][:, 0:nn], in1=tcs[2][:, 0:nn], op=ALU.add),
                    [TCk[3], TCk[2]], [SK(bi, 1)])
                yield
                cur = 1 - cur
                sh *= 2
                k += 1
            fin = scn_[cur]
            for part in range(2):
                P.op("act", lambda e, part=part: e.activation(out=xpv[:, ql, part, 0:1], in_=xs_[:, part:part + 1], func=AF.Identity),
                     reads=[XS], writes=[("xpv", ql, part, 0)])
                yield
                P.op("act", lambda e, part=part: e.activation(out=xpv[:, ql, part, 1:OWNCH], in_=fin[:, part, 0:OWNCH - 1], func=AF.Identity),
                     reads=[SK(cur, part)], writes=[("xpv", ql, part, 1)])
                yield

        for t in range(8):
            for ql in range(4):
                pr = 4 * t + ql
                for gl in range(2):
                    r0, r1 = gl * 64, (gl + 1) * 64
                    c0 = 32 * ql + 16 * gl
                    for part in range(2):
                        P.op("dve", lambda e, ql=ql, part=part, r0=r0, r1=r1, c0=c0, pr=pr: e.tensor_copy(
                            out=HZ[r0:r1, ql, part, :, c0:c0 + 16], in_=HSb[r0:r1, :, part, pr * 16:(pr + 1) * 16]), reads=[TC], writes=[TC])
                        P.op("dve", lambda e, ql=ql, part=part, r0=r0, r1=r1, c0=c0, pr=pr: e.tensor_copy(
                            out=BZ[r0:r1, ql, part, c0:c0 + 16], in_=BSb[r0:r1, part, pr * 16:(pr + 1) * 16]), reads=[TC], writes=[TC])
            P.op("dve", lambda e, t=t: e.tensor_scalar(out=dm[:], in0=ident[:], scalar1=vec[:, 128 + t:129 + t], scalar2=None, op0=ALU.mult),
                 reads=[TC, "vec", "ident"], writes=[TC])
            for d in range(LC):
                b = d % 2
                n = 0
                for ql in range(4):
                    for part in range(2):
                        P.op("pe", lambda e, ql=ql, part=part, d=d, b=b, n=n: e.matmul(
                            ps[b][:, 0:128], BZ[:, ql, part, :], HZ[:, ql, part, d, :], start=(n == 0), stop=(n == 7)),
                            reads=[TC], writes=[psk[b]])
                        n += 1
                if d == 0:
                    P.op("dve", lambda e, b=b: e.tensor_tensor(out=KTb[:, 0, :], in0=ps[b][:, 0:128], in1=dm[:], op=ALU.add), reads=[TC, psk[b]], writes=[TC])
                else:
                    P.op("dve", lambda e, b=b, d=d: e.tensor_copy(out=KTb[:, d, :], in_=ps[b][:, 0:128]), reads=[TC, psk[b]], writes=[TC])
            P.dma("uown", uown[:], UT_scr[t, :, S - OWNCH * LC:S], writes=["uown"])
            for q0 in (0, 2):
                gens = [pair_chain(t, q0, 0), pair_chain(t, q0 + 1, 1)]
                while gens:
                    for g_ in list(gens):
                        try:
                            next(g_)
                        except StopIteration:
                            gens.remove(g_)
            uov = uown[:].rearrange("p (m l) -> p m l", l=LC)
            xkeys = [("xpv", ql, part, i) for ql in range(4) for part in range(2) for i in range(2)]
            for j in range(LC):
                b = j % 4
                nmm = 8 + j + 1
                n = 0
                for ql in range(4):
                    for part in range(2):
                        P.op("pe", lambda e, ql=ql, part=part, j=j, b=b, n=n, nmm=nmm: e.matmul(
                            ps[b][:, 0:OWNCH], HZ[:, ql, part, j + 1, :], xpv[:, ql, part, :], start=(n == 0), stop=(n == nmm - 1)),
                            reads=[TC] + xkeys, writes=[psk[b]])
                        n += 1
                for d in range(j + 1):
                    P.op("pe", lambda e, d=d, j=j, b=b, n=n, nmm=nmm: e.matmul(
                        ps[b][:, 0:OWNCH], KTb[:, d, :], uov[:, :, j - d], start=(n == 0), stop=(n == nmm - 1)),
                        reads=[TC, "uown"], writes=[psk[b]])
                    n += 1
                P.op("act", lambda e, b=b, t=t, j=j: e.activation(
                    out=ygT[:, t, j:j + (OWNCH - 1) * LC + 1:LC], in_=ps[b][:, 0:OWNCH], func=AF.Gelu_apprx_tanh),
                    reads=[psk[b]], writes=[("ygT", t)])
        P.barrier()
    if STOP_AFTER == 3:
        return
    build_phase4(nc, P, gs, sb, ps, psk, rr, affine_evac, evac_eng, L, ygT)


def build_phase4(nc, P, gs, sb, ps, psk, rr, affine_evac, evac_eng, L, ygT):
    catT, vec, valid, ident, identb, onesb, onesf = (L[k] for k in ("catT", "vec", "valid", "ident", "identb", "onesb", "onesf"))
    G_IN, B_IN, G1, B1, G2, B2, BPG, MISC = (L[k] for k in ("G_IN", "B_IN", "G1", "B1", "G2", "B2", "BPG", "MISC"))
    H0_scr = L["H0_scr"]
    w_glu, w_o, w_up, w_down, w_ple, w_pg, pc, convp, out_d = (L[k] for k in ("w_glu", "w_o", "w_up", "w_down", "w_ple", "w_pg", "pc", "convp", "out_d"))
    OB = QB[1:]

    def nb():
        b = rr["ps"] % 8
        rr["ps"] += 1
        return b

    r1 = sb(gs, "r1", [128, 16, NQ], F32)
    h1b = catT
    mean_t = sb(gs, "mean_t", [128, NQ], F32)
    rstd_t = sb(gs, "rstd_t", [128, NQ], F32)
    sqt = [sb(gs, f"sqt{i}", [128, 512], F32) for i in range(2)]

    def layer_norm_fm(blocks, gcols, bcols, key):
        for (c0, c1) in blocks:
            n = c1 - c0
            bs, bq = nb(), nb()
            for kt in range(16):
                P.op("pe", lambda e, kt=kt, bs=bs, c0=c0, c1=c1, n=n: e.matmul(ps[bs][:, 0:n], onesf[:], r1[:, kt, c0:c1], start=(kt == 0), stop=(kt == 15)),
                     reads=[(key, kt, c0), "onesf"], writes=[psk[bs]])
            for kt in range(16):
                si = kt % 2
                P.op("act", lambda e, kt=kt, si=si, c0=c0, c1=c1, n=n: e.activation(out=sqt[si][:, 0:n], in_=r1[:, kt, c0:c1], func=AF.Square),
                     reads=[(key, kt, c0)], writes=[("sqt", si)])
                P.op("pe", lambda e, kt=kt, si=si, bq=bq, n=n: e.matmul(ps[bq][:, 0:n], onesf[:], sqt[si][:, 0:n], start=(kt == 0), stop=(kt == 15)),
                     reads=[("sqt", si), "onesf"], writes=[psk[bq]])
            P.op("dve", lambda e, bs=bs, c0=c0, c1=c1, n=n: e.tensor_scalar(out=mean_t[:, c0:c1], in0=ps[bs][:, 0:n], scalar1=1.0 / D, scalar2=None, op0=ALU.mult),
                 reads=[psk[bs]], writes=[("mean", c0)])
            P.op("dve", lambda e, c0=c0, c1=c1: e.tensor_tensor(out=rstd_t[:, c0:c1], in0=mean_t[:, c0:c1], in1=mean_t[:, c0:c1], op=ALU.mult),
                 reads=[("mean", c0)], writes=[("rstd", c0)])
            P.op("dve", lambda e, bq=bq, c0=c0, c1=c1, n=n: e.scalar_tensor_tensor(out=rstd_t[:, c0:c1], in0=ps[bq][:, 0:n], scalar=1.0 / D, in1=rstd_t[:, c0:c1],
                                                                                  op0=ALU.mult, op1=ALU.subtract),
                 reads=[psk[bq], ("rstd", c0)], writes=[("rstd", c0)])
            P.op("act", lambda e, c0=c0, c1=c1: e.activation(out=rstd_t[:, c0:c1], in_=rstd_t[:, c0:c1], func=AF.Sqrt, bias=EPS, scale=1.0),
                 reads=[("rstd", c0)], writes=[("rstd", c0)])
            P.op("dve", lambda e, c0=c0, c1=c1: e.reciprocal(out=rstd_t[:, c0:c1], in_=rstd_t[:, c0:c1]), reads=[("rstd", c0)], writes=[("rstd", c0)])
            for kt in range(16):
                P.op("dve", lambda e, kt=kt, c0=c0, c1=c1: e.tensor_tensor(out=r1[:, kt, c0:c1], in0=r1[:, kt, c0:c1], in1=mean_t[:, c0:c1], op=ALU.subtract),
                     reads=[(key, kt, c0), ("mean", c0)], writes=[(key, kt, c0)])
                P.op("dve", lambda e, kt=kt, c0=c0, c1=c1: e.tensor_tensor(out=r1[:, kt, c0:c1], in0=r1[:, kt, c0:c1], in1=rstd_t[:, c0:c1], op=ALU.mult),
                     reads=[(key, kt, c0), ("rstd", c0)], writes=[(key, kt, c0)])
                P.op("dve", lambda e, kt=kt, c0=c0, c1=c1: e.tensor_scalar(out=r1[:, kt, c0:c1], in0=r1[:, kt, c0:c1], scalar1=gcols[:, kt:kt + 1], scalar2=bcols[:, kt:kt + 1],
                                                                          op0=ALU.mult, op1=ALU.add),
                     reads=[(key, kt, c0), "vec"], writes=[(key, kt, c0)])

    def wtile(st, name, n):
        return [sb(st, f"{name}{i}", [128, n, 128], BF16) for i in range(2)]

    with ExitStack() as s4:
      with ExitStack() as s4g:
        Wg = sb(s4g, "Wg", [128, 8, 1024], BF16)
        for kt in range(8):
            P.dma(f"wg{kt % 4}", Wg[:, kt, :], w_glu[kt * 128:(kt + 1) * 128, :], writes=[("Wg", kt)], q="pool")
        sg = [sb(s4g, f"sg{i}", [128, 512], F32) for i in range(2)]
        nsg = 0
        for mt in range(8):
            for (c0, c1) in QB:
                n = c1 - c0
                b = nb()
                for kt in range(8):
                    P.op("pe", lambda e, kt=kt, b=b, mt=mt, c0=c0, c1=c1, n=n: e.matmul(ps[b][:, 0:n], Wg[:, kt, mt * 128:(mt + 1) * 128], ygT[:, kt, c0:c1],
                                                                                         start=(kt == 0), stop=(kt == 7)),
                         reads=[("Wg", kt), ("ygT", kt)], writes=[psk[b]])
                si = nsg % 2
                nsg += 1
                P.op("act", lambda e, b=b, si=si, mt=mt, n=n: e.activation(out=sg[si][:, 0:n], in_=ps[b][:, 0:n], func=AF.Sigmoid, bias=MISC[:, mt:mt + 1], scale=1.0),
                     reads=[psk[b], "vec"], writes=[("sg", si)])
                P.op("dve", lambda e, si=si, mt=mt, c0=c0, c1=c1, n=n: e.tensor_tensor(out=catT[:, 8 + mt, c0:c1], in0=ygT[:, mt, c0:c1], in1=sg[si][:, 0:n], op=ALU.mult),
                     reads=[("sg", si), ("ygT", mt)], writes=[("catT", 8 + mt, c0)])
        P.barrier()
        Wo = [sb(s4g, f"Wo{i}", [128, 16, 512], BF16) for i in range(2)]
        h0m = [sb(s4g, f"h0m{i}", [128, NQ], F32) for i in range(2)]
        wo_v = w_o.rearrange("(kt p) m -> p kt m", p=128)
        for mt in range(16):
            wg_, mi = (mt // 4) % 2, mt % 4
            wi = mt % 2
            if mi == 0:
                for kq in range(4):
                    P.dma(f"wo{wg_}", Wo[wg_][:, kq * 4:(kq + 1) * 4, :], wo_v[:, kq * 4:(kq + 1) * 4, (mt // 4) * 512:(mt // 4 + 1) * 512],
                          writes=[("Wo", wg_)], q="pool")
            P.dma(f"h0m{wi}", h0m[wi][:, HO:NQ], H0_scr[mt, :, HO:NQ], writes=[("h0m", wi)])
            P.op("dve", lambda e, wi=wi, mt=mt: e.tensor_copy(out=h0m[wi][:, 0:HO], in_=L["halo_f"][:, mt, :]), reads=[("h0m", wi)], writes=[("h0m", wi)])
            for (c0, c1) in QB:
                n = c1 - c0
                b = nb()
                for kt in range(16):
                    P.op("pe", lambda e, kt=kt, b=b, wg_=wg_, mi=mi, c0=c0, c1=c1, n=n: e.matmul(ps[b][:, 0:n], Wo[wg_][:, kt, mi * 128:(mi + 1) * 128], catT[:, kt, c0:c1], start=(kt == 0), stop=(kt == 15)),
                         reads=[("Wo", wg_), ("catT", kt, c0)], writes=[psk[b]])
                P.op("dve", lambda e, b=b, wi=wi, mt=mt, c0=c0, c1=c1, n=n: e.scalar_tensor_tensor(out=r1[:, mt, c0:c1], in0=h0m[wi][:, c0:c1], scalar=ALPHA, in1=ps[b][:, 0:n],
                                                                                                  op0=ALU.mult, op1=ALU.add),
                     reads=[psk[b], ("h0m", wi)], writes=[("r1", mt, c0)])
        layer_norm_fm(QB, G1, B1, "r1")
        for kt in range(16):
            P.op("act", lambda e, kt=kt: e.activation(out=h1b[:, kt, :], in_=r1[:, kt, :], func=AF.Identity),
                 reads=[("r1", kt, c0) for (c0, c1) in QB], writes=[("h1b", kt)])
            P.op("dve", lambda e, kt=kt: e.tensor_scalar(out=h1b[:, kt, HO - 2:HO], in0=h1b[:, kt, HO - 2:HO], scalar1=valid[:, 16:17], scalar2=None, op0=ALU.mult),
                 reads=[("h1b", kt), "valid"], writes=[("h1b", kt)])
        P.barrier()

    with ExitStack() as s5:
        pTb = sb(s5, "pTb", [128, 2, OWN], BF16)
        pt = [sb(s5, f"pt{i}", [128, 256], F32) for i in range(2)]
        for tt in range(8):
            pi_ = tt % 2
            P.dma(f"pt{pi_}", pt[pi_][:], pc[tt * 128:(tt + 1) * 128, :], writes=[("pt", pi_)])
            b = nb()
            for k2 in range(2):
                P.op("pe", lambda e, k2=k2, b=b, pi_=pi_: e.transpose(out=ps[b][:, k2 * 128:(k2 + 1) * 128], in_=pt[pi_][:, k2 * 128:(k2 + 1) * 128], identity=ident[:]),
                     reads=[("pt", pi_), "ident"], writes=[psk[b]])
            for k2 in range(2):
                P.op("dve", lambda e, k2=k2, b=b, tt=tt: e.tensor_copy(out=pTb[:, k2, tt * 128:(tt + 1) * 128], in_=ps[b][:, k2 * 128:(k2 + 1) * 128]),
                     reads=[psk[b]], writes=[("pTb", tt)])
        Wpg = [sb(s5, f"Wpg{i}", [128, 16, 512], BF16) for i in range(2)]
        Wpl = [sb(s5, f"Wpl{i}", [128, 2, 512], BF16) for i in range(2)]
        sg = [sb(s5, f"sgp{i}", [128, 512], F32) for i in range(2)]
        wpg_v = w_pg.rearrange("(kt p) m -> p kt m", p=128)
        wpl_v = w_ple.rearrange("(kt p) m -> p kt m", p=128)
        nsg = 0
        for mt in range(16):
            wi, mi = (mt // 4) % 2, mt % 4
            if mi == 0:
                for kq in range(4):
                    P.dma(f"wpg{wi}", Wpg[wi][:, kq * 4:(kq + 1) * 4, :], wpg_v[:, kq * 4:(kq + 1) * 4, (mt // 4) * 512:(mt // 4 + 1) * 512],
                          writes=[("Wpg", wi)], q="pool")
                P.dma(f"wpl{wi}", Wpl[wi][:], wpl_v[:, :, (mt // 4) * 512:(mt // 4 + 1) * 512], writes=[("Wpl", wi)], q="pool")
            for (c0, c1) in OB:
                bg, bp = nb(), nb()
                for kt in range(16):
                    P.op("pe", lambda e, kt=kt, bg=bg, wi=wi, mi=mi, c0=c0, c1=c1: e.matmul(ps[bg][:], Wpg[wi][:, kt, mi * 128:(mi + 1) * 128], h1b[:, kt, c0:c1], start=(kt == 0), stop=(kt == 15)),
                         reads=[("Wpg", wi), ("h1b", kt)], writes=[psk[bg]])
                for k2 in range(2):
                    P.op("pe", lambda e, k2=k2, bp=bp, wi=wi, mi=mi, c0=c0, c1=c1: e.matmul(ps[bp][:], Wpl[wi][:, k2, mi * 128:(mi + 1) * 128], pTb[:, k2, c0 - HO:c1 - HO], start=(k2 == 0), stop=(k2 == 1)),
                         reads=[("Wpl", wi)] + [("pTb", tt) for tt in range(8)], writes=[psk[bp]])
                si = nsg % 2
                nsg += 1
                P.op("act", lambda e, bg=bg, si=si, mt=mt: e.activation(out=sg[si][:], in_=ps[bg][:], func=AF.Sigmoid, bias=BPG[:, mt:mt + 1], scale=1.0),
                     reads=[psk[bg], "vec"], writes=[("sgp", si)])
                P.op("dve", lambda e, bp=bp, si=si: e.tensor_tensor(out=sg[si][:], in0=ps[bp][:], in1=sg[si][:], op=ALU.mult),
                     reads=[psk[bp], ("sgp", si)], writes=[("sgp", si)])
                P.op("dve", lambda e, si=si, mt=mt, c0=c0, c1=c1: e.scalar_tensor_tensor(out=r1[:, mt, c0:c1], in0=r1[:, mt, c0:c1], scalar=ALPHA, in1=sg[si][:],
                                                                                        op0=ALU.mult, op1=ALU.add),
                     reads=[("sgp", si), ("r1", mt, c0)], writes=[("r1", mt, c0)])
        P.barrier()

    with ExitStack() as s6:
        cvp = sb(s6, "cvp", [128, 4, 88], F32)
        P.dma("cvp", cvp[:].rearrange("p a b -> p (a b)"), convp[:, :], writes=["cvp"])
        Wu = [sb(s6, f"Wu{i}", [128, 16, 512], BF16) for i in range(2)]
        Wd = sb(s6, "Wd", [128, 4, D], BF16)
        actb = sb(s6, "actb", [128, 4, OWN], BF16)
        hv = [sb(s6, f"hv{i}", [128, NQ], F32) for i in range(2)]
        cv = [sb(s6, f"cv{i}", [128, OWN], F32) for i in range(2)]
        wu_v = w_up.rearrange("(kt p) m -> p kt m", p=128)
        for jg in range(11):
            for jj in range(4):
                P.dma("wd", Wd[:, jj, :], w_down[(jg * 4 + jj) * 128:(jg * 4 + jj + 1) * 128, :], writes=[("Wd", jj)], q="pool")
            for vg in range(2):
                for kq in range(4):
                    P.dma(f"wu{vg}", Wu[vg][:, kq * 4:(kq + 1) * 4, :], wu_v[:, kq * 4:(kq + 1) * 4, vg * DFF + jg * 512:vg * DFF + (jg + 1) * 512],
                          writes=[("Wu", vg)], q="pool")
            for jj in range(4):
                j = jg * 4 + jj
                for vg in range(2):
                    ft = vg * 44 + j
                    for (c0, c1) in QB:
                        n = c1 - c0
                        b = nb()
                        for kt in range(16):
                            P.op("pe", lambda e, kt=kt, b=b, vg=vg, jj=jj, c0=c0, c1=c1, n=n: e.matmul(ps[b][:, 0:n], Wu[vg][:, kt, jj * 128:(jj + 1) * 128], h1b[:, kt, c0:c1], start=(kt == 0), stop=(kt == 15)),
                                 reads=[("Wu", vg), ("h1b", kt)], writes=[psk[b]])
                        affine_evac(evac_eng(), hv[vg][:, c0:c1], ps[b][:, 0:n], reads=[psk[b]], writes=[("hv", vg)])
                    P.op("dve", lambda e, vg=vg, ft=ft: e.tensor_scalar(out=cv[vg][:], in0=hv[vg][:, HO - 2:HO - 2 + OWN], scalar1=cvp[:, 0, ft:ft + 1], scalar2=cvp[:, 3, ft:ft + 1],
                                                                       op0=ALU.mult, op1=ALU.add),
                         reads=[("hv", vg), "cvp"], writes=[("cv", vg)])
                    for tap in (1, 2):
                        P.op("dve", lambda e, vg=vg, ft=ft, tap=tap: e.scalar_tensor_tensor(out=cv[vg][:], in0=hv[vg][:, HO - 2 + tap:HO - 2 + tap + OWN], scalar=cvp[:, tap, ft:ft + 1], in1=cv[vg][:],
                                                                                             op0=ALU.mult, op1=ALU.add),
                             reads=[("hv", vg), ("cv", vg), "cvp"], writes=[("cv", vg)])
                P.op("act", lambda e: e.activation(out=cv[1][:], in_=cv[1][:], func=AF.Gelu_apprx_tanh), reads=[("cv", 1)], writes=[("cv", 1)])
                P.op("dve", lambda e, jj=jj: e.tensor_tensor(out=actb[:, jj, :], in0=cv[0][:], in1=cv[1][:], op=ALU.mult),
                     reads=[("cv", 0), ("cv", 1)], writes=[("actb", jj)])
            for mt in range(16):
                for (c0, c1) in OB:
                    b = nb()
                    for jj in range(4):
                        P.op("pe", lambda e, jj=jj, b=b, mt=mt, c0=c0, c1=c1: e.matmul(ps[b][:], Wd[:, jj, mt * 128:(mt + 1) * 128], actb[:, jj, c0 - HO:c1 - HO], start=(jj == 0), stop=(jj == 3)),
                             reads=[("Wd", jj), ("actb", jj)], writes=[psk[b]])
                    P.op("dve", lambda e, b=b, mt=mt, c0=c0, c1=c1: e.tensor_tensor(out=r1[:, mt, c0:c1], in0=r1[:, mt, c0:c1], in1=ps[b][:], op=ALU.add),
                         reads=[psk[b], ("r1", mt, c0)], writes=[("r1", mt, c0)])
        P.barrier()

    layer_norm_fm(OB, G2, B2, "r1")
    with ExitStack() as s7:
        ot = [sb(s7, f"ot{i}", [128, D], F32) for i in range(2)]
        for tt in range(8):
            oi = tt % 2
            for k4 in range(4):
                b = nb()
                for j in range(4):
                    kt = k4 * 4 + j
                    P.op("pe", lambda e, kt=kt, j=j, b=b, tt=tt: e.transpose(out=ps[b][:, j * 128:(j + 1) * 128], in_=r1[:, kt, HO + tt * 128:HO + (tt + 1) * 128], identity=ident[:]),
                         reads=[("r1", kt, c0) for (c0, c1) in OB] + ["ident"], writes=[psk[b]])
                affine_evac(evac_eng(), ot[oi][:, k4 * 512:(k4 + 1) * 512], ps[b][:], reads=[psk[b]], writes=[("ot", oi)])
            P.dma(f"ot{oi}", out_d[tt * 128:(tt + 1) * 128, :], ot[oi][:], reads=[("ot", oi)], writes=[("out", tt)])
        P.barrier()


_CACHE = {}


def _cols(v, n):
    return np.ascontiguousarray(np.asarray(v, np.float32).reshape(n, 128).T)


def kernel(x, p, ln_in_g, ln_in_b, w_in, lambda_q1, lambda_k1, lambda_q2, lambda_k2, g_subln, a_re, a_im, log_dt,
           b_re, b_im, c_re, c_im, d_skip, w_glu, b_glu, w_o, ln1_g, ln1_b, w_up, conv_w, conv_b, w_down, w_ple,
           w_pg, b_pg, ln2_g, ln2_b):
    f = lambda a: np.asarray(a, np.float32)
    x = f(x)[0]
    pp = f(p)[0, 0]
    if "nc" not in _CACHE:
        _CACHE["nc"] = build_program()
    nc = _CACHE["nc"]
    vecs = np.zeros((128, 144), np.float32)
    for i, v in enumerate([ln_in_g, ln_in_b, f(ln1_g)[0], f(ln1_b)[0], f(ln2_g)[0], f(ln2_b)[0], f(b_pg)[0]]):
        vecs[:, i * 16:(i + 1) * 16] = _cols(v, 16)
    vecs[:, 112:120] = _cols(f(b_glu)[0], 8)
    vecs[:, 120] = f(g_subln)[0]
    vecs[:, 128:136] = _cols(f(d_skip)[0], 8)
    convp = np.zeros((128, 4, 88), np.float32)
    for t in range(3):
        convp[:, t, :] = _cols(f(conv_w)[0, t], 88)
    convp[:, 3, :] = _cols(f(conv_b)[0], 88)
    lamv = np.stack([np.broadcast_to(f(v)[0], (128, 64)) for v in (lambda_q1, lambda_k1, lambda_q2, lambda_k2)], axis=1)
    ar, ai, ldt = f(a_re)[0], f(a_im)[0], f(log_dt)[0]
    br, bi, cr, ci = f(b_re)[0], f(b_im)[0], f(c_re)[0], f(c_im)[0]
    def clay(a_gn):
        a = a_gn.reshape(8, 8, 1, 64)
        a = np.broadcast_to(a, (8, 8, 16, 64))
        return a.transpose(1, 2, 0, 3).reshape(128, 512)
    def clay_b(b_gnc):
        a = b_gnc.reshape(8, 8, 64, 16).transpose(1, 3, 0, 2)
        return a.reshape(128, 512)
    ssmC = np.stack([clay(ar), clay(ai), clay_b(br), clay_b(bi), clay(np.broadcast_to(ldt[:, None], (64, 64)))], axis=1)
    def slay(a_gn):
        a = a_gn.reshape(32, 2, 64, 1)
        a = np.broadcast_to(a, (32, 2, 64, 16))
        return a.transpose(1, 2, 0, 3).reshape(128, 512)
    def slay_c(c_gcn):
        a = c_gcn.reshape(32, 2, 16, 64).transpose(1, 3, 0, 2)
        return a.reshape(128, 512)
    def slay_b(b_gnc):
        a = b_gnc.reshape(32, 2, 64, 16).transpose(1, 2, 0, 3)
        return a.reshape(128, 512)
    ssmS = np.stack([slay(ar), slay(ai), slay(np.broadcast_to(ldt[:, None], (64, 64))), slay_c(cr), slay_c(ci), slay_b(br), slay_b(bi)], axis=1)
    ident = np.eye(128, dtype=np.float32)
    slopes = 2.0 ** (-np.arange(1, 9, dtype=np.float64))
    kk = np.arange(128)[:, None]
    qq = np.arange(128)[None, :]
    dtile = np.zeros((128, 8, 128), np.float32)
    for h in range(8):
        dmat = 8.0 * (-slopes[h] * np.abs(qq - kk) + slopes[h] * (qq - 127))
        dmat = np.where((kk // 64) <= (qq // 64), dmat, NEG)
        dtile[:, h, :] = dmat
    shared = {
        "w_in": f(w_in)[0], "w_glu": f(w_glu)[0], "w_o": f(w_o)[0], "w_up": f(w_up)[0], "w_down": f(w_down)[0],
        "w_ple": f(w_ple)[0], "w_pg": f(w_pg)[0], "vecs": vecs, "convp": convp.reshape(128, -1),
        "lamv": np.ascontiguousarray(lamv.reshape(128, -1)), "ident": ident, "dtile": dtile.reshape(128, -1),
        "ssmC": np.ascontiguousarray(ssmC.reshape(128, -1)), "ssmS": np.ascontiguousarray(ssmS.reshape(128, -1)),
    }
    in_maps = []
    origins = [7167] + [7168 + 128 * i + 127 for i in range(8)]
    diagt = [55] + [56 + i for i in range(8)]
    for c in range(NCORES):
        start = S - OWN * (c + 1)
        xcx = np.zeros((S, D), np.float32)
        xcx[start:] = x[:OWN * (c + 1)]
        kb = np.zeros((128, 8, 64, 9), np.float32)
        kpos = (np.arange(64)[None, :] * 128 + np.arange(128)[:, None]).astype(np.float64)
        vmask = kpos >= start
        for h in range(8):
            per = GW[h] // 128
            for s_ in range(9):
                if s_ == 0:
                    og = origins[0]
                else:
                    qt = s_ - 1
                    og = origins[1 + (qt // per) * per + per - 1]
                val = slopes[h] * (kpos - og)
                val[:, diagt[s_]] = -slopes[h] * (og - origins[s_])
                val = np.where(vmask, np.minimum(val, 0.0), NEG)
                kb[:, h, :, s_] = val
        vd = np.zeros((128, 17), np.float32)
        for tb in range(16):
            vd[:, tb] = 1.0 if tb * 512 >= start else 0.0
        vd[:, 16] = 1.0 if c > 0 else 0.0
        m = dict(shared)
        m.update({"xc": xcx, "pc": np.ascontiguousarray(pp[c * OWN:(c + 1) * OWN]), "kbias": kb.reshape(128, -1), "valid": vd})
        in_maps.append(m)
    res = run_bass_kernel_spmd(nc, in_maps, core_ids=list(range(NCORES)))
    out = np.concatenate([np.asarray(res.results[c]["out"], np.float32) for c in range(NCORES)], axis=0)
    return out[None].astype(np.float32)
```

```python
import math
from contextlib import ExitStack

import numpy as np
import concourse.bass as bass
import concourse.mybir as mybir
from concourse.bass_utils import run_bass_kernel_spmd

F32 = mybir.dt.float32
BF16 = mybir.dt.bfloat16
AF = mybir.ActivationFunctionType
ALU = mybir.AluOpType

NCORES = 8
S = 8192
D = 2048
OWN = 1024
NQ = 1040
HO = 16
QB = [(14, 16), (16, 528), (528, 1040)]
DFF = 5632
ALPHA = 2.0 ** 0.25
EPS = 1e-5
LAM_INIT = 0.8 - 0.6 * math.exp(0.0)
NEG = -1.0e30
GW = [128, 256, 512, 512, 512, 512, 512, 512]
DEBUG = False
PH1_BLOCKS = 16
NO_STORE = False
NO_HALO = False
STOP_AFTER = None


class Prog:
    ENG = ("pe", "act", "dve", "pool", "sp")

    def __init__(self, nc, stack, same_engine_sync=True):
        self.nc = nc
        self.stack = stack
        self.e = {"pe": nc.tensor, "act": nc.scalar, "dve": nc.vector, "pool": nc.gpsimd, "sp": nc.sync}
        self.sem = {k: stack.enter_context(nc.semaphore("s_" + k)) for k in ("pe", "act", "dve", "pool")}
        self.cnt = {k: 0 for k in self.sem}
        self.dsem = {}
        self.dcnt = {}
        self.seen = {}
        self.lastw = {}
        self.readers = {}
        self.ses = same_engine_sync
        self.nops = 0

    def _semof(self, tok):
        kind, src, val = tok
        return (self.sem[src] if kind == "eng" else self.dsem[src]), val

    def _wait(self, eng, tok):
        kind, src, val = tok
        if kind == "eng" and src == eng and (eng == "pe" or not self.ses):
            return
        key = (eng, kind, src)
        if self.seen.get(key, 0) >= val:
            return
        self.seen[key] = val
        s, v = self._semof(tok)
        self.e[eng].wait_ge(s, v)

    def _deps(self, eng, reads, writes):
        for b in reads:
            t = self.lastw.get(b)
            if t is not None:
                self._wait(eng, t)
        for b in writes:
            t = self.lastw.get(b)
            if t is not None:
                self._wait(eng, t)
            for t in self.readers.get(b, ()):
                self._wait(eng, t)

    def _commit(self, tok, reads, writes):
        for b in writes:
            self.lastw[b] = tok
            self.readers[b] = []
        for b in reads:
            r = self.readers.setdefault(b, [])
            r[:] = [t for t in r if (t[0], t[1]) != (tok[0], tok[1])]
            r.append(tok)

    def op(self, eng, fn, reads=(), writes=()):
        for k in reads:
            if isinstance(k, str) and k[:2] == "ps" and k[2:].isdigit():
                for t in self.readers.get(k, ()):
                    if t[1] != eng:
                        self._wait(eng, t)
        self._deps(eng, reads, writes)
        self.cnt[eng] += 1
        tok = ("eng", eng, self.cnt[eng])
        fn(self.e[eng]).then_inc(self.sem[eng], 1)
        self._commit(tok, reads, writes)
        self.nops += 1
        return tok

    def dma(self, slot, out, in_, reads=(), writes=(), q="sp", **kw):
        if slot not in self.dsem:
            self.dsem[slot] = self.stack.enter_context(self.nc.semaphore("d_" + slot))
            self.dcnt[slot] = 0
        self._deps(q, reads, writes)
        self.dcnt[slot] += 16
        tok = ("dma", slot, self.dcnt[slot])
        self.e[q].dma_start(out=out, in_=in_, **kw).then_inc(self.dsem[slot], 16)
        self._commit(tok, reads, writes)
        self.nops += 1
        return tok

    def barrier(self):
        toks = [("eng", k, self.cnt[k]) for k in self.sem if self.cnt[k] > 0]
        toks += [("dma", s, c) for s, c in self.dcnt.items()]
        for eng in self.ENG:
            for t in toks:
                self._wait(eng, t)
        self.lastw.clear()
        self.readers.clear()

    def finish(self):
        for k in self.sem:
            if self.cnt[k]:
                self._wait("sp", ("eng", k, self.cnt[k]))
        for s, c in self.dcnt.items():
            self._wait("sp", ("dma", s, c))


def build_program():
    nc = bass.Bass("TRN2", target_bir_lowering=False)

    def din(name, shape, dt=F32):
        return nc.dram_tensor(name, list(shape), dt, kind="ExternalInput").ap()

    def dscr(name, shape, dt):
        return nc.dram_tensor(name, list(shape), dt, kind="Internal").ap()

    xc = din("xc", [S, D])
    pc = din("pc", [OWN, 256])
    w_in = din("w_in", [D, 4096])
    w_glu = din("w_glu", [1024, 1024])
    w_o = din("w_o", [D, D])
    w_up = din("w_up", [D, 2 * DFF])
    w_down = din("w_down", [DFF, D])
    w_ple = din("w_ple", [256, D])
    w_pg = din("w_pg", [D, D])
    vecs = din("vecs", [128, 16 * 9])
    convp = din("convp", [128, 4 * 88])
    lamv = din("lamv", [128, 4 * 64])
    ident_d = din("ident", [128, 128])
    kbias_d = din("kbias", [128, 8 * 64 * 9])
    dt_d = din("dtile", [128, 8 * 128])
    valid_d = din("valid", [128, 17])
    ssmC = din("ssmC", [128, 5 * 512])
    ssmS = din("ssmS", [128, 7 * 512])
    out_d = nc.dram_tensor("out", [OWN, D], F32, kind="ExternalOutput").ap()
    if DEBUG:
        dbg_cat = nc.dram_tensor("dbg_cat", [16, 128, NQ], F32, kind="ExternalOutput").ap()

    KT_scr = dscr("KT_scr", [8, 128, S], BF16)
    V_scr = dscr("V_scr", [64, 128, 1024], BF16)
    UT_scr = dscr("UT_scr", [8, 128, S], BF16)
    H0_scr = dscr("H0_scr", [16, 128, NQ], F32)
    XT_scr = dscr("XT_scr", [16, 128, NQ], BF16)

    with ExitStack() as gs:
        P = Prog(nc, gs)

        def sb(st, name, shape, dt):
            return st.enter_context(nc.sbuf_tensor("sb_" + name, list(shape), dt))

        ps = [gs.enter_context(nc.psum_tensor(f"ps{i}", [128, 512], F32)) for i in range(8)]
        psk = [f"ps{i}" for i in range(8)]

        ident = sb(gs, "ident", [128, 128], F32)
        identb = sb(gs, "identb", [128, 128], BF16)
        onesb = sb(gs, "onesb", [128, 128], BF16)
        onesf = sb(gs, "onesf", [128, 128], F32)
        vec = sb(gs, "vec", [128, 144], F32)
        valid = sb(gs, "valid", [128, 17], F32)
        lamcol = sb(gs, "lamcol", [128, 4], F32)
        halo_f = sb(gs, "halo_f", [128, 16, 16], F32)
        halo_b = sb(gs, "halo_b", [128, 16, 16], BF16)
        P.dma("c0", ident[:], ident_d[:, :], writes=["ident"])
        P.dma("c1", vec[:], vecs[:, :], writes=["vec"])
        P.dma("c2", valid[:], valid_d[:, :], writes=["valid"])
        P.op("dve", lambda e: e.tensor_copy(out=identb[:], in_=ident[:]), reads=["ident"], writes=["identb"])
        P.op("dve", lambda e: e.memset(onesb[:], 1.0), writes=["onesb"])
        P.op("dve", lambda e: e.memset(onesf[:], 1.0), writes=["onesf"])
        G_IN, B_IN, G1, B1, G2, B2, BPG, MISC = [vec[:, i * 16:(i + 1) * 16] for i in range(8)]

        with ExitStack() as s0:
            lv = sb(s0, "lv", [128, 4, 64], F32)
            lt = sb(s0, "lt", [128, 64], F32)
            ld = sb(s0, "ld", [128, 2], F32)
            P.dma("c3", lv[:].rearrange("p a b -> p (a b)"), lamv[:, :], writes=["lv"])
            for i in range(2):
                P.op("dve", lambda e, i=i: e.tensor_tensor(out=lt[:], in0=lv[:, 2 * i, :], in1=lv[:, 2 * i + 1, :], op=ALU.mult),
                     reads=["lv"], writes=["lt"])
                P.op("dve", lambda e, i=i: e.tensor_reduce(out=ld[:, i:i + 1], in_=lt[:], axis=mybir.AxisListType.X, op=ALU.add),
                     reads=["lt"], writes=[("ld", i)])
            P.op("act", lambda e: e.activation(out=ld[:], in_=ld[:], func=AF.Exp), reads=[("ld", 0), ("ld", 1)], writes=["ld"])
            P.op("dve", lambda e: e.tensor_tensor(out=lamcol[:, 0:1], in0=ld[:, 1:2], in1=ld[:, 0:1], op=ALU.subtract),
                 reads=["ld"], writes=["lamcol"])
            P.op("dve", lambda e: e.tensor_scalar(out=lamcol[:, 0:1], in0=lamcol[:, 0:1], scalar1=-LAM_INIT, scalar2=None, op0=ALU.add),
                 reads=["lamcol"], writes=["lamcol"])
            P.barrier()

        rr = {"ps": 0, "ev": 0}

        def evac_eng():
            rr["ev"] += 1
            return "dve" if rr["ev"] % 2 else "act"

        def affine_evac(eng, out, in_, scale=None, bias=None, reads=(), writes=()):
            if eng == "act":
                kw = {}
                if scale is not None:
                    kw["scale"] = scale
                if bias is not None:
                    kw["bias"] = bias
                P.op("act", lambda e: e.activation(out=out, in_=in_, func=AF.Identity, **kw), reads=reads, writes=writes)
            else:
                if scale is None and bias is None:
                    P.op("dve", lambda e: e.tensor_copy(out=out, in_=in_), reads=reads, writes=writes)
                elif bias is None:
                    P.op("dve", lambda e: e.tensor_scalar(out=out, in0=in_, scalar1=scale, scalar2=None, op0=ALU.mult), reads=reads, writes=writes)
                elif scale is None:
                    P.op("dve", lambda e: e.tensor_scalar(out=out, in0=in_, scalar1=bias, scalar2=None, op0=ALU.add), reads=reads, writes=writes)
                else:
                    P.op("dve", lambda e: e.tensor_scalar(out=out, in0=in_, scalar1=scale, scalar2=bias, op0=ALU.mult, op1=ALU.add), reads=reads, writes=writes)

        with ExitStack() as s1:
            W = sb(s1, "W1", [128, 16, 3072], BF16)
            for kt in range(16):
                for cb in range(3):
                    P.dma(f"w1_{kt}", W[:, kt, cb * 1024:(cb + 1) * 1024],
                          w_in[kt * 128:(kt + 1) * 128, 1024 + cb * 1024:1024 + (cb + 1) * 1024],
                          writes=[("W1", kt)], q="pool")
            xq = [sb(s1, f"xq{i}", [128, D], F32) for i in range(2)]
            xh = [sb(s1, f"xh{i}", [128, D], F32) for i in range(4)]
            h0T = [sb(s1, "h0T0", [128, 16, 512], BF16)]
            stats = [sb(s1, f"stats{i}", [128, 4, 6], F32) for i in range(2)]
            mv = [sb(s1, f"mv{i}", [128, 2], F32) for i in range(2)]
            rstd = [sb(s1, f"rstd{i}", [128, 1], F32) for i in range(2)]
            nmr = [sb(s1, f"nmr{i}", [128, 1], F32) for i in range(2)]
            stg = [sb(s1, f"stg{i}", [128, 512], BF16) for i in range(6)]
            h0f = [sb(s1, f"h0f{i}", [128, 512], F32) for i in range(2)]
            nstg = 0
            nx = 0
            for tb in range(16 - PH1_BLOCKS, 16):
                hb = 0
                r0 = tb * 512
                for tt in range(4):
                    xi = nx % 2
                    nx += 1
                    P.dma(f"x{xi}", xq[xi][:], xc[r0 + tt * 128:r0 + (tt + 1) * 128, :], writes=[("xq", xi)])
                    for c in range(4):
                        P.op("dve", lambda e, c=c, xi=xi: e.bn_stats(out=stats[xi][:, c, :], in_=xq[xi][:, c * 512:(c + 1) * 512]),
                             reads=[("xq", xi)], writes=[("stats", xi, c)])
                    P.op("dve", lambda e, xi=xi: e.bn_aggr(out=mv[xi][:], in_=stats[xi][:].rearrange("p a b -> p (a b)")),
                         reads=[("stats", xi, c) for c in range(4)], writes=[("mv", xi)])
                    P.op("act", lambda e, xi=xi: e.activation(out=rstd[xi][:], in_=mv[xi][:, 1:2], func=AF.Sqrt, bias=EPS, scale=1.0),
                         reads=[("mv", xi)], writes=[("rstd", xi)])
                    P.op("dve", lambda e, xi=xi: e.reciprocal(out=rstd[xi][:], in_=rstd[xi][:]), reads=[("rstd", xi)], writes=[("rstd", xi)])
                    P.op("dve", lambda e, xi=xi: e.scalar_tensor_tensor(out=nmr[xi][:], in0=mv[xi][:, 0:1], scalar=-1.0, in1=rstd[xi][:],
                                                                        op0=ALU.mult, op1=ALU.mult),
                         reads=[("mv", xi), ("rstd", xi)], writes=[("nmr", xi)])
                    P.op("act", lambda e, xi=xi, tt=tt: e.activation(out=xh[tt][:], in_=xq[xi][:], func=AF.Identity, bias=nmr[xi][:], scale=rstd[xi][:]),
                         reads=[("xq", xi), ("nmr", xi), ("rstd", xi)], writes=[("xh", tt)])
                for kt in range(16):
                    b = kt % 2
                    for tt in range(4):
                        P.op("pe", lambda e, kt=kt, tt=tt, b=b: e.transpose(out=ps[b][:, tt * 128:(tt + 1) * 128],
                                                                          in_=xh[tt][:, kt * 128:(kt + 1) * 128], identity=ident[:]),
                             reads=[("xh", tt), "ident"], writes=[psk[b]])
                    affine_evac(evac_eng(), h0T[hb][:, kt, :], ps[b][:], scale=G_IN[:, kt:kt + 1], bias=B_IN[:, kt:kt + 1],
                                reads=[psk[b], "vec"], writes=[("h0T", hb, kt)])
                    if tb == 13:
                        affine_evac("dve", halo_f[:, kt, :], ps[b][:, 496:512], scale=G_IN[:, kt:kt + 1], bias=B_IN[:, kt:kt + 1],
                                    reads=[psk[b], "vec"], writes=[("halo_f", kt)])
                        P.op("dve", lambda e, kt=kt: e.tensor_copy(out=halo_b[:, kt, :], in_=h0T[hb][:, kt, 496:512]),
                             reads=[("h0T", hb, kt)], writes=[("halo_b", kt)])
                    if tb >= 14:
                        fi = kt % 2
                        affine_evac("dve", h0f[fi][:], ps[b][:], scale=G_IN[:, kt:kt + 1], bias=B_IN[:, kt:kt + 1],
                                    reads=[psk[b], "vec"], writes=[("h0f", fi)])
                        dc = HO + (tb - 14) * 512
                        P.dma(f"h0s{fi}", H0_scr[kt, :, dc:dc + 512], h0f[fi][:], reads=[("h0f", fi)], writes=[("H0", kt, tb)])
                        P.dma(f"xts{fi}", XT_scr[kt, :, dc:dc + 512], h0T[hb][:, kt, :], reads=[("h0T", hb, kt)], writes=[("XT", kt, tb)])
                hreads = [("h0T", hb, kt) for kt in range(16)]

                def proj_fm(col0, dst, dkey, scale=None):
                    nonlocal nstg
                    b = 2 + rr["ps"] % 6
                    rr["ps"] += 1
                    for kt in range(16):
                        P.op("pe", lambda e, kt=kt, b=b: e.matmul(ps[b][:], W[:, kt, col0:col0 + 128], h0T[hb][:, kt, :],
                                                                  start=(kt == 0), stop=(kt == 15)),
                             reads=[("W1", kt), ("h0T", hb, kt)], writes=[psk[b]])
                    si = nstg % 6
                    nstg += 1
                    affine_evac(evac_eng(), stg[si][:], ps[b][:], scale=scale, reads=[psk[b], "valid"], writes=[("stg", si)])
                    if not NO_STORE:
                        P.dma(f"stg{si}", dst, stg[si][:], reads=[("stg", si)], writes=[dkey])

                for h in range(8):
                    proj_fm(h * 128, KT_scr[h, :, r0:r0 + 512], ("KT", h, tb))
                for t in range(8):
                    proj_fm(2048 + t * 128, UT_scr[t, :, r0:r0 + 512], ("UT", t, tb), scale=valid[:, tb:tb + 1])
                for tt in range(4):
                    for fh in range(2):
                        b = 2 + rr["ps"] % 6
                        rr["ps"] += 1
                        for kt in range(16):
                            P.op("pe", lambda e, kt=kt, b=b, tt=tt, fh=fh: e.matmul(
                                ps[b][:], h0T[hb][:, kt, tt * 128:(tt + 1) * 128], W[:, kt, 1024 + fh * 512:1024 + (fh + 1) * 512],
                                start=(kt == 0), stop=(kt == 15)),
                                reads=[("W1", kt), ("h0T", hb, kt)], writes=[psk[b]])
                        si = nstg % 6
                        nstg += 1
                        affine_evac(evac_eng(), stg[si][:], ps[b][:], reads=[psk[b]], writes=[("stg", si)])
                        if not NO_STORE:
                            P.dma(f"stg{si}", V_scr[tb * 4 + tt, :, fh * 512:(fh + 1) * 512], stg[si][:], reads=[("stg", si)],
                                  writes=[("V", tb * 4 + tt, fh)])
            P.barrier()
        if STOP_AFTER == 1:
            P.finish()
            return nc

        catT = sb(gs, "catT", [128, 16, NQ], BF16)
        with ExitStack() as s2:
            qT = sb(s2, "qT", [128, 8, NQ], BF16)
            dtile = sb(s2, "dtile", [128, 8, 128], BF16)
            P.dma("dt", dtile[:].rearrange("p a b -> p (a b)"), dt_d[:, :], writes=["dtile"], q="pool")
            with ExitStack() as s2a:
                xT = sb(s2a, "xT", [128, 16, NQ], BF16)
                Wq = sb(s2a, "Wq", [128, 16, 1024], BF16)
                for kt in range(16):
                    P.dma(f"xt{kt % 4}", xT[:, kt, HO:NQ], XT_scr[kt, :, HO:NQ], writes=[("xT", kt)])
                    P.op("dve", lambda e, kt=kt: e.tensor_copy(out=xT[:, kt, 0:HO], in_=halo_b[:, kt, :]), reads=[("xT", kt)], writes=[("xT", kt)])
                    P.dma(f"wq{kt % 4}", Wq[:, kt, :], w_in[kt * 128:(kt + 1) * 128, 0:1024], writes=[("Wq", kt)], q="pool")
                for h in range(8):
                    for (c0, c1) in QB:
                        b = rr["ps"] % 8
                        rr["ps"] += 1
                        for kt in range(16):
                            P.op("pe", lambda e, kt=kt, b=b, h=h, c0=c0, c1=c1: e.matmul(
                                ps[b][:, 0:c1 - c0], Wq[:, kt, h * 128:(h + 1) * 128], xT[:, kt, c0:c1], start=(kt == 0), stop=(kt == 15)),
                                reads=[("Wq", kt), ("xT", kt)], writes=[psk[b]])
                        affine_evac(evac_eng(), qT[:, h, c0:c1], ps[b][:, 0:c1 - c0], reads=[psk[b]], writes=[("qT", h, c0)])
                P.barrier()

            kT = [sb(s2, f"kT{i}", [128, S], BF16) for i in range(2)]
            vh = [sb(s2, f"vh{i}", [128, 64, 128], BF16) for i in range(2)]
            kb = [sb(s2, f"kb{i}", [128, 64, 9], F32) for i in range(2)]
            pT = [[sb(s2, f"pT{m}_{i}", [128, 512], BF16) for i in range(3)] for m in range(2)]
            rl = [sb(s2, f"rl{m}", [128, 512], F32) for m in range(2)]
            oo = sb(s2, "oo", [128, 512], F32)
            o2 = sb(s2, "o2", [128, 512], F32)
            sq = sb(s2, "sq", [128, 512], F32)
            gs8 = sb(s2, "gs8", [128, 1], F32)
            P.op("dve", lambda e: e.tensor_scalar(out=gs8[:], in0=MISC[:, 8:9], scalar1=1.0 - LAM_INIT, scalar2=None, op0=ALU.mult),
                 reads=["vec"], writes=["gs8"])
            npt = 0
            for h in range(8):
                hb = h % 2
                for j in range(4):
                    P.dma(f"kt{hb}", kT[hb][:, j * 2048:(j + 1) * 2048], KT_scr[h, :, j * 2048:(j + 1) * 2048], writes=[("kT", hb)])
                for j in range(4):
                    P.dma(f"vh{hb}", vh[hb][:, j * 16:(j + 1) * 16, :],
                          V_scr[j * 16:(j + 1) * 16, :, h * 128:(h + 1) * 128].rearrange("k p d -> p k d"), writes=[("vh", hb)])
                P.dma(f"kb{hb}", kb[hb][:].rearrange("p a b -> p (a b)"), kbias_d[:, h * 576:(h + 1) * 576], writes=[("kb", hb)])
                for qi, (c0, c1) in enumerate(QB):
                    n = c1 - c0
                    if qi == 0:
                        subs = [(0, 2, 0, 55)]
                        klast = 55
                    else:
                        subs = [(i * 128, 128, 1 + (qi - 1) * 4 + i, 56 + (qi - 1) * 4 + i) for i in range(4)]
                        klast = 56 + (qi - 1) * 4 + 3
                    acc = [4, 5, 6, 7]
                    per = GW[h] // 128
                    steps = []
                    for kt in range(klast + 1):
                        act_subs = [s_ for s_ in subs if s_[3] >= kt]
                        groups = []
                        if qi == 0:
                            groups = [(sl, sw, sid) for (sl, sw, sid, dk) in act_subs]
                        else:
                            for g0 in range(0, 4, per):
                                grp = [s_ for s_ in subs[g0:g0 + per] if s_[3] >= kt]
                                if len(grp) == per and all(s_[3] > kt for s_ in grp):
                                    groups.append((grp[0][0], sum(x[1] for x in grp), grp[-1][2]))
                                else:
                                    groups += [(sl, sw, sid) for (sl, sw, sid, dk) in grp]
                        steps.append((kt, act_subs, groups))

                    def emit_S(i):
                        kt, act_subs, groups = steps[i]
                        lo = act_subs[0][0]
                        hi = act_subs[-1][0] + act_subs[-1][1]
                        pi = i % 3
                        for m in range(2):
                            b = (i % 2) * 2 + m
                            P.op("pe", lambda e, m=m, b=b, kt=kt, lo=lo, hi=hi: e.matmul(
                                ps[b][:, lo:hi], kT[hb][m * 64:(m + 1) * 64, kt * 128:(kt + 1) * 128],
                                qT[m * 64:(m + 1) * 64, h, c0 + lo:c0 + hi], start=True, stop=True),
                                reads=[("kT", hb)] + [("qT", h, c0)], writes=[psk[b]])
                            for (sl, sw, sid, dk) in act_subs:
                                if dk == kt:
                                    dsl = dtile[:, h, 128 - sw:128] if sw < 128 else dtile[:, h, :]
                                    P.op("pe", lambda e, b=b, sl=sl, sw=sw, dsl=dsl: e.matmul(
                                        ps[b][:, sl:sl + sw], identb[:], dsl, start=False, stop=True, skip_group_check=True),
                                        reads=["identb", "dtile"], writes=[psk[b]])
                            for (sl, sw, sid) in groups:
                                P.op("act", lambda e, m=m, b=b, sl=sl, sw=sw, sid=sid, kt=kt, pi=pi: e.activation(
                                    out=pT[m][pi][:, sl:sl + sw], in_=ps[b][:, sl:sl + sw], func=AF.Exp,
                                    bias=kb[hb][:, kt, sid:sid + 1], scale=0.125),
                                    reads=[psk[b], ("kb", hb)], writes=[("pT", m, pi, sl)])

                    def emit_PV(i):
                        kt, act_subs, groups = steps[i]
                        lo = act_subs[0][0]
                        hi = act_subs[-1][0] + act_subs[-1][1]
                        pi = i % 3
                        first = (i == 0)
                        for m in range(2):
                            P.op("pe", lambda e, m=m, kt=kt, lo=lo, hi=hi, pi=pi, first=first: e.matmul(
                                ps[acc[m]][:, lo:hi], vh[hb][:, kt, :], pT[m][pi][:, lo:hi], start=first, stop=(kt == klast),
                                skip_group_check=True),
                                reads=[("vh", hb)] + [("pT", m, pi, g_[0]) for g_ in groups], writes=[psk[acc[m]]])
                            P.op("pe", lambda e, m=m, kt=kt, lo=lo, hi=hi, pi=pi, first=first: e.matmul(
                                ps[acc[2 + m]][:, lo:hi], onesb[:], pT[m][pi][:, lo:hi], start=first, stop=(kt == klast),
                                skip_group_check=True),
                                reads=["onesb"] + [("pT", m, pi, g_[0]) for g_ in groups], writes=[psk[acc[2 + m]]])

                    emit_S(0)
                    for i in range(len(steps)):
                        if i + 1 < len(steps):
                            emit_S(i + 1)
                        emit_PV(i)
                    for m in range(2):
                        P.op("dve", lambda e, m=m: e.tensor_scalar(out=rl[m][:, 0:n], in0=ps[acc[2 + m]][:, 0:n], scalar1=1e-37, scalar2=None, op0=ALU.add),
                             reads=[psk[acc[2 + m]]], writes=[("rl", m)])
                        P.op("dve", lambda e, m=m: e.reciprocal(out=rl[m][:, 0:n], in_=rl[m][:, 0:n]), reads=[("rl", m)], writes=[("rl", m)])
                    P.op("dve", lambda e: e.tensor_tensor(out=oo[:, 0:n], in0=ps[acc[0]][:, 0:n], in1=rl[0][:, 0:n], op=ALU.mult),
                         reads=[psk[acc[0]], ("rl", 0)], writes=["oo"])
                    P.op("dve", lambda e: e.tensor_tensor(out=o2[:, 0:n], in0=ps[acc[1]][:, 0:n], in1=rl[1][:, 0:n], op=ALU.mult),
                         reads=[psk[acc[1]], ("rl", 1)], writes=["o2"])
                    P.op("dve", lambda e: e.scalar_tensor_tensor(out=oo[:, 0:n], in0=o2[:, 0:n], scalar=lamcol[:, 0:1], in1=oo[:, 0:n],
                                                                 op0=ALU.mult, op1=ALU.add),
                         reads=["o2", "oo", "lamcol"], writes=["oo"])
                    P.op("act", lambda e: e.activation(out=sq[:, 0:n], in_=oo[:, 0:n], func=AF.Square), reads=["oo"], writes=["sq"])
                    P.op("pe", lambda e: e.matmul(ps[0][:, 0:n], onesf[:], sq[:, 0:n], start=True, stop=True),
                         reads=["onesf", "sq"], writes=[psk[0]])
                    P.op("act", lambda e: e.activation(out=sq[:, 0:n], in_=ps[0][:, 0:n], func=AF.Sqrt, bias=EPS, scale=1.0 / 128.0),
                         reads=[psk[0]], writes=["sq"])
                    P.op("dve", lambda e: e.reciprocal(out=sq[:, 0:n], in_=sq[:, 0:n]), reads=["sq"], writes=["sq"])
                    P.op("dve", lambda e: e.tensor_tensor(out=oo[:, 0:n], in0=oo[:, 0:n], in1=sq[:, 0:n], op=ALU.mult),
                         reads=["oo", "sq"], writes=["oo"])
                    P.op("dve", lambda e, h=h, c0=c0, c1=c1: e.tensor_scalar(out=catT[:, h, c0:c1], in0=oo[:, 0:n], scalar1=gs8[:], scalar2=None, op0=ALU.mult),
                         reads=["oo", "gs8"], writes=[("catT", h, c0)])
            P.barrier()
        if STOP_AFTER == 2:
            P.finish()
            return nc

        build_rest(nc, P, gs, sb, ps, psk, rr, affine_evac, evac_eng, locals())
        P.finish()
    return nc


def build_rest(nc, P, gs, sb, ps, psk, rr, affine_evac, evac_eng, L):
    catT, vec, valid, ident, identb, onesb, onesf = (L[k] for k in ("catT", "vec", "valid", "ident", "identb", "onesb", "onesf"))
    G_IN, B_IN, G1, B1, G2, B2, BPG, MISC = (L[k] for k in ("G_IN", "B_IN", "G1", "B1", "G2", "B2", "BPG", "MISC"))
    UT_scr, H0_scr, ssmC, ssmS = (L[k] for k in ("UT_scr", "H0_scr", "ssmC", "ssmS"))
    w_glu, w_o, w_up, w_down, w_ple, w_pg, pc, convp, out_d = (L[k] for k in ("w_glu", "w_o", "w_up", "w_down", "w_ple", "w_pg", "pc", "convp", "out_d"))
    PI = math.pi
    LC = 8
    NCH = S // LC
    OWNCH = 130
    CTX = NCH - OWNCH

    def TT(o, a, b, op, key):
        P.op("dve", lambda e: e.tensor_tensor(out=o, in0=a, in1=b, op=op), reads=[key], writes=[key])

    def TS(o, a, s1, s2, op0, op1, key):
        if s2 is None:
            P.op("dve", lambda e: e.tensor_scalar(out=o, in0=a, scalar1=s1, scalar2=None, op0=op0), reads=[key], writes=[key])
        else:
            P.op("dve", lambda e: e.tensor_scalar(out=o, in0=a, scalar1=s1, scalar2=s2, op0=op0, op1=op1), reads=[key], writes=[key])

    def STT(o, a, sc, b, op0, op1, key, extra_r=()):
        P.op("dve", lambda e: e.scalar_tensor_tensor(out=o, in0=a, scalar=sc, in1=b, op0=op0, op1=op1), reads=[key] + list(extra_r), writes=[key])

    def ACT(o, a, func, key, **kw):
        P.op("act", lambda e: e.activation(out=o, in_=a, func=func, **kw), reads=[key], writes=[key])

    ygT = sb(gs, "ygT", [128, 8, NQ], BF16)

    with ExitStack() as s3:
        GFb = sb(s3, "GFb", [128, LC, 2, 512], BF16)
        HSb = sb(s3, "HSb", [128, LC + 1, 2, 512], BF16)
        BSb = sb(s3, "BSb", [128, 2, 512], BF16)
        DL = sb(s3, "DL", [128, 11, 2, 32], F32)
        LAMS = sb(s3, "LAMS", [128, 2, 32], F32)
        K = "ssm"

        def lam_calc(st, F, ar, ai, ldt, pre):
            t = {n: sb(st, pre + n, [128, F], F32) for n in ("dt", "mag", "ang", "m", "s", "c", "lr", "li", "den", "cr", "ci", "t1")}
            ACT(t["dt"][:], ldt, AF.Exp, K)
            TT(t["mag"][:], t["dt"][:], ar, ALU.mult, K)
            ACT(t["mag"][:], t["mag"][:], AF.Exp, K)
            TT(t["ang"][:], t["dt"][:], ai, ALU.mult, K)
            for j in range(7):
                TS(t["m"][:], t["ang"][:], (2 * j + 1) * PI, -2 * PI, ALU.is_ge, ALU.mult, K)
                if j == 0:
                    TT(t["s"][:], t["ang"][:], t["m"][:], ALU.add, K)
                else:
                    TT(t["s"][:], t["s"][:], t["m"][:], ALU.add, K)
            TS(t["c"][:], t["s"][:], PI / 2, None, ALU.add, None, K)
            TS(t["m"][:], t["c"][:], PI, -2 * PI, ALU.is_ge, ALU.mult, K)
            TT(t["c"][:], t["c"][:], t["m"][:], ALU.add, K)
            ACT(t["s"][:], t["s"][:], AF.Sin, K)
            ACT(t["c"][:], t["c"][:], AF.Sin, K)
            TT(t["lr"][:], t["mag"][:], t["c"][:], ALU.mult, K)
            TT(t["li"][:], t["mag"][:], t["s"][:], ALU.mult, K)
            TT(t["den"][:], ar, ar, ALU.mult, K)
            TT(t["t1"][:], ai, ai, ALU.mult, K)
            TT(t["den"][:], t["den"][:], t["t1"][:], ALU.add, K)
            P.op("dve", lambda e: e.reciprocal(out=t["den"][:], in_=t["den"][:]), reads=[K], writes=[K])
            TS(t["mag"][:], t["lr"][:], -1.0, None, ALU.add, None, K)
            TT(t["cr"][:], t["mag"][:], ar, ALU.mult, K)
            TT(t["t1"][:], t["li"][:], ai, ALU.mult, K)
            TT(t["cr"][:], t["cr"][:], t["t1"][:], ALU.add, K)
            TT(t["cr"][:], t["cr"][:], t["den"][:], ALU.mult, K)
            TT(t["ci"][:], t["li"][:], ar, ALU.mult, K)
            TT(t["t1"][:], t["mag"][:], ai, ALU.mult, K)
            TT(t["ci"][:], t["ci"][:], t["t1"][:], ALU.subtract, K)
            TT(t["ci"][:], t["ci"][:], t["den"][:], ALU.mult, K)
            return t["lr"], t["li"], t["cr"], t["ci"]

        def cmul(orr, oi, ar_, ai_, br_, bi_, t1, t2):
            TT(t1, ar_, br_, ALU.mult, K)
            TT(t2, ai_, bi_, ALU.mult, K)
            TT(t2, t1, t2, ALU.subtract, K)
            TT(t1, ar_, bi_, ALU.mult, K)
            TT(oi, ai_, br_, ALU.mult, K)
            TT(oi, oi, t1, ALU.add, K)
            P.op("dve", lambda e: e.tensor_copy(out=orr, in_=t2), reads=[K], writes=[K])

        with ExitStack() as sc:
            cs = sb(sc, "cs", [128, 5, 512], F32)
            P.dma("ssmc", cs[:].rearrange("p a b -> p (a b)"), ssmC[:, :], writes=[K])
            lr, li, cr, ci = lam_calc(sc, 512, cs[:, 0, :], cs[:, 1, :], cs[:, 4, :], "C_")
            g = [[sb(sc, f"g{i}{j}", [128, 512], F32) for j in range(2)] for i in range(2)]
            t1 = sb(sc, "ct1", [128, 512], F32)
            t2 = sb(sc, "ct2", [128, 512], F32)
            cmul(g[0][0][:], g[0][1][:], cr[:], ci[:], cs[:, 2, :], cs[:, 3, :], t1[:], t2[:])
            for e_ in range(LC):
                cur, nxt = g[e_ % 2], g[(e_ + 1) % 2]
                for part in range(2):
                    P.op("dve", lambda e, e_=e_, part=part, cur=cur: e.tensor_copy(out=GFb[:, e_, part, :], in_=cur[part][:]), reads=[K], writes=[K])
                if e_ < LC - 1:
                    cmul(nxt[0][:], nxt[1][:], lr[:], li[:], cur[0][:], cur[1][:], t1[:], t2[:])
            ss = sb(sc, "ss", [128, 7, 512], F32)
            P.dma("ssms", ss[:].rearrange("p a b -> p (a b)"), ssmS[:, :], writes=[K])
            lrs, lis, crs, cis = lam_calc(sc, 512, ss[:, 0, :], ss[:, 1, :], ss[:, 2, :], "S_")
            cmul(g[0][0][:], g[0][1][:], crs[:], cis[:], ss[:, 5, :], ss[:, 6, :], t1[:], t2[:])
            for part in range(2):
                P.op("dve", lambda e, part=part: e.tensor_copy(out=BSb[:, part, :], in_=g[0][part][:]), reads=[K], writes=[K])
            pw = [[sb(sc, f"pw{i}{j}", [128, 512], F32) for j in range(2)] for i in range(2)]
            P.op("dve", lambda e: e.memset(pw[0][0][:], 1.0), reads=[K], writes=[K])
            P.op("dve", lambda e: e.memset(pw[0][1][:], 0.0), reads=[K], writes=[K])
            for e_ in range(LC + 1):
                cur, nxt = pw[e_ % 2], pw[(e_ + 1) % 2]
                cmul(g[1][0][:], g[1][1][:], ss[:, 3, :], ss[:, 4, :], cur[0][:], cur[1][:], t1[:], t2[:])
                P.op("dve", lambda e, e_=e_: e.tensor_copy(out=HSb[:, e_, 0, :], in_=g[1][0][:]), reads=[K], writes=[K])
                TS(HSb[:, e_, 1, :], g[1][1][:], -1.0, None, ALU.mult, None, K)
                if e_ < LC:
                    cmul(nxt[0][:], nxt[1][:], lrs[:], lis[:], cur[0][:], cur[1][:], t1[:], t2[:])
            lamL = pw[LC % 2]
            v32 = lambda tl: tl[:].rearrange("p (q c) -> p q c", c=16)[:, :, 0]
            for part in range(2):
                P.op("dve", lambda e, part=part: e.tensor_copy(out=LAMS[:, part, :], in_=v32(lamL[part])), reads=[K], writes=[K])
                P.op("dve", lambda e, part=part: e.tensor_copy(out=DL[:, 0, part, :], in_=v32(lamL[part])), reads=[K], writes=[K])
            d1 = sb(sc, "d1", [128, 32], F32)
            d2 = sb(sc, "d2", [128, 32], F32)
            for k in range(10):
                cmul(DL[:, k + 1, 0, :], DL[:, k + 1, 1, :], DL[:, k, 0, :], DL[:, k, 1, :], DL[:, k, 0, :], DL[:, k, 1, :], d1[:], d2[:])
            P.barrier()

        HZ = sb(s3, "HZ", [128, 4, 2, LC + 1, 128], BF16)
        BZ = sb(s3, "BZ", [128, 4, 2, 128], BF16)
        KTb = sb(s3, "KTb", [128, LC, 128], BF16)
        uown = sb(s3, "uown", [128, OWNCH * LC], BF16)
        xpv = sb(s3, "xpv", [128, 4, 2, OWNCH], BF16)
        dm = sb(s3, "dm", [128, 128], F32)
        UZ = [sb(s3, f"UZ{i}", [128, S], BF16) for i in range(2)]
        Wr = [sb(s3, f"Wr{i}", [128, 2, CTX], F32) for i in range(2)]
        junk = [sb(s3, f"junk{i}", [128, CTX], F32) for i in range(2)]
        acc8 = [sb(s3, f"acc8{i}", [128, 8], F32) for i in range(2)]
        tcol2 = [sb(s3, f"tcol2{i}", [128, 130], F32) for i in range(2)]
        xs = [sb(s3, f"xs{i}", [128, 2], F32) for i in range(2)]
        scn = [[sb(s3, f"scn{c}{i}", [128, 2, OWNCH], F32) for i in range(2)] for c in range(2)]
        tcol = [sb(s3, f"tcol{i}", [128, 512], F32) for i in range(2)]
        TC = "tilec"
        HZK = lambda ql, part: [("HZ", ql, gl, part) for gl in range(2)]
        BZK = lambda ql, part: [("BZ", ql, gl, part) for gl in range(2)]
        P.op("pool", lambda e: e.memset(HZ[:].rearrange("p a b c d -> p (a b c d)"), 0.0), writes=[k_ for ql in range(4) for part in range(2) for k_ in HZK(ql, part)])
        P.op("pool", lambda e: e.memset(BZ[:].rearrange("p a b d -> p (a b d)"), 0.0), writes=[k_ for ql in range(4) for part in range(2) for k_ in BZK(ql, part)])

        def pair_chain(t, ql, c):
            pr = 4 * t + ql
            vb = 4 if c == 0 else 0
            Wr_, junk_, acc_, xs_, scn_, tcol_, UZ_ = Wr[c], junk[c], acc8[c], xs[c], scn[c], tcol[c], UZ[c]
            WRr, WRi = ("wr", c, 0), ("wr", c, 1)
            TCk = [("tc", c, i) for i in range(4)]
            tcs = [tcol_[:, i * 130:(i + 1) * 130] for i in range(3)] + [tcol_[:, 390:512]]

            def dve(fn, r, w):
                P.op("dve", fn, reads=r, writes=w)

            def act_mul(o, a_, col, r, w):
                P.op("act", lambda e: e.activation(out=o, in_=a_, func=AF.Identity, scale=col), reads=r, writes=w)

            dve(lambda e: e.memset(Wr_[:, 0, CTX - 1:CTX], 1.0), [], [WRr])
            yield
            dve(lambda e: e.memset(Wr_[:, 1, CTX - 1:CTX], 0.0), [], [WRi])
            yield
            have = 1
            k = 0
            while have < CTX:
                nn = min(have, CTX - have)
                src_lo = CTX - nn
                dst_lo = CTX - have - nn
                pr_c = DL[:, k, 0, pr:pr + 1]
                pi_c = DL[:, k, 1, pr:pr + 1]
                sr = Wr_[:, 0, src_lo:src_lo + nn]
                si = Wr_[:, 1, src_lo:src_lo + nn]
                t0 = junk_[:, 0:nn] if nn > 122 else tcs[0][:, 0:nn]
                t1 = junk_[:, 512:512 + nn] if nn > 122 else tcs[1][:, 0:nn]
                k0 = ("jk", c, 0) if nn > 122 else TCk[0]
                k1 = ("jk", c, 1) if nn > 122 else TCk[1]
                act_mul(t0, si, pi_c, [WRi], [k0])
                yield
                act_mul(t1, si, pr_c, [WRi], [k1])
                yield
                dve(lambda e, dst_lo=dst_lo, nn=nn, sr=sr, pr_c=pr_c, t0=t0: e.scalar_tensor_tensor(
                    out=Wr_[:, 0, dst_lo:dst_lo + nn], in0=sr, scalar=pr_c, in1=t0, op0=ALU.mult, op1=ALU.subtract), [WRr, k0], [WRr])
                yield
                dve(lambda e, dst_lo=dst_lo, nn=nn, sr=sr, pi_c=pi_c, t1=t1: e.scalar_tensor_tensor(
                    out=Wr_[:, 1, dst_lo:dst_lo + nn], in0=sr, scalar=pi_c, in1=t1, op0=ALU.mult, op1=ALU.add), [WRr, k1], [WRi])
                yield
                have += nn
                k += 1
            for gl in range(2):
                gp = 2 * ql + gl
                P.op("pool", lambda e: e.memset(UZ_[:], 0.0), reads=[("UZ", c)], writes=[("UZ", c)])
                P.dma(f"uz{c}", UZ_[gp * 16:(gp + 1) * 16, :], UT_scr[t, gp * 16:(gp + 1) * 16, :], reads=[], writes=[("UZ", c)])
                uzv = UZ_[:].rearrange("p (m l) -> p m l", l=LC)
                for part in range(2):
                    for half in range(2):
                        b = vb + part * 2 + half
                        for e_ in range(LC):
                            P.op("pe", lambda e, part=part, gl=gl, half=half, b=b, e_=e_, uzv=uzv: e.matmul(
                                ps[b][gl * 64:(gl + 1) * 64, :], GFb[:, e_, part, t * 64:(t + 1) * 64],
                                uzv[:, half * 512:(half + 1) * 512, LC - 1 - e_], start=(e_ == 0), stop=(e_ == LC - 1), skip_group_check=True),
                                reads=[("UZ", c)], writes=[psk[b]])
                yield
            vbanks = psk[vb:vb + 4]

            def vsl(part, lo, hi):
                return ps[vb + part * 2 + lo // 512][:, lo % 512:(hi - 1) % 512 + 1]
            segs = [(0, 512, 0), (512, CTX, 512)]
            WRk = [WRr, WRi]
            for idx, (wp, vp) in enumerate([(0, 0), (1, 1), (0, 1), (1, 0)]):
                for si_, (a_, bnd, joff) in enumerate(segs):
                    jk = ("jk", c, si_)
                    dve(lambda e, a_=a_, bnd=bnd, wp=wp, vp=vp, joff=joff: e.tensor_tensor(
                        out=junk_[:, joff:joff + bnd - a_], in0=Wr_[:, wp, a_:bnd], in1=vsl(vp, a_, bnd), op=ALU.mult),
                        [WRk[wp]] + vbanks, [jk])
                    yield
                    dve(lambda e, a_=a_, bnd=bnd, joff=joff, idx=idx, si_=si_: e.tensor_reduce(
                        out=acc_[:, 4 * si_ + idx:4 * si_ + idx + 1], in_=junk_[:, joff:joff + bnd - a_], axis=mybir.AxisListType.X, op=ALU.add),
                        [jk], [("acc", c, si_, idx)])
                    yield
            akeys = [("acc", c, si_, idx) for si_ in range(2) for idx in range(4)]
            XS = ("xs", c)
            dve(lambda e: e.tensor_tensor(out=acc_[:, 0:4], in0=acc_[:, 0:4], in1=acc_[:, 4:8], op=ALU.add), akeys, [("acc", c, 0, 0)])
            yield
            dve(lambda e: e.tensor_tensor(out=xs_[:, 0:1], in0=acc_[:, 0:1], in1=acc_[:, 1:2], op=ALU.subtract), [("acc", c, 0, 0)], [XS])
            yield
            dve(lambda e: e.tensor_tensor(out=xs_[:, 1:2], in0=acc_[:, 2:3], in1=acc_[:, 3:4], op=ALU.add), [("acc", c, 0, 0), XS], [XS])
            yield
            SK = lambda bi, part: ("scn", c, bi, part)
            for part in range(2):
                dve(lambda e, part=part: e.tensor_copy(out=scn_[0][:, part, 0:512 - (CTX - 512)], in_=ps[vb + 1 + part * 2][:, CTX - 512:512]),
                    vbanks, [SK(0, part)])
                yield
            lr_c = LAMS[:, 0, pr:pr + 1]
            li_c = LAMS[:, 1, pr:pr + 1]
            I0, I1 = ("ini", c, 0), ("ini", c, 1)
            ini = tcs[3]
            act_mul(ini[:, 0:1], xs_[:, 1:2], li_c, [XS], [I0])
            yield
            act_mul(ini[:, 2:3], xs_[:, 1:2], lr_c, [XS], [I1])
            yield
            dve(lambda e: e.scalar_tensor_tensor(out=ini[:, 1:2], in0=xs_[:, 0:1], scalar=lr_c, in1=ini[:, 0:1], op0=ALU.mult, op1=ALU.subtract), [XS, I0], [I0])
            yield
            dve(lambda e: e.scalar_tensor_tensor(out=ini[:, 3:4], in0=xs_[:, 0:1], scalar=li_c, in1=ini[:, 2:3], op0=ALU.mult, op1=ALU.add), [XS, I1], [I1])
            yield
            dve(lambda e: e.tensor_tensor(out=scn_[0][:, 0, 0:1], in0=scn_[0][:, 0, 0:1], in1=ini[:, 1:2], op=ALU.add), [I0, SK(0, 0)], [SK(0, 0)])
            yield
            dve(lambda e: e.tensor_tensor(out=scn_[0][:, 1, 0:1], in0=scn_[0][:, 1, 0:1], in1=ini[:, 3:4], op=ALU.add), [I1, SK(0, 1)], [SK(0, 1)])
            yield
            cur = 0
            sh = 1
            k = 0
            while sh < OWNCH:
                A, B = scn_[cur], scn_[1 - cur]
                ai, bi = cur, 1 - cur
                nn = OWNCH - sh
                pr_c = DL[:, k, 0, pr:pr + 1]
                pi_c = DL[:, k, 1, pr:pr + 1]
                for part in range(2):
                    P.op("act", lambda e, part=part, A=A, B=B, sh=sh: e.activation(out=B[:, part, 0:sh], in_=A[:, part, 0:sh], func=AF.Identity),
                         reads=[SK(ai, part)], writes=[SK(bi, part)])
                    yield
                act_mul(tcs[1][:, 0:nn], A[:, 1, 0:nn], pi_c, [SK(ai, 1)], [TCk[1]])
                yield
                act_mul(tcs[2][:, 0:nn], A[:, 0, 0:nn], pi_c, [SK(ai, 0)], [TCk[2]])
                yield
                dve(lambda e, A=A, nn=nn, sh=sh, pr_c=pr_c: e.scalar_tensor_tensor(out=tcs[0][:, 0:nn], in0=A[:, 0, 0:nn], scalar=pr_c, in1=A[:, 0, sh:OWNCH],
                                                                                   op0=ALU.mult, op1=ALU.add), [SK(ai, 0)], [TCk[0]])
                yield
                dve(lambda e, A=A, nn=nn, sh=sh, pr_c=pr_c: e.scalar_tensor_tensor(out=tcol2[c][:, 0:nn], in0=A[:, 1, 0:nn], scalar=pr_c, in1=A[:, 1, sh:OWNCH],
                                                                                   op0=ALU.mult, op1=ALU.add), [SK(ai, 1)], [TCk[3]])
                yield
                dve(lambda e, B=B, nn=nn, sh=sh: e.tensor_tensor(out=B[:, 0, sh:OWNCH], in0=tcs[0][:, 0:nn], in1=tcs[1][:, 0:nn], op=ALU.subtract),
                    [TCk[0], TCk[1]], [SK(bi, 0)])
                yield
                dve(lambda e, B=B, nn=nn, sh=sh: e.tensor_tensor(out=B[:, 1, sh:OWNCH], in0=tcol2[c][:, 0:nn], in1=tcs[2][:, 0:nn], op=ALU.add),
                    [TCk[3], TCk[2]], [SK(bi, 1)])
                yield
                cur = 1 - cur
                sh *= 2
                k += 1
            fin = scn_[cur]
            for part in range(2):
                P.op("act", lambda e, part=part: e.activation(out=xpv[:, ql, part, 0:1], in_=xs_[:, part:part + 1], func=AF.Identity),
                     reads=[XS], writes=[("xpv", ql, part, 0)])
                yield
                P.op("act", lambda e, part=part: e.activation(out=xpv[:, ql, part, 1:OWNCH], in_=fin[:, part, 0:OWNCH - 1], func=AF.Identity),
                     reads=[SK(cur, part)], writes=[("xpv", ql, part, 1)])
                yield

        for t in range(8):
            for ql in range(4):
                pr = 4 * t + ql
                for gl in range(2):
                    r0, r1 = gl * 64, (gl + 1) * 64
                    c0 = 32 * ql + 16 * gl
                    for part in range(2):
                        P.op("dve", lambda e, ql=ql, part=part, r0=r0, r1=r1, c0=c0, pr=pr: e.tensor_copy(
                            out=HZ[r0:r1, ql, part, :, c0:c0 + 16], in_=HSb[r0:r1, :, part, pr * 16:(pr + 1) * 16]), reads=[], writes=[("HZ", ql, gl, part)])
                        P.op("dve", lambda e, ql=ql, part=part, r0=r0, r1=r1, c0=c0, pr=pr: e.tensor_copy(
                            out=BZ[r0:r1, ql, part, c0:c0 + 16], in_=BSb[r0:r1, part, pr * 16:(pr + 1) * 16]), reads=[], writes=[("BZ", ql, gl, part)])
            P.op("dve", lambda e, t=t: e.tensor_scalar(out=dm[:], in0=ident[:], scalar1=vec[:, 128 + t:129 + t], scalar2=None, op0=ALU.mult),
                 reads=["vec", "ident"], writes=["dm"])
            for d in range(LC):
                b = d % 2
                n = 0
                for ql in range(4):
                    for part in range(2):
                        P.op("pe", lambda e, ql=ql, part=part, d=d, b=b, n=n: e.matmul(
                            ps[b][:, 0:128], BZ[:, ql, part, :], HZ[:, ql, part, d, :], start=(n == 0), stop=(n == 7)),
                            reads=HZK(ql, part) + BZK(ql, part), writes=[psk[b]])
                        n += 1
                if d == 0:
                    P.op("dve", lambda e, b=b: e.tensor_tensor(out=KTb[:, 0, :], in0=ps[b][:, 0:128], in1=dm[:], op=ALU.add), reads=["dm", psk[b]], writes=[("KTb", 0)])
                else:
                    P.op("dve", lambda e, b=b, d=d: e.tensor_copy(out=KTb[:, d, :], in_=ps[b][:, 0:128]), reads=[psk[b]], writes=[("KTb", d)])
            P.dma("uown", uown[:], UT_scr[t, :, S - OWNCH * LC:S], writes=["uown"])
            for q0 in (0, 2):
                gens = [pair_chain(t, q0, 0), pair_chain(t, q0 + 1, 1)]
                while gens:
                    for g_ in list(gens):
                        try:
                            next(g_)
                        except StopIteration:
                            gens.remove(g_)
            uov = uown[:].rearrange("p (m l) -> p m l", l=LC)
            xkeys = [("xpv", ql, part, i) for ql in range(4) for part in range(2) for i in range(2)]
            for j in range(LC):
                b = j % 4
                nmm = 8 + j + 1
                n = 0
                for ql in range(4):
                    for part in range(2):
                        P.op("pe", lambda e, ql=ql, part=part, j=j, b=b, n=n, nmm=nmm: e.matmul(
                            ps[b][:, 0:OWNCH], HZ[:, ql, part, j + 1, :], xpv[:, ql, part, :], start=(n == 0), stop=(n == nmm - 1)),
                            reads=HZK(ql, part) + xkeys, writes=[psk[b]])
                        n += 1
                for d in range(j + 1):
                    P.op("pe", lambda e, d=d, j=j, b=b, n=n, nmm=nmm: e.matmul(
                        ps[b][:, 0:OWNCH], KTb[:, d, :], uov[:, :, j - d], start=(n == 0), stop=(n == nmm - 1)),
                        reads=[("KTb", d), "uown"], writes=[psk[b]])
                    n += 1
                P.op("act", lambda e, b=b, t=t, j=j: e.activation(
                    out=ygT[:, t, j:j + (OWNCH - 1) * LC + 1:LC], in_=ps[b][:, 0:OWNCH], func=AF.Gelu_apprx_tanh),
                    reads=[psk[b]], writes=[("ygT", t)])
        P.barrier()
    if STOP_AFTER == 3:
        return
    build_phase4(nc, P, gs, sb, ps, psk, rr, affine_evac, evac_eng, L, ygT)


def build_phase4(nc, P, gs, sb, ps, psk, rr, affine_evac, evac_eng, L, ygT):
    catT, vec, valid, ident, identb, onesb, onesf = (L[k] for k in ("catT", "vec", "valid", "ident", "identb", "onesb", "onesf"))
    G_IN, B_IN, G1, B1, G2, B2, BPG, MISC = (L[k] for k in ("G_IN", "B_IN", "G1", "B1", "G2", "B2", "BPG", "MISC"))
    H0_scr = L["H0_scr"]
    w_glu, w_o, w_up, w_down, w_ple, w_pg, pc, convp, out_d = (L[k] for k in ("w_glu", "w_o", "w_up", "w_down", "w_ple", "w_pg", "pc", "convp", "out_d"))
    OB = QB[1:]

    def nb():
        b = rr["ps"] % 8
        rr["ps"] += 1
        return b

    r1 = sb(gs, "r1", [128, 16, NQ], F32)
    h1b = catT
    mean_t = sb(gs, "mean_t", [128, NQ], F32)
    rstd_t = sb(gs, "rstd_t", [128, NQ], F32)
    sqt = [sb(gs, f"sqt{i}", [128, 512], F32) for i in range(2)]

    def layer_norm_fm(blocks, gcols, bcols, key):
        for (c0, c1) in blocks:
            n = c1 - c0
            bs, bq = nb(), nb()
            for kt in range(16):
                P.op("pe", lambda e, kt=kt, bs=bs, c0=c0, c1=c1, n=n: e.matmul(ps[bs][:, 0:n], onesf[:], r1[:, kt, c0:c1], start=(kt == 0), stop=(kt == 15)),
                     reads=[(key, kt, c0), "onesf"], writes=[psk[bs]])
            for kt in range(16):
                si = kt % 2
                P.op("act", lambda e, kt=kt, si=si, c0=c0, c1=c1, n=n: e.activation(out=sqt[si][:, 0:n], in_=r1[:, kt, c0:c1], func=AF.Square),
                     reads=[(key, kt, c0)], writes=[("sqt", si)])
                P.op("pe", lambda e, kt=kt, si=si, bq=bq, n=n: e.matmul(ps[bq][:, 0:n], onesf[:], sqt[si][:, 0:n], start=(kt == 0), stop=(kt == 15)),
                     reads=[("sqt", si), "onesf"], writes=[psk[bq]])
            P.op("dve", lambda e, bs=bs, c0=c0, c1=c1, n=n: e.tensor_scalar(out=mean_t[:, c0:c1], in0=ps[bs][:, 0:n], scalar1=1.0 / D, scalar2=None, op0=ALU.mult),
                 reads=[psk[bs]], writes=[("mean", c0)])
            P.op("dve", lambda e, c0=c0, c1=c1: e.tensor_tensor(out=rstd_t[:, c0:c1], in0=mean_t[:, c0:c1], in1=mean_t[:, c0:c1], op=ALU.mult),
                 reads=[("mean", c0)], writes=[("rstd", c0)])
            P.op("dve", lambda e, bq=bq, c0=c0, c1=c1, n=n: e.scalar_tensor_tensor(out=rstd_t[:, c0:c1], in0=ps[bq][:, 0:n], scalar=1.0 / D, in1=rstd_t[:, c0:c1],
                                                                                  op0=ALU.mult, op1=ALU.subtract),
                 reads=[psk[bq], ("rstd", c0)], writes=[("rstd", c0)])
            P.op("act", lambda e, c0=c0, c1=c1: e.activation(out=rstd_t[:, c0:c1], in_=rstd_t[:, c0:c1], func=AF.Sqrt, bias=EPS, scale=1.0),
                 reads=[("rstd", c0)], writes=[("rstd", c0)])
            P.op("dve", lambda e, c0=c0, c1=c1: e.reciprocal(out=rstd_t[:, c0:c1], in_=rstd_t[:, c0:c1]), reads=[("rstd", c0)], writes=[("rstd", c0)])
            for kt in range(16):
                P.op("dve", lambda e, kt=kt, c0=c0, c1=c1: e.tensor_tensor(out=r1[:, kt, c0:c1], in0=r1[:, kt, c0:c1], in1=mean_t[:, c0:c1], op=ALU.subtract),
                     reads=[(key, kt, c0), ("mean", c0)], writes=[(key, kt, c0)])
                P.op("dve", lambda e, kt=kt, c0=c0, c1=c1: e.tensor_tensor(out=r1[:, kt, c0:c1], in0=r1[:, kt, c0:c1], in1=rstd_t[:, c0:c1], op=ALU.mult),
                     reads=[(key, kt, c0), ("rstd", c0)], writes=[(key, kt, c0)])
                P.op("dve", lambda e, kt=kt, c0=c0, c1=c1: e.tensor_scalar(out=r1[:, kt, c0:c1], in0=r1[:, kt, c0:c1], scalar1=gcols[:, kt:kt + 1], scalar2=bcols[:, kt:kt + 1],
                                                                          op0=ALU.mult, op1=ALU.add),
                     reads=[(key, kt, c0), "vec"], writes=[(key, kt, c0)])

    def wtile(st, name, n):
        return [sb(st, f"{name}{i}", [128, n, 128], BF16) for i in range(2)]

    with ExitStack() as s4:
      with ExitStack() as s4g:
        Wg = sb(s4g, "Wg", [128, 8, 1024], BF16)
        for kt in range(8):
            P.dma(f"wg{kt % 4}", Wg[:, kt, :], w_glu[kt * 128:(kt + 1) * 128, :], writes=[("Wg", kt)], q="pool")
        sg = [sb(s4g, f"sg{i}", [128, 512], F32) for i in range(2)]
        nsg = 0
        for mt in range(8):
            for (c0, c1) in QB:
                n = c1 - c0
                b = nb()
                for kt in range(8):
                    P.op("pe", lambda e, kt=kt, b=b, mt=mt, c0=c0, c1=c1, n=n: e.matmul(ps[b][:, 0:n], Wg[:, kt, mt * 128:(mt + 1) * 128], ygT[:, kt, c0:c1],
                                                                                         start=(kt == 0), stop=(kt == 7)),
                         reads=[("Wg", kt), ("ygT", kt)], writes=[psk[b]])
                si = nsg % 2
                nsg += 1
                P.op("act", lambda e, b=b, si=si, mt=mt, n=n: e.activation(out=sg[si][:, 0:n], in_=ps[b][:, 0:n], func=AF.Sigmoid, bias=MISC[:, mt:mt + 1], scale=1.0),
                     reads=[psk[b], "vec"], writes=[("sg", si)])
                P.op("dve", lambda e, si=si, mt=mt, c0=c0, c1=c1, n=n: e.tensor_tensor(out=catT[:, 8 + mt, c0:c1], in0=ygT[:, mt, c0:c1], in1=sg[si][:, 0:n], op=ALU.mult),
                     reads=[("sg", si), ("ygT", mt)], writes=[("catT", 8 + mt, c0)])
        P.barrier()
        Wo = [sb(s4g, f"Wo{i}", [128, 16, 512], BF16) for i in range(2)]
        h0m = [sb(s4g, f"h0m{i}", [128, NQ], F32) for i in range(2)]
        wo_v = w_o.rearrange("(kt p) m -> p kt m", p=128)
        for mt in range(16):
            wg_, mi = (mt // 4) % 2, mt % 4
            wi = mt % 2
            if mi == 0:
                for kq in range(4):
                    P.dma(f"wo{wg_}", Wo[wg_][:, kq * 4:(kq + 1) * 4, :], wo_v[:, kq * 4:(kq + 1) * 4, (mt // 4) * 512:(mt // 4 + 1) * 512],
                          writes=[("Wo", wg_)], q="pool")
            P.dma(f"h0m{wi}", h0m[wi][:, HO:NQ], H0_scr[mt, :, HO:NQ], writes=[("h0m", wi)])
            P.op("dve", lambda e, wi=wi, mt=mt: e.tensor_copy(out=h0m[wi][:, 0:HO], in_=L["halo_f"][:, mt, :]), reads=[("h0m", wi)], writes=[("h0m", wi)])
            for (c0, c1) in QB:
                n = c1 - c0
                b = nb()
                for kt in range(16):
                    P.op("pe", lambda e, kt=kt, b=b, wg_=wg_, mi=mi, c0=c0, c1=c1, n=n: e.matmul(ps[b][:, 0:n], Wo[wg_][:, kt, mi * 128:(mi + 1) * 128], catT[:, kt, c0:c1], start=(kt == 0), stop=(kt == 15)),
                         reads=[("Wo", wg_), ("catT", kt, c0)], writes=[psk[b]])
                P.op("dve", lambda e, b=b, wi=wi, mt=mt, c0=c0, c1=c1, n=n: e.scalar_tensor_tensor(out=r1[:, mt, c0:c1], in0=h0m[wi][:, c0:c1], scalar=ALPHA, in1=ps[b][:, 0:n],
                                                                                                  op0=ALU.mult, op1=ALU.add),
                     reads=[psk[b], ("h0m", wi)], writes=[("r1", mt, c0)])
        layer_norm_fm(QB, G1, B1, "r1")
        for kt in range(16):
            P.op("act", lambda e, kt=kt: e.activation(out=h1b[:, kt, :], in_=r1[:, kt, :], func=AF.Identity),
                 reads=[("r1", kt, c0) for (c0, c1) in QB], writes=[("h1b", kt)])
            P.op("dve", lambda e, kt=kt: e.tensor_scalar(out=h1b[:, kt, HO - 2:HO], in0=h1b[:, kt, HO - 2:HO], scalar1=valid[:, 16:17], scalar2=None, op0=ALU.mult),
                 reads=[("h1b", kt), "valid"], writes=[("h1b", kt)])
        P.barrier()

    with ExitStack() as s5:
        pTb = sb(s5, "pTb", [128, 2, OWN], BF16)
        pt = [sb(s5, f"pt{i}", [128, 256], F32) for i in range(2)]
        for tt in range(8):
            pi_ = tt % 2
            P.dma(f"pt{pi_}", pt[pi_][:], pc[tt * 128:(tt + 1) * 128, :], writes=[("pt", pi_)])
            b = nb()
            for k2 in range(2):
                P.op("pe", lambda e, k2=k2, b=b, pi_=pi_: e.transpose(out=ps[b][:, k2 * 128:(k2 + 1) * 128], in_=pt[pi_][:, k2 * 128:(k2 + 1) * 128], identity=ident[:]),
                     reads=[("pt", pi_), "ident"], writes=[psk[b]])
            for k2 in range(2):
                P.op("dve", lambda e, k2=k2, b=b, tt=tt: e.tensor_copy(out=pTb[:, k2, tt * 128:(tt + 1) * 128], in_=ps[b][:, k2 * 128:(k2 + 1) * 128]),
                     reads=[psk[b]], writes=[("pTb", tt)])
        Wpg = [sb(s5, f"Wpg{i}", [128, 16, 512], BF16) for i in range(2)]
        Wpl = [sb(s5, f"Wpl{i}", [128, 2, 512], BF16) for i in range(2)]
        sg = [sb(s5, f"sgp{i}", [128, 512], F32) for i in range(2)]
        wpg_v = w_pg.rearrange("(kt p) m -> p kt m", p=128)
        wpl_v = w_ple.rearrange("(kt p) m -> p kt m", p=128)
        nsg = 0
        for mt in range(16):
            wi, mi = (mt // 4) % 2, mt % 4
            if mi == 0:
                for kq in range(4):
                    P.dma(f"wpg{wi}", Wpg[wi][:, kq * 4:(kq + 1) * 4, :], wpg_v[:, kq * 4:(kq + 1) * 4, (mt // 4) * 512:(mt // 4 + 1) * 512],
                          writes=[("Wpg", wi)], q="pool")
                P.dma(f"wpl{wi}", Wpl[wi][:], wpl_v[:, :, (mt // 4) * 512:(mt // 4 + 1) * 512], writes=[("Wpl", wi)], q="pool")
            for (c0, c1) in OB:
                bg, bp = nb(), nb()
                for kt in range(16):
                    P.op("pe", lambda e, kt=kt, bg=bg, wi=wi, mi=mi, c0=c0, c1=c1: e.matmul(ps[bg][:], Wpg[wi][:, kt, mi * 128:(mi + 1) * 128], h1b[:, kt, c0:c1], start=(kt == 0), stop=(kt == 15)),
                         reads=[("Wpg", wi), ("h1b", kt)], writes=[psk[bg]])
                for k2 in range(2):
                    P.op("pe", lambda e, k2=k2, bp=bp, wi=wi, mi=mi, c0=c0, c1=c1: e.matmul(ps[bp][:], Wpl[wi][:, k2, mi * 128:(mi + 1) * 128], pTb[:, k2, c0 - HO:c1 - HO], start=(k2 == 0), stop=(k2 == 1)),
                         reads=[("Wpl", wi)] + [("pTb", tt) for tt in range(8)], writes=[psk[bp]])
                si = nsg % 2
                nsg += 1
                P.op("act", lambda e, bg=bg, si=si, mt=mt: e.activation(out=sg[si][:], in_=ps[bg][:], func=AF.Sigmoid, bias=BPG[:, mt:mt + 1], scale=1.0),
                     reads=[psk[bg], "vec"], writes=[("sgp", si)])
                P.op("dve", lambda e, bp=bp, si=si: e.tensor_tensor(out=sg[si][:], in0=ps[bp][:], in1=sg[si][:], op=ALU.mult),
                     reads=[psk[bp], ("sgp", si)], writes=[("sgp", si)])
                P.op("dve", lambda e, si=si, mt=mt, c0=c0, c1=c1: e.scalar_tensor_tensor(out=r1[:, mt, c0:c1], in0=r1[:, mt, c0:c1], scalar=ALPHA, in1=sg[si][:],
                                                                                        op0=ALU.mult, op1=ALU.add),
                     reads=[("sgp", si), ("r1", mt, c0)], writes=[("r1", mt, c0)])
        P.barrier()

    with ExitStack() as s6:
        cvp = sb(s6, "cvp", [128, 4, 88], F32)
        P.dma("cvp", cvp[:].rearrange("p a b -> p (a b)"), convp[:, :], writes=["cvp"])
        Wu = [sb(s6, f"Wu{i}", [128, 16, 512], BF16) for i in range(2)]
        Wd = sb(s6, "Wd", [128, 4, D], BF16)
        actb = sb(s6, "actb", [128, 4, OWN], BF16)
        hv = [sb(s6, f"hv{i}", [128, NQ], F32) for i in range(2)]
        cv = [sb(s6, f"cv{i}", [128, OWN], F32) for i in range(2)]
        wu_v = w_up.rearrange("(kt p) m -> p kt m", p=128)
        for jg in range(11):
            for jj in range(4):
                P.dma("wd", Wd[:, jj, :], w_down[(jg * 4 + jj) * 128:(jg * 4 + jj + 1) * 128, :], writes=[("Wd", jj)], q="pool")
            for vg in range(2):
                for kq in range(4):
                    P.dma(f"wu{vg}", Wu[vg][:, kq * 4:(kq + 1) * 4, :], wu_v[:, kq * 4:(kq + 1) * 4, vg * DFF + jg * 512:vg * DFF + (jg + 1) * 512],
                          writes=[("Wu", vg)], q="pool")
            for jj in range(4):
                j = jg * 4 + jj
                for vg in range(2):
                    ft = vg * 44 + j
                    for (c0, c1) in QB:
                        n = c1 - c0
                        b = nb()
                        for kt in range(16):
                            P.op("pe", lambda e, kt=kt, b=b, vg=vg, jj=jj, c0=c0, c1=c1, n=n: e.matmul(ps[b][:, 0:n], Wu[vg][:, kt, jj * 128:(jj + 1) * 128], h1b[:, kt, c0:c1], start=(kt == 0), stop=(kt == 15)),
                                 reads=[("Wu", vg), ("h1b", kt)], writes=[psk[b]])
                        affine_evac(evac_eng(), hv[vg][:, c0:c1], ps[b][:, 0:n], reads=[psk[b]], writes=[("hv", vg)])
                    P.op("dve", lambda e, vg=vg, ft=ft: e.tensor_scalar(out=cv[vg][:], in0=hv[vg][:, HO - 2:HO - 2 + OWN], scalar1=cvp[:, 0, ft:ft + 1], scalar2=cvp[:, 3, ft:ft + 1],
                                                                       op0=ALU.mult, op1=ALU.add),
                         reads=[("hv", vg), "cvp"], writes=[("cv", vg)])
                    for tap in (1, 2):
                        P.op("dve", lambda e, vg=vg, ft=ft, tap=tap: e.scalar_tensor_tensor(out=cv[vg][:], in0=hv[vg][:, HO - 2 + tap:HO - 2 + tap + OWN], scalar=cvp[:, tap, ft:ft + 1], in1=cv[vg][:],
                                                                                             op0=ALU.mult, op1=ALU.add),
                             reads=[("hv", vg), ("cv", vg), "cvp"], writes=[("cv", vg)])
                P.op("act", lambda e: e.activation(out=cv[1][:], in_=cv[1][:], func=AF.Gelu_apprx_tanh), reads=[("cv", 1)], writes=[("cv", 1)])
                P.op("dve", lambda e, jj=jj: e.tensor_tensor(out=actb[:, jj, :], in0=cv[0][:], in1=cv[1][:], op=ALU.mult),
                     reads=[("cv", 0), ("cv", 1)], writes=[("actb", jj)])
            for mt in range(16):
                for (c0, c1) in OB:
                    b = nb()
                    for jj in range(4):
                        P.op("pe", lambda e, jj=jj, b=b, mt=mt, c0=c0, c1=c1: e.matmul(ps[b][:], Wd[:, jj, mt * 128:(mt + 1) * 128], actb[:, jj, c0 - HO:c1 - HO], start=(jj == 0), stop=(jj == 3)),
                             reads=[("Wd", jj), ("actb", jj)], writes=[psk[b]])
                    P.op("dve", lambda e, b=b, mt=mt, c0=c0, c1=c1: e.tensor_tensor(out=r1[:, mt, c0:c1], in0=r1[:, mt, c0:c1], in1=ps[b][:], op=ALU.add),
                         reads=[psk[b], ("r1", mt, c0)], writes=[("r1", mt, c0)])
        P.barrier()

    layer_norm_fm(OB, G2, B2, "r1")
    with ExitStack() as s7:
        ot = [sb(s7, f"ot{i}", [128, D], F32) for i in range(2)]
        for tt in range(8):
            oi = tt % 2
            for k4 in range(4):
                b = nb()
                for j in range(4):
                    kt = k4 * 4 + j
                    P.op("pe", lambda e, kt=kt, j=j, b=b, tt=tt: e.transpose(out=ps[b][:, j * 128:(j + 1) * 128], in_=r1[:, kt, HO + tt * 128:HO + (tt + 1) * 128], identity=ident[:]),
                         reads=[("r1", kt, c0) for (c0, c1) in OB] + ["ident"], writes=[psk[b]])
                affine_evac(evac_eng(), ot[oi][:, k4 * 512:(k4 + 1) * 512], ps[b][:], reads=[psk[b]], writes=[("ot", oi)])
            P.dma(f"ot{oi}", out_d[tt * 128:(tt + 1) * 128, :], ot[oi][:], reads=[("ot", oi)], writes=[("out", tt)])
        P.barrier()


_CACHE = {}


def _cols(v, n):
    return np.ascontiguousarray(np.asarray(v, np.float32).reshape(n, 128).T)


def kernel(x, p, ln_in_g, ln_in_b, w_in, lambda_q1, lambda_k1, lambda_q2, lambda_k2, g_subln, a_re, a_im, log_dt,
           b_re, b_im, c_re, c_im, d_skip, w_glu, b_glu, w_o, ln1_g, ln1_b, w_up, conv_w, conv_b, w_down, w_ple,
           w_pg, b_pg, ln2_g, ln2_b):
    f = lambda a: np.asarray(a, np.float32)
    x = f(x)[0]
    pp = f(p)[0, 0]
    if "nc" not in _CACHE:
        _CACHE["nc"] = build_program()
    nc = _CACHE["nc"]
    vecs = np.zeros((128, 144), np.float32)
    for i, v in enumerate([ln_in_g, ln_in_b, f(ln1_g)[0], f(ln1_b)[0], f(ln2_g)[0], f(ln2_b)[0], f(b_pg)[0]]):
        vecs[:, i * 16:(i + 1) * 16] = _cols(v, 16)
    vecs[:, 112:120] = _cols(f(b_glu)[0], 8)
    vecs[:, 120] = f(g_subln)[0]
    vecs[:, 128:136] = _cols(f(d_skip)[0], 8)
    convp = np.zeros((128, 4, 88), np.float32)
    for t in range(3):
        convp[:, t, :] = _cols(f(conv_w)[0, t], 88)
    convp[:, 3, :] = _cols(f(conv_b)[0], 88)
    lamv = np.stack([np.broadcast_to(f(v)[0], (128, 64)) for v in (lambda_q1, lambda_k1, lambda_q2, lambda_k2)], axis=1)
    ar, ai, ldt = f(a_re)[0], f(a_im)[0], f(log_dt)[0]
    br, bi, cr, ci = f(b_re)[0], f(b_im)[0], f(c_re)[0], f(c_im)[0]
    def clay(a_gn):
        a = a_gn.reshape(8, 8, 1, 64)
        a = np.broadcast_to(a, (8, 8, 16, 64))
        return a.transpose(1, 2, 0, 3).reshape(128, 512)
    def clay_b(b_gnc):
        a = b_gnc.reshape(8, 8, 64, 16).transpose(1, 3, 0, 2)
        return a.reshape(128, 512)
    ssmC = np.stack([clay(ar), clay(ai), clay_b(br), clay_b(bi), clay(np.broadcast_to(ldt[:, None], (64, 64)))], axis=1)
    def slay(a_gn):
        a = a_gn.reshape(32, 2, 64, 1)
        a = np.broadcast_to(a, (32, 2, 64, 16))
        return a.transpose(1, 2, 0, 3).reshape(128, 512)
    def slay_c(c_gcn):
        a = c_gcn.reshape(32, 2, 16, 64).transpose(1, 3, 0, 2)
        return a.reshape(128, 512)
    def slay_b(b_gnc):
        a = b_gnc.reshape(32, 2, 64, 16).transpose(1, 2, 0, 3)
        return a.reshape(128, 512)
    ssmS = np.stack([slay(ar), slay(ai), slay(np.broadcast_to(ldt[:, None], (64, 64))), slay_c(cr), slay_c(ci), slay_b(br), slay_b(bi)], axis=1)
    ident = np.eye(128, dtype=np.float32)
    slopes = 2.0 ** (-np.arange(1, 9, dtype=np.float64))
    kk = np.arange(128)[:, None]
    qq = np.arange(128)[None, :]
    dtile = np.zeros((128, 8, 128), np.float32)
    for h in range(8):
        dmat = 8.0 * (-slopes[h] * np.abs(qq - kk) + slopes[h] * (qq - 127))
        dmat = np.where((kk // 64) <= (qq // 64), dmat, NEG)
        dtile[:, h, :] = dmat
    shared = {
        "w_in": f(w_in)[0], "w_glu": f(w_glu)[0], "w_o": f(w_o)[0], "w_up": f(w_up)[0], "w_down": f(w_down)[0],
        "w_ple": f(w_ple)[0], "w_pg": f(w_pg)[0], "vecs": vecs, "convp": convp.reshape(128, -1),
        "lamv": np.ascontiguousarray(lamv.reshape(128, -1)), "ident": ident, "dtile": dtile.reshape(128, -1),
        "ssmC": np.ascontiguousarray(ssmC.reshape(128, -1)), "ssmS": np.ascontiguousarray(ssmS.reshape(128, -1)),
    }
    in_maps = []
    origins = [7167] + [7168 + 128 * i + 127 for i in range(8)]
    diagt = [55] + [56 + i for i in range(8)]
    for c in range(NCORES):
        start = S - OWN * (c + 1)
        xcx = np.zeros((S, D), np.float32)
        xcx[start:] = x[:OWN * (c + 1)]
        kb = np.zeros((128, 8, 64, 9), np.float32)
        kpos = (np.arange(64)[None, :] * 128 + np.arange(128)[:, None]).astype(np.float64)
        vmask = kpos >= start
        for h in range(8):
            per = GW[h] // 128
            for s_ in range(9):
                if s_ == 0:
                    og = origins[0]
                else:
                    qt = s_ - 1
                    og = origins[1 + (qt // per) * per + per - 1]
                val = slopes[h] * (kpos - og)
                val[:, diagt[s_]] = -slopes[h] * (og - origins[s_])
                val = np.where(vmask, np.minimum(val, 0.0), NEG)
                kb[:, h, :, s_] = val
        vd = np.zeros((128, 17), np.float32)
        for tb in range(16):
            vd[:, tb] = 1.0 if tb * 512 >= start else 0.0
        vd[:, 16] = 1.0 if c > 0 else 0.0
        m = dict(shared)
        m.update({"xc": xcx, "pc": np.ascontiguousarray(pp[c * OWN:(c + 1) * OWN]), "kbias": kb.reshape(128, -1), "valid": vd})
        in_maps.append(m)
    res = run_bass_kernel_spmd(nc, in_maps, core_ids=list(range(NCORES)))
    out = np.concatenate([np.asarray(res.results[c]["out"], np.float32) for c in range(NCORES)], axis=0)
    return out[None].astype(np.float32)
```

```python
import math
from contextlib import ExitStack

import numpy as np
import concourse.bass as bass
import concourse.mybir as mybir
from concourse.bass_utils import run_bass_kernel_spmd

F32 = mybir.dt.float32
BF16 = mybir.dt.bfloat16
AF = mybir.ActivationFunctionType
ALU = mybir.AluOpType

NCORES = 8
S = 8192
D = 2048
OWN = 1024
NQ = 1040
HO = 16
QB = [(14, 16), (16, 528), (528, 1040)]
DFF = 5632
ALPHA = 2.0 ** 0.25
EPS = 1e-5
LAM_INIT = 0.8 - 0.6 * math.exp(0.0)
NEG = -1.0e30
GW = [128, 256, 512, 512, 512, 512, 512, 512]
DEBUG = False
PH1_BLOCKS = 16
NO_STORE = False
NO_HALO = False
STOP_AFTER = None


class Prog:
    ENG = ("pe", "act", "dve", "pool", "sp")

    def __init__(self, nc, stack, same_engine_sync=True):
        self.nc = nc
        self.stack = stack
        self.e = {"pe": nc.tensor, "act": nc.scalar, "dve": nc.vector, "pool": nc.gpsimd, "sp": nc.sync}
        self.sem = {k: stack.enter_context(nc.semaphore("s_" + k)) for k in ("pe", "act", "dve", "pool")}
        self.cnt = {k: 0 for k in self.sem}
        self.dsem = {}
        self.dcnt = {}
        self.seen = {}
        self.lastw = {}
        self.readers = {}
        self.ses = same_engine_sync
        self.nops = 0

    def _semof(self, tok):
        kind, src, val = tok
        return (self.sem[src] if kind == "eng" else self.dsem[src]), val

    def _wait(self, eng, tok):
        kind, src, val = tok
        if kind == "eng" and src == eng and (eng == "pe" or not self.ses):
            return
        key = (eng, kind, src)
        if self.seen.get(key, 0) >= val:
            return
        self.seen[key] = val
        s, v = self._semof(tok)
        self.e[eng].wait_ge(s, v)

    def _deps(self, eng, reads, writes):
        for b in reads:
            t = self.lastw.get(b)
            if t is not None:
                self._wait(eng, t)
        for b in writes:
            t = self.lastw.get(b)
            if t is not None:
                self._wait(eng, t)
            for t in self.readers.get(b, ()):
                self._wait(eng, t)

    def _commit(self, tok, reads, writes):
        for b in writes:
            self.lastw[b] = tok
            self.readers[b] = []
        for b in reads:
            r = self.readers.setdefault(b, [])
            r[:] = [t for t in r if (t[0], t[1]) != (tok[0], tok[1])]
            r.append(tok)

    def op(self, eng, fn, reads=(), writes=()):
        for k in reads:
            if isinstance(k, str) and k[:2] == "ps" and k[2:].isdigit():
                for t in self.readers.get(k, ()):
                    if t[1] != eng:
                        self._wait(eng, t)
        self._deps(eng, reads, writes)
        self.cnt[eng] += 1
        tok = ("eng", eng, self.cnt[eng])
        fn(self.e[eng]).then_inc(self.sem[eng], 1)
        self._commit(tok, reads, writes)
        self.nops += 1
        return tok

    def dma(self, slot, out, in_, reads=(), writes=(), q="sp", **kw):
        if slot not in self.dsem:
            self.dsem[slot] = self.stack.enter_context(self.nc.semaphore("d_" + slot))
            self.dcnt[slot] = 0
        self._deps(q, reads, writes)
        self.dcnt[slot] += 16
        tok = ("dma", slot, self.dcnt[slot])
        self.e[q].dma_start(out=out, in_=in_, **kw).then_inc(self.dsem[slot], 16)
        self._commit(tok, reads, writes)
        self.nops += 1
        return tok

    def barrier(self):
        toks = [("eng", k, self.cnt[k]) for k in self.sem if self.cnt[k] > 0]
        toks += [("dma", s, c) for s, c in self.dcnt.items()]
        for eng in self.ENG:
            for t in toks:
                self._wait(eng, t)
        self.lastw.clear()
        self.readers.clear()

    def finish(self):
        for k in self.sem:
            if self.cnt[k]:
                self._wait("sp", ("eng", k, self.cnt[k]))
        for s, c in self.dcnt.items():
            self._wait("sp", ("dma", s, c))


def build_program():
    nc = bass.Bass("TRN2", target_bir_lowering=False)

    def din(name, shape, dt=F32):
        return nc.dram_tensor(name, list(shape), dt, kind="ExternalInput").ap()

    def dscr(name, shape, dt):
        return nc.dram_tensor(name, list(shape), dt, kind="Internal").ap()

    xc = din("xc", [S, D])
    pc = din("pc", [OWN, 256])
    w_in = din("w_in", [D, 4096])
    w_glu = din("w_glu", [1024, 1024])
    w_o = din("w_o", [D, D])
    w_up = din("w_up", [D, 2 * DFF])
    w_down = din("w_down", [DFF, D])
    w_ple = din("w_ple", [256, D])
    w_pg = din("w_pg", [D, D])
    vecs = din("vecs", [128, 16 * 9])
    convp = din("convp", [128, 4 * 88])
    lamv = din("lamv", [128, 4 * 64])
    ident_d = din("ident", [128, 128])
    kbias_d = din("kbias", [128, 8 * 64 * 9])
    dt_d = din("dtile", [128, 8 * 128])
    valid_d = din("valid", [128, 17])
    ssmC = din("ssmC", [128, 5 * 512])
    ssmS = din("ssmS", [128, 7 * 512])
    out_d = nc.dram_tensor("out", [OWN, D], F32, kind="ExternalOutput").ap()
    if DEBUG:
        dbg_cat = nc.dram_tensor("dbg_cat", [16, 128, NQ], F32, kind="ExternalOutput").ap()

    KT_scr = dscr("KT_scr", [8, 128, S], BF16)
    V_scr = dscr("V_scr", [64, 128, 1024], BF16)
    UT_scr = dscr("UT_scr", [8, 128, S], BF16)
    H0_scr = dscr("H0_scr", [16, 128, NQ], F32)
    XT_scr = dscr("XT_scr", [16, 128, NQ], BF16)

    with ExitStack() as gs:
        P = Prog(nc, gs)

        def sb(st, name, shape, dt):
            return st.enter_context(nc.sbuf_tensor("sb_" + name, list(shape), dt))

        ps = [gs.enter_context(nc.psum_tensor(f"ps{i}", [128, 512], F32)) for i in range(8)]
        psk = [f"ps{i}" for i in range(8)]

        ident = sb(gs, "ident", [128, 128], F32)
        identb = sb(gs, "identb", [128, 128], BF16)
        onesb = sb(gs, "onesb", [128, 128], BF16)
        onesf = sb(gs, "onesf", [128, 128], F32)
        vec = sb(gs, "vec", [128, 144], F32)
        valid = sb(gs, "valid", [128, 17], F32)
        lamcol = sb(gs, "lamcol", [128, 4], F32)
        halo_f = sb(gs, "halo_f", [128, 16, 16], F32)
        halo_b = sb(gs, "halo_b", [128, 16, 16], BF16)
        P.dma("c0", ident[:], ident_d[:, :], writes=["ident"])
        P.dma("c1", vec[:], vecs[:, :], writes=["vec"])
        P.dma("c2", valid[:], valid_d[:, :], writes=["valid"])
        P.op("dve", lambda e: e.tensor_copy(out=identb[:], in_=ident[:]), reads=["ident"], writes=["identb"])
        P.op("dve", lambda e: e.memset(onesb[:], 1.0), writes=["onesb"])
        P.op("dve", lambda e: e.memset(onesf[:], 1.0), writes=["onesf"])
        G_IN, B_IN, G1, B1, G2, B2, BPG, MISC = [vec[:, i * 16:(i + 1) * 16] for i in range(8)]

        with ExitStack() as s0:
            lv = sb(s0, "lv", [128, 4, 64], F32)
            lt = sb(s0, "lt", [128, 64], F32)
            ld = sb(s0, "ld", [128, 2], F32)
            P.dma("c3", lv[:].rearrange("p a b -> p (a b)"), lamv[:, :], writes=["lv"])
            for i in range(2):
                P.op("dve", lambda e, i=i: e.tensor_tensor(out=lt[:], in0=lv[:, 2 * i, :], in1=lv[:, 2 * i + 1, :], op=ALU.mult),
                     reads=["lv"], writes=["lt"])
                P.op("dve", lambda e, i=i: e.tensor_reduce(out=ld[:, i:i + 1], in_=lt[:], axis=mybir.AxisListType.X, op=ALU.add),
                     reads=["lt"], writes=[("ld", i)])
            P.op("act", lambda e: e.activation(out=ld[:], in_=ld[:], func=AF.Exp), reads=[("ld", 0), ("ld", 1)], writes=["ld"])
            P.op("dve", lambda e: e.tensor_tensor(out=lamcol[:, 0:1], in0=ld[:, 1:2], in1=ld[:, 0:1], op=ALU.subtract),
                 reads=["ld"], writes=["lamcol"])
            P.op("dve", lambda e: e.tensor_scalar(out=lamcol[:, 0:1], in0=lamcol[:, 0:1], scalar1=-LAM_INIT, scalar2=None, op0=ALU.add),
                 reads=["lamcol"], writes=["lamcol"])
            P.barrier()

        rr = {"ps": 0, "ev": 0}

        def evac_eng():
            rr["ev"] += 1
            return "dve" if rr["ev"] % 2 else "act"

        def affine_evac(eng, out, in_, scale=None, bias=None, reads=(), writes=()):
            if eng == "act":
                kw = {}
                if scale is not None:
                    kw["scale"] = scale
                if bias is not None:
                    kw["bias"] = bias
                P.op("act", lambda e: e.activation(out=out, in_=in_, func=AF.Identity, **kw), reads=reads, writes=writes)
            else:
                if scale is None and bias is None:
                    P.op("dve", lambda e: e.tensor_copy(out=out, in_=in_), reads=reads, writes=writes)
                elif bias is None:
                    P.op("dve", lambda e: e.tensor_scalar(out=out, in0=in_, scalar1=scale, scalar2=None, op0=ALU.mult), reads=reads, writes=writes)
                elif scale is None:
                    P.op("dve", lambda e: e.tensor_scalar(out=out, in0=in_, scalar1=bias, scalar2=None, op0=ALU.add), reads=reads, writes=writes)
                else:
                    P.op("dve", lambda e: e.tensor_scalar(out=out, in0=in_, scalar1=scale, scalar2=bias, op0=ALU.mult, op1=ALU.add), reads=reads, writes=writes)

        with ExitStack() as s1:
            W = sb(s1, "W1", [128, 16, 3072], BF16)
            for kt in range(16):
                for cb in range(3):
                    P.dma(f"w1_{kt}", W[:, kt, cb * 1024:(cb + 1) * 1024],
                          w_in[kt * 128:(kt + 1) * 128, 1024 + cb * 1024:1024 + (cb + 1) * 1024],
                          writes=[("W1", kt)], q="pool")
            xq = [sb(s1, f"xq{i}", [128, D], F32) for i in range(2)]
            xh = [sb(s1, f"xh{i}", [128, D], F32) for i in range(4)]
            h0T = [sb(s1, "h0T0", [128, 16, 512], BF16)]
            stats = [sb(s1, f"stats{i}", [128, 4, 6], F32) for i in range(2)]
            mv = [sb(s1, f"mv{i}", [128, 2], F32) for i in range(2)]
            rstd = [sb(s1, f"rstd{i}", [128, 1], F32) for i in range(2)]
            nmr = [sb(s1, f"nmr{i}", [128, 1], F32) for i in range(2)]
            stg = [sb(s1, f"stg{i}", [128, 512], BF16) for i in range(6)]
            h0f = [sb(s1, f"h0f{i}", [128, 512], F32) for i in range(2)]
            nstg = 0
            nx = 0
            for tb in range(16 - PH1_BLOCKS, 16):
                hb = 0
                r0 = tb * 512
                for tt in range(4):
                    xi = nx % 2
                    nx += 1
                    P.dma(f"x{xi}", xq[xi][:], xc[r0 + tt * 128:r0 + (tt + 1) * 128, :], writes=[("xq", xi)])
                    for c in range(4):
                        P.op("dve", lambda e, c=c, xi=xi: e.bn_stats(out=stats[xi][:, c, :], in_=xq[xi][:, c * 512:(c + 1) * 512]),
                             reads=[("xq", xi)], writes=[("stats", xi, c)])
                    P.op("dve", lambda e, xi=xi: e.bn_aggr(out=mv[xi][:], in_=stats[xi][:].rearrange("p a b -> p (a b)")),
                         reads=[("stats", xi, c) for c in range(4)], writes=[("mv", xi)])
                    P.op("act", lambda e, xi=xi: e.activation(out=rstd[xi][:], in_=mv[xi][:, 1:2], func=AF.Sqrt, bias=EPS, scale=1.0),
                         reads=[("mv", xi)], writes=[("rstd", xi)])
                    P.op("dve", lambda e, xi=xi: e.reciprocal(out=rstd[xi][:], in_=rstd[xi][:]), reads=[("rstd", xi)], writes=[("rstd", xi)])
                    P.op("dve", lambda e, xi=xi: e.scalar_tensor_tensor(out=nmr[xi][:], in0=mv[xi][:, 0:1], scalar=-1.0, in1=rstd[xi][:],
                                                                        op0=ALU.mult, op1=ALU.mult),
                         reads=[("mv", xi), ("rstd", xi)], writes=[("nmr", xi)])
                    P.op("act", lambda e, xi=xi, tt=tt: e.activation(out=xh[tt][:], in_=xq[xi][:], func=AF.Identity, bias=nmr[xi][:], scale=rstd[xi][:]),
                         reads=[("xq", xi), ("nmr", xi), ("rstd", xi)], writes=[("xh", tt)])
                for kt in range(16):
                    b = kt % 2
                    for tt in range(4):
                        P.op("pe", lambda e, kt=kt, tt=tt, b=b: e.transpose(out=ps[b][:, tt * 128:(tt + 1) * 128],
                                                                          in_=xh[tt][:, kt * 128:(kt + 1) * 128], identity=ident[:]),
                             reads=[("xh", tt), "ident"], writes=[psk[b]])
                    affine_evac(evac_eng(), h0T[hb][:, kt, :], ps[b][:], scale=G_IN[:, kt:kt + 1], bias=B_IN[:, kt:kt + 1],
                                reads=[psk[b], "vec"], writes=[("h0T", hb, kt)])
                    if tb == 13:
                        affine_evac("dve", halo_f[:, kt, :], ps[b][:, 496:512], scale=G_IN[:, kt:kt + 1], bias=B_IN[:, kt:kt + 1],
                                    reads=[psk[b], "vec"], writes=[("halo_f", kt)])
                        P.op("dve", lambda e, kt=kt: e.tensor_copy(out=halo_b[:, kt, :], in_=h0T[hb][:, kt, 496:512]),
                             reads=[("h0T", hb, kt)], writes=[("halo_b", kt)])
                    if tb >= 14:
                        fi = kt % 2
                        affine_evac("dve", h0f[fi][:], ps[b][:], scale=G_IN[:, kt:kt + 1], bias=B_IN[:, kt:kt + 1],
                                    reads=[psk[b], "vec"], writes=[("h0f", fi)])
                        dc = HO + (tb - 14) * 512
                        P.dma(f"h0s{fi}", H0_scr[kt, :, dc:dc + 512], h0f[fi][:], reads=[("h0f", fi)], writes=[("H0", kt, tb)])
                        P.dma(f"xts{fi}", XT_scr[kt, :, dc:dc + 512], h0T[hb][:, kt, :], reads=[("h0T", hb, kt)], writes=[("XT", kt, tb)])
                hreads = [("h0T", hb, kt) for kt in range(16)]

                def proj_fm(col0, dst, dkey, scale=None):
                    nonlocal nstg
                    b = 2 + rr["ps"] % 6
                    rr["ps"] += 1
                    for kt in range(16):
                        P.op("pe", lambda e, kt=kt, b=b: e.matmul(ps[b][:], W[:, kt, col0:col0 + 128], h0T[hb][:, kt, :],
                                                                  start=(kt == 0), stop=(kt == 15)),
                             reads=[("W1", kt), ("h0T", hb, kt)], writes=[psk[b]])
                    si = nstg % 6
                    nstg += 1
                    affine_evac(evac_eng(), stg[si][:], ps[b][:], scale=scale, reads=[psk[b], "valid"], writes=[("stg", si)])
                    if not NO_STORE:
                        P.dma(f"stg{si}", dst, stg[si][:], reads=[("stg", si)], writes=[dkey])

                for h in range(8):
                    proj_fm(h * 128, KT_scr[h, :, r0:r0 + 512], ("KT", h, tb))
                for t in range(8):
                    proj_fm(2048 + t * 128, UT_scr[t, :, r0:r0 + 512], ("UT", t, tb), scale=valid[:, tb:tb + 1])
                for tt in range(4):
                    for fh in range(2):
                        b = 2 + rr["ps"] % 6
                        rr["ps"] += 1
                        for kt in range(16):
                            P.op("pe", lambda e, kt=kt, b=b, tt=tt, fh=fh: e.matmul(
                                ps[b][:], h0T[hb][:, kt, tt * 128:(tt + 1) * 128], W[:, kt, 1024 + fh * 512:1024 + (fh + 1) * 512],
                                start=(kt == 0), stop=(kt == 15)),
                                reads=[("W1", kt), ("h0T", hb, kt)], writes=[psk[b]])
                        si = nstg % 6
                        nstg += 1
                        affine_evac(evac_eng(), stg[si][:], ps[b][:], reads=[psk[b]], writes=[("stg", si)])
                        if not NO_STORE:
                            P.dma(f"stg{si}", V_scr[tb * 4 + tt, :, fh * 512:(fh + 1) * 512], stg[si][:], reads=[("stg", si)],
                                  writes=[("V", tb * 4 + tt, fh)])
            P.barrier()
        if STOP_AFTER == 1:
            P.finish()
            return nc

        catT = sb(gs, "catT", [128, 16, NQ], BF16)
        with ExitStack() as s2:
            qT = sb(s2, "qT", [128, 8, NQ], BF16)
            dtile = sb(s2, "dtile", [128, 8, 128], BF16)
            P.dma("dt", dtile[:].rearrange("p a b -> p (a b)"), dt_d[:, :], writes=["dtile"], q="pool")
            with ExitStack() as s2a:
                xT = sb(s2a, "xT", [128, 16, NQ], BF16)
                Wq = sb(s2a, "Wq", [128, 16, 1024], BF16)
                for kt in range(16):
                    P.dma(f"xt{kt % 4}", xT[:, kt, HO:NQ], XT_scr[kt, :, HO:NQ], writes=[("xT", kt)])
                    P.op("dve", lambda e, kt=kt: e.tensor_copy(out=xT[:, kt, 0:HO], in_=halo_b[:, kt, :]), reads=[("xT", kt)], writes=[("xT", kt)])
                    P.dma(f"wq{kt % 4}", Wq[:, kt, :], w_in[kt * 128:(kt + 1) * 128, 0:1024], writes=[("Wq", kt)], q="pool")
                for h in range(8):
                    for (c0, c1) in QB:
                        b = rr["ps"] % 8
                        rr["ps"] += 1
                        for kt in range(16):
                            P.op("pe", lambda e, kt=kt, b=b, h=h, c0=c0, c1=c1: e.matmul(
                                ps[b][:, 0:c1 - c0], Wq[:, kt, h * 128:(h + 1) * 128], xT[:, kt, c0:c1], start=(kt == 0), stop=(kt == 15)),
                                reads=[("Wq", kt), ("xT", kt)], writes=[psk[b]])
                        affine_evac(evac_eng(), qT[:, h, c0:c1], ps[b][:, 0:c1 - c0], reads=[psk[b]], writes=[("qT", h, c0)])
                P.barrier()

            kT = [sb(s2, f"kT{i}", [128, S], BF16) for i in range(2)]
            vh = [sb(s2, f"vh{i}", [128, 64, 128], BF16) for i in range(2)]
            kb = [sb(s2, f"kb{i}", [128, 64, 9], F32) for i in range(2)]
            pT = [[sb(s2, f"pT{m}_{i}", [128, 512], BF16) for i in range(3)] for m in range(2)]
            rl = [sb(s2, f"rl{m}", [128, 512], F32) for m in range(2)]
            oo = sb(s2, "oo", [128, 512], F32)
            o2 = sb(s2, "o2", [128, 512], F32)
            sq = sb(s2, "sq", [128, 512], F32)
            gs8 = sb(s2, "gs8", [128, 1], F32)
            P.op("dve", lambda e: e.tensor_scalar(out=gs8[:], in0=MISC[:, 8:9], scalar1=1.0 - LAM_INIT, scalar2=None, op0=ALU.mult),
                 reads=["vec"], writes=["gs8"])
            npt = 0
            for h in range(8):
                hb = h % 2
                for j in range(4):
                    P.dma(f"kt{hb}", kT[hb][:, j * 2048:(j + 1) * 2048], KT_scr[h, :, j * 2048:(j + 1) * 2048], writes=[("kT", hb)])
                for j in range(4):
                    P.dma(f"vh{hb}", vh[hb][:, j * 16:(j + 1) * 16, :],
                          V_scr[j * 16:(j + 1) * 16, :, h * 128:(h + 1) * 128].rearrange("k p d -> p k d"), writes=[("vh", hb)])
                P.dma(f"kb{hb}", kb[hb][:].rearrange("p a b -> p (a b)"), kbias_d[:, h * 576:(h + 1) * 576], writes=[("kb", hb)])
                for qi, (c0, c1) in enumerate(QB):
                    n = c1 - c0
                    if qi == 0:
                        subs = [(0, 2, 0, 55)]
                        klast = 55
                    else:
                        subs = [(i * 128, 128, 1 + (qi - 1) * 4 + i, 56 + (qi - 1) * 4 + i) for i in range(4)]
                        klast = 56 + (qi - 1) * 4 + 3
                    acc = [4, 5, 6, 7]
                    per = GW[h] // 128
                    steps = []
                    for kt in range(klast + 1):
                        act_subs = [s_ for s_ in subs if s_[3] >= kt]
                        groups = []
                        if qi == 0:
                            groups = [(sl, sw, sid) for (sl, sw, sid, dk) in act_subs]
                        else:
                            for g0 in range(0, 4, per):
                                grp = [s_ for s_ in subs[g0:g0 + per] if s_[3] >= kt]
                                if len(grp) == per and all(s_[3] > kt for s_ in grp):
                                    groups.append((grp[0][0], sum(x[1] for x in grp), grp[-1][2]))
                                else:
                                    groups += [(sl, sw, sid) for (sl, sw, sid, dk) in grp]
                        steps.append((kt, act_subs, groups))

                    def emit_S(i):
                        kt, act_subs, groups = steps[i]
                        lo = act_subs[0][0]
                        hi = act_subs[-1][0] + act_subs[-1][1]
                        pi = i % 3
                        for m in range(2):
                            b = (i % 2) * 2 + m
                            P.op("pe", lambda e, m=m, b=b, kt=kt, lo=lo, hi=hi: e.matmul(
                                ps[b][:, lo:hi], kT[hb][m * 64:(m + 1) * 64, kt * 128:(kt + 1) * 128],
                                qT[m * 64:(m + 1) * 64, h, c0 + lo:c0 + hi], start=True, stop=True),
                                reads=[("kT", hb)] + [("qT", h, c0)], writes=[psk[b]])
                            for (sl, sw, sid, dk) in act_subs:
                                if dk == kt:
                                    dsl = dtile[:, h, 128 - sw:128] if sw < 128 else dtile[:, h, :]
                                    P.op("pe", lambda e, b=b, sl=sl, sw=sw, dsl=dsl: e.matmul(
                                        ps[b][:, sl:sl + sw], identb[:], dsl, start=False, stop=True, skip_group_check=True),
                                        reads=["identb", "dtile"], writes=[psk[b]])
                            for (sl, sw, sid) in groups:
                                P.op("act", lambda e, m=m, b=b, sl=sl, sw=sw, sid=sid, kt=kt, pi=pi: e.activation(
                                    out=pT[m][pi][:, sl:sl + sw], in_=ps[b][:, sl:sl + sw], func=AF.Exp,
                                    bias=kb[hb][:, kt, sid:sid + 1], scale=0.125),
                                    reads=[psk[b], ("kb", hb)], writes=[("pT", m, pi, sl)])

                    def emit_PV(i):
                        kt, act_subs, groups = steps[i]
                        lo = act_subs[0][0]
                        hi = act_subs[-1][0] + act_subs[-1][1]
                        pi = i % 3
                        first = (i == 0)
                        for m in range(2):
                            P.op("pe", lambda e, m=m, kt=kt, lo=lo, hi=hi, pi=pi, first=first: e.matmul(
                                ps[acc[m]][:, lo:hi], vh[hb][:, kt, :], pT[m][pi][:, lo:hi], start=first, stop=(kt == klast),
                                skip_group_check=True),
                                reads=[("vh", hb)] + [("pT", m, pi, g_[0]) for g_ in groups], writes=[psk[acc[m]]])
                            P.op("pe", lambda e, m=m, kt=kt, lo=lo, hi=hi, pi=pi, first=first: e.matmul(
                                ps[acc[2 + m]][:, lo:hi], onesb[:], pT[m][pi][:, lo:hi], start=first, stop=(kt == klast),
                                skip_group_check=True),
                                reads=["onesb"] + [("pT", m, pi, g_[0]) for g_ in groups], writes=[psk[acc[2 + m]]])

                    emit_S(0)
                    for i in range(len(steps)):
                        if i + 1 < len(steps):
                            emit_S(i + 1)
                        emit_PV(i)
                    P.op("act", lambda e: e.activation(out=oo[:, 0:n], in_=ps[acc[0]][:, 0:n], func=AF.Identity), reads=[psk[acc[0]]], writes=["oo"])
                    P.op("act", lambda e: e.activation(out=o2[:, 0:n], in_=ps[acc[1]][:, 0:n], func=AF.Identity), reads=[psk[acc[1]]], writes=["o2"])
                    for m in range(2):
                        P.op("dve", lambda e, m=m: e.tensor_scalar(out=rl[m][:, 0:n], in0=ps[acc[2 + m]][:, 0:n], scalar1=1e-37, scalar2=None, op0=ALU.add),
                             reads=[psk[acc[2 + m]]], writes=[("rl", m)])
                    for m in range(2):
                        P.op("dve", lambda e, m=m: e.reciprocal(out=rl[m][:, 0:n], in_=rl[m][:, 0:n]), reads=[("rl", m)], writes=[("rl", m)])
                    P.op("dve", lambda e: e.tensor_tensor(out=oo[:, 0:n], in0=oo[:, 0:n], in1=rl[0][:, 0:n], op=ALU.mult),
                         reads=["oo", ("rl", 0)], writes=["oo"])
                    P.op("dve", lambda e: e.tensor_tensor(out=o2[:, 0:n], in0=o2[:, 0:n], in1=rl[1][:, 0:n], op=ALU.mult),
                         reads=["o2", ("rl", 1)], writes=["o2"])
                    P.op("dve", lambda e: e.scalar_tensor_tensor(out=oo[:, 0:n], in0=o2[:, 0:n], scalar=lamcol[:, 0:1], in1=oo[:, 0:n],
                                                                 op0=ALU.mult, op1=ALU.add),
                         reads=["o2", "oo", "lamcol"], writes=["oo"])
                    P.op("act", lambda e: e.activation(out=sq[:, 0:n], in_=oo[:, 0:n], func=AF.Square), reads=["oo"], writes=["sq"])
                    P.op("pe", lambda e: e.matmul(ps[0][:, 0:n], onesf[:], sq[:, 0:n], start=True, stop=True),
                         reads=["onesf", "sq"], writes=[psk[0]])
                    P.op("act", lambda e: e.activation(out=sq[:, 0:n], in_=ps[0][:, 0:n], func=AF.Sqrt, bias=EPS, scale=1.0 / 128.0),
                         reads=[psk[0]], writes=["sq"])
                    P.op("dve", lambda e: e.reciprocal(out=sq[:, 0:n], in_=sq[:, 0:n]), reads=["sq"], writes=["sq"])
                    P.op("dve", lambda e: e.tensor_tensor(out=oo[:, 0:n], in0=oo[:, 0:n], in1=sq[:, 0:n], op=ALU.mult),
                         reads=["oo", "sq"], writes=["oo"])
                    P.op("dve", lambda e, h=h, c0=c0, c1=c1: e.tensor_scalar(out=catT[:, h, c0:c1], in0=oo[:, 0:n], scalar1=gs8[:], scalar2=None, op0=ALU.mult),
                         reads=["oo", "gs8"], writes=[("catT", h, c0)])
            P.barrier()
        if STOP_AFTER == 2:
            P.finish()
            return nc

        build_rest(nc, P, gs, sb, ps, psk, rr, affine_evac, evac_eng, locals())
        P.finish()
    return nc


def build_rest(nc, P, gs, sb, ps, psk, rr, affine_evac, evac_eng, L):
    catT, vec, valid, ident, identb, onesb, onesf = (L[k] for k in ("catT", "vec", "valid", "ident", "identb", "onesb", "onesf"))
    G_IN, B_IN, G1, B1, G2, B2, BPG, MISC = (L[k] for k in ("G_IN", "B_IN", "G1", "B1", "G2", "B2", "BPG", "MISC"))
    UT_scr, H0_scr, ssmC, ssmS = (L[k] for k in ("UT_scr", "H0_scr", "ssmC", "ssmS"))
    w_glu, w_o, w_up, w_down, w_ple, w_pg, pc, convp, out_d = (L[k] for k in ("w_glu", "w_o", "w_up", "w_down", "w_ple", "w_pg", "pc", "convp", "out_d"))
    PI = math.pi
    LC = 8
    NCH = S // LC
    OWNCH = 130
    CTX = NCH - OWNCH

    def TT(o, a, b, op, key):
        P.op("dve", lambda e: e.tensor_tensor(out=o, in0=a, in1=b, op=op), reads=[key], writes=[key])

    def TS(o, a, s1, s2, op0, op1, key):
        if s2 is None:
            P.op("dve", lambda e: e.tensor_scalar(out=o, in0=a, scalar1=s1, scalar2=None, op0=op0), reads=[key], writes=[key])
        else:
            P.op("dve", lambda e: e.tensor_scalar(out=o, in0=a, scalar1=s1, scalar2=s2, op0=op0, op1=op1), reads=[key], writes=[key])

    def STT(o, a, sc, b, op0, op1, key, extra_r=()):
        P.op("dve", lambda e: e.scalar_tensor_tensor(out=o, in0=a, scalar=sc, in1=b, op0=op0, op1=op1), reads=[key] + list(extra_r), writes=[key])

    def ACT(o, a, func, key, **kw):
        P.op("act", lambda e: e.activation(out=o, in_=a, func=func, **kw), reads=[key], writes=[key])

    ygT = sb(gs, "ygT", [128, 8, NQ], BF16)

    with ExitStack() as s3:
        GFb = sb(s3, "GFb", [128, LC, 2, 512], BF16)
        HSb = sb(s3, "HSb", [128, LC + 1, 2, 512], BF16)
        BSb = sb(s3, "BSb", [128, 2, 512], BF16)
        DL = sb(s3, "DL", [128, 11, 2, 32], F32)
        LAMS = sb(s3, "LAMS", [128, 2, 32], F32)
        K = "ssm"

        def lam_calc(st, F, ar, ai, ldt, pre):
            t = {n: sb(st, pre + n, [128, F], F32) for n in ("dt", "mag", "ang", "m", "s", "c", "lr", "li", "den", "cr", "ci", "t1")}
            ACT(t["dt"][:], ldt, AF.Exp, K)
            TT(t["mag"][:], t["dt"][:], ar, ALU.mult, K)
            ACT(t["mag"][:], t["mag"][:], AF.Exp, K)
            TT(t["ang"][:], t["dt"][:], ai, ALU.mult, K)
            for j in range(7):
                TS(t["m"][:], t["ang"][:], (2 * j + 1) * PI, -2 * PI, ALU.is_ge, ALU.mult, K)
                if j == 0:
                    TT(t["s"][:], t["ang"][:], t["m"][:], ALU.add, K)
                else:
                    TT(t["s"][:], t["s"][:], t["m"][:], ALU.add, K)
            TS(t["c"][:], t["s"][:], PI / 2, None, ALU.add, None, K)
            TS(t["m"][:], t["c"][:], PI, -2 * PI, ALU.is_ge, ALU.mult, K)
            TT(t["c"][:], t["c"][:], t["m"][:], ALU.add, K)
            ACT(t["s"][:], t["s"][:], AF.Sin, K)
            ACT(t["c"][:], t["c"][:], AF.Sin, K)
            TT(t["lr"][:], t["mag"][:], t["c"][:], ALU.mult, K)
            TT(t["li"][:], t["mag"][:], t["s"][:], ALU.mult, K)
            TT(t["den"][:], ar, ar, ALU.mult, K)
            TT(t["t1"][:], ai, ai, ALU.mult, K)
            TT(t["den"][:], t["den"][:], t["t1"][:], ALU.add, K)
            P.op("dve", lambda e: e.reciprocal(out=t["den"][:], in_=t["den"][:]), reads=[K], writes=[K])
            TS(t["mag"][:], t["lr"][:], -1.0, None, ALU.add, None, K)
            TT(t["cr"][:], t["mag"][:], ar, ALU.mult, K)
            TT(t["t1"][:], t["li"][:], ai, ALU.mult, K)
            TT(t["cr"][:], t["cr"][:], t["t1"][:], ALU.add, K)
            TT(t["cr"][:], t["cr"][:], t["den"][:], ALU.mult, K)
            TT(t["ci"][:], t["li"][:], ar, ALU.mult, K)
            TT(t["t1"][:], t["mag"][:], ai, ALU.mult, K)
            TT(t["ci"][:], t["ci"][:], t["t1"][:], ALU.subtract, K)
            TT(t["ci"][:], t["ci"][:], t["den"][:], ALU.mult, K)
            return t["lr"], t["li"], t["cr"], t["ci"]

        def cmul(orr, oi, ar_, ai_, br_, bi_, t1, t2):
            TT(t1, ar_, br_, ALU.mult, K)
            TT(t2, ai_, bi_, ALU.mult, K)
            TT(t2, t1, t2, ALU.subtract, K)
            TT(t1, ar_, bi_, ALU.mult, K)
            TT(oi, ai_, br_, ALU.mult, K)
            TT(oi, oi, t1, ALU.add, K)
            P.op("dve", lambda e: e.tensor_copy(out=orr, in_=t2), reads=[K], writes=[K])

        with ExitStack() as sc:
            cs = sb(sc, "cs", [128, 5, 512], F32)
            P.dma("ssmc", cs[:].rearrange("p a b -> p (a b)"), ssmC[:, :], writes=[K])
            lr, li, cr, ci = lam_calc(sc, 512, cs[:, 0, :], cs[:, 1, :], cs[:, 4, :], "C_")
            g = [[sb(sc, f"g{i}{j}", [128, 512], F32) for j in range(2)] for i in range(2)]
            t1 = sb(sc, "ct1", [128, 512], F32)
            t2 = sb(sc, "ct2", [128, 512], F32)
            cmul(g[0][0][:], g[0][1][:], cr[:], ci[:], cs[:, 2, :], cs[:, 3, :], t1[:], t2[:])
            for e_ in range(LC):
                cur, nxt = g[e_ % 2], g[(e_ + 1) % 2]
                for part in range(2):
                    P.op("dve", lambda e, e_=e_, part=part, cur=cur: e.tensor_copy(out=GFb[:, e_, part, :], in_=cur[part][:]), reads=[K], writes=[K])
                if e_ < LC - 1:
                    cmul(nxt[0][:], nxt[1][:], lr[:], li[:], cur[0][:], cur[1][:], t1[:], t2[:])
            ss = sb(sc, "ss", [128, 7, 512], F32)
            P.dma("ssms", ss[:].rearrange("p a b -> p (a b)"), ssmS[:, :], writes=[K])
            lrs, lis, crs, cis = lam_calc(sc, 512, ss[:, 0, :], ss[:, 1, :], ss[:, 2, :], "S_")
            cmul(g[0][0][:], g[0][1][:], crs[:], cis[:], ss[:, 5, :], ss[:, 6, :], t1[:], t2[:])
            for part in range(2):
                P.op("dve", lambda e, part=part: e.tensor_copy(out=BSb[:, part, :], in_=g[0][part][:]), reads=[K], writes=[K])
            pw = [[sb(sc, f"pw{i}{j}", [128, 512], F32) for j in range(2)] for i in range(2)]
            P.op("dve", lambda e: e.memset(pw[0][0][:], 1.0), reads=[K], writes=[K])
            P.op("dve", lambda e: e.memset(pw[0][1][:], 0.0), reads=[K], writes=[K])
            for e_ in range(LC + 1):
                cur, nxt = pw[e_ % 2], pw[(e_ + 1) % 2]
                cmul(g[1][0][:], g[1][1][:], ss[:, 3, :], ss[:, 4, :], cur[0][:], cur[1][:], t1[:], t2[:])
                P.op("dve", lambda e, e_=e_: e.tensor_copy(out=HSb[:, e_, 0, :], in_=g[1][0][:]), reads=[K], writes=[K])
                TS(HSb[:, e_, 1, :], g[1][1][:], -1.0, None, ALU.mult, None, K)
                if e_ < LC:
                    cmul(nxt[0][:], nxt[1][:], lrs[:], lis[:], cur[0][:], cur[1][:], t1[:], t2[:])
            lamL = pw[LC % 2]
            v32 = lambda tl: tl[:].rearrange("p (q c) -> p q c", c=16)[:, :, 0]
            for part in range(2):
                P.op("dve", lambda e, part=part: e.tensor_copy(out=LAMS[:, part, :], in_=v32(lamL[part])), reads=[K], writes=[K])
                P.op("dve", lambda e, part=part: e.tensor_copy(out=DL[:, 0, part, :], in_=v32(lamL[part])), reads=[K], writes=[K])
            d1 = sb(sc, "d1", [128, 32], F32)
            d2 = sb(sc, "d2", [128, 32], F32)
            for k in range(10):
                cmul(DL[:, k + 1, 0, :], DL[:, k + 1, 1, :], DL[:, k, 0, :], DL[:, k, 1, :], DL[:, k, 0, :], DL[:, k, 1, :], d1[:], d2[:])
            P.barrier()

        HZ = sb(s3, "HZ", [128, 4, 2, LC + 1, 128], BF16)
        BZ = sb(s3, "BZ", [128, 4, 2, 128], BF16)
        KTb = sb(s3, "KTb", [128, LC, 128], BF16)
        uown = sb(s3, "uown", [128, OWNCH * LC], BF16)
        xpv = sb(s3, "xpv", [128, 4, 2, OWNCH], BF16)
        dm = sb(s3, "dm", [128, 128], F32)
        UZ = [sb(s3, f"UZ{i}", [128, S], BF16) for i in range(2)]
        Wr = [sb(s3, f"Wr{i}", [128, 2, CTX], F32) for i in range(2)]
        junk = [sb(s3, f"junk{i}", [128, CTX], F32) for i in range(2)]
        acc8 = [sb(s3, f"acc8{i}", [128, 8], F32) for i in range(2)]
        tcol2 = [sb(s3, f"tcol2{i}", [128, 130], F32) for i in range(2)]
        xs = [sb(s3, f"xs{i}", [128, 2], F32) for i in range(2)]
        scn = [[sb(s3, f"scn{c}{i}", [128, 2, OWNCH], F32) for i in range(2)] for c in range(2)]
        tcol = [sb(s3, f"tcol{i}", [128, 512], F32) for i in range(2)]
        TC = "tilec"
        HZK = lambda ql, part: [("HZ", ql, gl, part) for gl in range(2)]
        BZK = lambda ql, part: [("BZ", ql, gl, part) for gl in range(2)]
        P.op("pool", lambda e: e.memset(HZ[:].rearrange("p a b c d -> p (a b c d)"), 0.0), writes=[k_ for ql in range(4) for part in range(2) for k_ in HZK(ql, part)])
        P.op("pool", lambda e: e.memset(BZ[:].rearrange("p a b d -> p (a b d)"), 0.0), writes=[k_ for ql in range(4) for part in range(2) for k_ in BZK(ql, part)])

        def pair_chain(t, ql, c):
            pr = 4 * t + ql
            vb = 4 if c == 0 else 0
            Wr_, junk_, acc_, xs_, scn_, tcol_, UZ_ = Wr[c], junk[c], acc8[c], xs[c], scn[c], tcol[c], UZ[c]
            WRr, WRi = ("wr", c, 0), ("wr", c, 1)
            TCk = [("tc", c, i) for i in range(4)]
            tcs = [tcol_[:, i * 130:(i + 1) * 130] for i in range(3)] + [tcol_[:, 390:512]]

            def dve(fn, r, w):
                P.op("dve", fn, reads=r, writes=w)

            def act_mul(o, a_, col, r, w):
                P.op("act", lambda e: e.activation(out=o, in_=a_, func=AF.Identity, scale=col), reads=r, writes=w)

            dve(lambda e: e.memset(Wr_[:, 0, CTX - 1:CTX], 1.0), [], [WRr])
            yield
            dve(lambda e: e.memset(Wr_[:, 1, CTX - 1:CTX], 0.0), [], [WRi])
            yield
            have = 1
            k = 0
            while have < CTX:
                nn = min(have, CTX - have)
                src_lo = CTX - nn
                dst_lo = CTX - have - nn
                pr_c = DL[:, k, 0, pr:pr + 1]
                pi_c = DL[:, k, 1, pr:pr + 1]
                sr = Wr_[:, 0, src_lo:src_lo + nn]
                si = Wr_[:, 1, src_lo:src_lo + nn]
                t0 = junk_[:, 0:nn] if nn > 122 else tcs[0][:, 0:nn]
                t1 = junk_[:, 512:512 + nn] if nn > 122 else tcs[1][:, 0:nn]
                k0 = ("jk", c, 0) if nn > 122 else TCk[0]
                k1 = ("jk", c, 1) if nn > 122 else TCk[1]
                act_mul(t0, si, pi_c, [WRi], [k0])
                yield
                act_mul(t1, si, pr_c, [WRi], [k1])
                yield
                dve(lambda e, dst_lo=dst_lo, nn=nn, sr=sr, pr_c=pr_c, t0=t0: e.scalar_tensor_tensor(
                    out=Wr_[:, 0, dst_lo:dst_lo + nn], in0=sr, scalar=pr_c, in1=t0, op0=ALU.mult, op1=ALU.subtract), [WRr, k0], [WRr])
                yield
                dve(lambda e, dst_lo=dst_lo, nn=nn, sr=sr, pi_c=pi_c, t1=t1: e.scalar_tensor_tensor(
                    out=Wr_[:, 1, dst_lo:dst_lo + nn], in0=sr, scalar=pi_c, in1=t1, op0=ALU.mult, op1=ALU.add), [WRr, k1], [WRi])
                yield
                have += nn
                k += 1
            for gl in range(2):
                gp = 2 * ql + gl
                P.op("pool", lambda e: e.memset(UZ_[:], 0.0), reads=[("UZ", c)], writes=[("UZ", c)])
                P.dma(f"uz{c}", UZ_[gp * 16:(gp + 1) * 16, :], UT_scr[t, gp * 16:(gp + 1) * 16, :], reads=[], writes=[("UZ", c)])
                uzv = UZ_[:].rearrange("p (m l) -> p m l", l=LC)
                for part in range(2):
                    for half in range(2):
                        b = vb + part * 2 + half
                        for e_ in range(LC):
                            P.op("pe", lambda e, part=part, gl=gl, half=half, b=b, e_=e_, uzv=uzv: e.matmul(
                                ps[b][gl * 64:(gl + 1) * 64, :], GFb[:, e_, part, t * 64:(t + 1) * 64],
                                uzv[:, half * 512:(half + 1) * 512, LC - 1 - e_], start=(e_ == 0), stop=(e_ == LC - 1), skip_group_check=True),
                                reads=[("UZ", c)], writes=[psk[b]])
                yield
            vbanks = psk[vb:vb + 4]

            def vsl(part, lo, hi):
                return ps[vb + part * 2 + lo // 512][:, lo % 512:(hi - 1) % 512 + 1]
            segs = [(0, 512, 0), (512, CTX, 512)]
            WRk = [WRr, WRi]
            for idx, (wp, vp) in enumerate([(0, 0), (1, 1), (0, 1), (1, 0)]):
                for si_, (a_, bnd, joff) in enumerate(segs):
                    jk = ("jk", c, si_)
                    dve(lambda e, a_=a_, bnd=bnd, wp=wp, vp=vp, joff=joff: e.tensor_tensor(
                        out=junk_[:, joff:joff + bnd - a_], in0=Wr_[:, wp, a_:bnd], in1=vsl(vp, a_, bnd), op=ALU.mult),
                        [WRk[wp]] + vbanks, [jk])
                    yield
                    dve(lambda e, a_=a_, bnd=bnd, joff=joff, idx=idx, si_=si_: e.tensor_reduce(
                        out=acc_[:, 4 * si_ + idx:4 * si_ + idx + 1], in_=junk_[:, joff:joff + bnd - a_], axis=mybir.AxisListType.X, op=ALU.add),
                        [jk], [("acc", c, si_, idx)])
                    yield
            akeys = [("acc", c, si_, idx) for si_ in range(2) for idx in range(4)]
            XS = ("xs", c)
            dve(lambda e: e.tensor_tensor(out=acc_[:, 0:4], in0=acc_[:, 0:4], in1=acc_[:, 4:8], op=ALU.add), akeys, [("acc", c, 0, 0)])
            yield
            dve(lambda e: e.tensor_tensor(out=xs_[:, 0:1], in0=acc_[:, 0:1], in1=acc_[:, 1:2], op=ALU.subtract), [("acc", c, 0, 0)], [XS])
            yield
            dve(lambda e: e.tensor_tensor(out=xs_[:, 1:2], in0=acc_[:, 2:3], in1=acc_[:, 3:4], op=ALU.add), [("acc", c, 0, 0), XS], [XS])
            yield
            SK = lambda bi, part: ("scn", c, bi, part)
            for part in range(2):
                dve(lambda e, part=part: e.tensor_copy(out=scn_[0][:, part, 0:512 - (CTX - 512)], in_=ps[vb + 1 + part * 2][:, CTX - 512:512]),
                    vbanks, [SK(0, part)])
                yield
            lr_c = LAMS[:, 0, pr:pr + 1]
            li_c = LAMS[:, 1, pr:pr + 1]
            I0, I1 = ("ini", c, 0), ("ini", c, 1)
            ini = tcs[3]
            act_mul(ini[:, 0:1], xs_[:, 1:2], li_c, [XS], [I0])
            yield
            act_mul(ini[:, 2:3], xs_[:, 1:2], lr_c, [XS], [I1])
            yield
            dve(lambda e: e.scalar_tensor_tensor(out=ini[:, 1:2], in0=xs_[:, 0:1], scalar=lr_c, in1=ini[:, 0:1], op0=ALU.mult, op1=ALU.subtract), [XS, I0], [I0])
            yield
            dve(lambda e: e.scalar_tensor_tensor(out=ini[:, 3:4], in0=xs_[:, 0:1], scalar=li_c, in1=ini[:, 2:3], op0=ALU.mult, op1=ALU.add), [XS, I1], [I1])
            yield
            dve(lambda e: e.tensor_tensor(out=scn_[0][:, 0, 0:1], in0=scn_[0][:, 0, 0:1], in1=ini[:, 1:2], op=ALU.add), [I0, SK(0, 0)], [SK(0, 0)])
            yield
            dve(lambda e: e.tensor_tensor(out=scn_[0][:, 1, 0:1], in0=scn_[0][:, 1, 0:1], in1=ini[:, 3:4], op=ALU.add), [I1, SK(0, 1)], [SK(0, 1)])
            yield
            cur = 0
            sh = 1
            k = 0
            while sh < OWNCH:
                A, B = scn_[cur], scn_[1 - cur]
                ai, bi = cur, 1 - cur
                nn = OWNCH - sh
                pr_c = DL[:, k, 0, pr:pr + 1]
                pi_c = DL[:, k, 1, pr:pr + 1]
                for part in range(2):
                    P.op("act", lambda e, part=part, A=A, B=B, sh=sh: e.activation(out=B[:, part, 0:sh], in_=A[:, part, 0:sh], func=AF.Identity),
                         reads=[SK(ai, part)], writes=[SK(bi, part)])
                    yield
                act_mul(tcs[1][:, 0:nn], A[:, 1, 0:nn], pi_c, [SK(ai, 1)], [TCk[1]])
                yield
                act_mul(tcs[2][:, 0:nn], A[:, 0, 0:nn], pi_c, [SK(ai, 0)], [TCk[2]])
                yield
                dve(lambda e, A=A, nn=nn, sh=sh, pr_c=pr_c: e.scalar_tensor_tensor(out=tcs[0][:, 0:nn], in0=A[:, 0, 0:nn], scalar=pr_c, in1=A[:, 0, sh:OWNCH],
                                                                                   op0=ALU.mult, op1=ALU.add), [SK(ai, 0)], [TCk[0]])
                yield
                dve(lambda e, A=A, nn=nn, sh=sh, pr_c=pr_c: e.scalar_tensor_tensor(out=tcol2[c][:, 0:nn], in0=A[:, 1, 0:nn], scalar=pr_c, in1=A[:, 1, sh:OWNCH],
                                                                                   op0=ALU.mult, op1=ALU.add), [SK(ai, 1)], [TCk[3]])
                yield
                dve(lambda e, B=B, nn=nn, sh=sh: e.tensor_tensor(out=B[:, 0, sh:OWNCH], in0=tcs[0][:, 0:nn], in1=tcs[1][:, 0:nn], op=ALU.subtract),
                    [TCk[0], TCk[1]], [SK(bi, 0)])
                yield
                dve(lambda e, B=B, nn=nn, sh=sh: e.tensor_tensor(out=B[:, 1, sh:OWNCH], in0=tcol2[c][:, 0:nn], in1=tcs[2][:, 0:nn], op=ALU.add),
                    [TCk[3], TCk[2]], [SK(bi, 1)])
                yield
                cur = 1 - cur
                sh *= 2
                k += 1
            fin = scn_[cur]
            for part in range(2):
                P.op("act", lambda e, part=part: e.activation(out=xpv[:, ql, part, 0:1], in_=xs_[:, part:part + 1], func=AF.Identity),
                     reads=[XS], writes=[("xpv", ql, part, 0)])
                yield
                P.op("act", lambda e, part=part: e.activation(out=xpv[:, ql, part, 1:OWNCH], in_=fin[:, part, 0:OWNCH - 1], func=AF.Identity),
                     reads=[SK(cur, part)], writes=[("xpv", ql, part, 1)])
                yield

        for t in range(8):
            for ql in range(4):
                pr = 4 * t + ql
                for gl in range(2):
                    r0, r1 = gl * 64, (gl + 1) * 64
                    c0 = 32 * ql + 16 * gl
                    for part in range(2):
                        P.op("dve", lambda e, ql=ql, part=part, r0=r0, r1=r1, c0=c0, pr=pr: e.tensor_copy(
                            out=HZ[r0:r1, ql, part, :, c0:c0 + 16], in_=HSb[r0:r1, :, part, pr * 16:(pr + 1) * 16]), reads=[], writes=[("HZ", ql, gl, part)])
                        P.op("dve", lambda e, ql=ql, part=part, r0=r0, r1=r1, c0=c0, pr=pr: e.tensor_copy(
                            out=BZ[r0:r1, ql, part, c0:c0 + 16], in_=BSb[r0:r1, part, pr * 16:(pr + 1) * 16]), reads=[], writes=[("BZ", ql, gl, part)])
            P.op("dve", lambda e, t=t: e.tensor_scalar(out=dm[:], in0=ident[:], scalar1=vec[:, 128 + t:129 + t], scalar2=None, op0=ALU.mult),
                 reads=["vec", "ident"], writes=["dm"])
            for d in range(LC):
                b = d % 2
                n = 0
                for ql in range(4):
                    for part in range(2):
                        P.op("pe", lambda e, ql=ql, part=part, d=d, b=b, n=n: e.matmul(
                            ps[b][:, 0:128], BZ[:, ql, part, :], HZ[:, ql, part, d, :], start=(n == 0), stop=(n == 7)),
                            reads=HZK(ql, part) + BZK(ql, part), writes=[psk[b]])
                        n += 1
                if d == 0:
                    P.op("dve", lambda e, b=b: e.tensor_tensor(out=KTb[:, 0, :], in0=ps[b][:, 0:128], in1=dm[:], op=ALU.add), reads=["dm", psk[b]], writes=[("KTb", 0)])
                else:
                    P.op("dve", lambda e, b=b, d=d: e.tensor_copy(out=KTb[:, d, :], in_=ps[b][:, 0:128]), reads=[psk[b]], writes=[("KTb", d)])
            P.dma("uown", uown[:], UT_scr[t, :, S - OWNCH * LC:S], writes=["uown"])
            for q0 in (0, 2):
                gens = [pair_chain(t, q0, 0), pair_chain(t, q0 + 1, 1)]
                while gens:
                    for g_ in list(gens):
                        try:
                            next(g_)
                        except StopIteration:
                            gens.remove(g_)
            uov = uown[:].rearrange("p (m l) -> p m l", l=LC)
            xkeys = [("xpv", ql, part, i) for ql in range(4) for part in range(2) for i in range(2)]
            for j in range(LC):
                b = j % 4
                nmm = 8 + j + 1
                n = 0
                for ql in range(4):
                    for part in range(2):
                        P.op("pe", lambda e, ql=ql, part=part, j=j, b=b, n=n, nmm=nmm: e.matmul(
                            ps[b][:, 0:OWNCH], HZ[:, ql, part, j + 1, :], xpv[:, ql, part, :], start=(n == 0), stop=(n == nmm - 1)),
                            reads=HZK(ql, part) + xkeys, writes=[psk[b]])
                        n += 1
                for d in range(j + 1):
                    P.op("pe", lambda e, d=d, j=j, b=b, n=n, nmm=nmm: e.matmul(
                        ps[b][:, 0:OWNCH], KTb[:, d, :], uov[:, :, j - d], start=(n == 0), stop=(n == nmm - 1)),
                        reads=[("KTb", d), "uown"], writes=[psk[b]])
                    n += 1
                P.op("act", lambda e, b=b, t=t, j=j: e.activation(
                    out=ygT[:, t, j:j + (OWNCH - 1) * LC + 1:LC], in_=ps[b][:, 0:OWNCH], func=AF.Gelu_apprx_tanh),
                    reads=[psk[b]], writes=[("ygT", t)])
        P.barrier()
    if STOP_AFTER == 3:
        return
    build_phase4(nc, P, gs, sb, ps, psk, rr, affine_evac, evac_eng, L, ygT)


def build_phase4(nc, P, gs, sb, ps, psk, rr, affine_evac, evac_eng, L, ygT):
    catT, vec, valid, ident, identb, onesb, onesf = (L[k] for k in ("catT", "vec", "valid", "ident", "identb", "onesb", "onesf"))
    G_IN, B_IN, G1, B1, G2, B2, BPG, MISC = (L[k] for k in ("G_IN", "B_IN", "G1", "B1", "G2", "B2", "BPG", "MISC"))
    H0_scr = L["H0_scr"]
    w_glu, w_o, w_up, w_down, w_ple, w_pg, pc, convp, out_d = (L[k] for k in ("w_glu", "w_o", "w_up", "w_down", "w_ple", "w_pg", "pc", "convp", "out_d"))
    OB = QB[1:]

    def nb():
        b = rr["ps"] % 8
        rr["ps"] += 1
        return b

    r1 = sb(gs, "r1", [128, 16, NQ], F32)
    h1b = catT
    mean_t = sb(gs, "mean_t", [128, NQ], F32)
    rstd_t = sb(gs, "rstd_t", [128, NQ], F32)
    sqt = [sb(gs, f"sqt{i}", [128, 512], F32) for i in range(2)]

    def layer_norm_fm(blocks, gcols, bcols, key):
        for (c0, c1) in blocks:
            n = c1 - c0
            bs, bq = nb(), nb()
            for kt in range(16):
                P.op("pe", lambda e, kt=kt, bs=bs, c0=c0, c1=c1, n=n: e.matmul(ps[bs][:, 0:n], onesf[:], r1[:, kt, c0:c1], start=(kt == 0), stop=(kt == 15)),
                     reads=[(key, kt, c0), "onesf"], writes=[psk[bs]])
            for kt in range(16):
                si = kt % 2
                P.op("act", lambda e, kt=kt, si=si, c0=c0, c1=c1, n=n: e.activation(out=sqt[si][:, 0:n], in_=r1[:, kt, c0:c1], func=AF.Square),
                     reads=[(key, kt, c0)], writes=[("sqt", si)])
                P.op("pe", lambda e, kt=kt, si=si, bq=bq, n=n: e.matmul(ps[bq][:, 0:n], onesf[:], sqt[si][:, 0:n], start=(kt == 0), stop=(kt == 15)),
                     reads=[("sqt", si), "onesf"], writes=[psk[bq]])
            P.op("dve", lambda e, bs=bs, c0=c0, c1=c1, n=n: e.tensor_scalar(out=mean_t[:, c0:c1], in0=ps[bs][:, 0:n], scalar1=1.0 / D, scalar2=None, op0=ALU.mult),
                 reads=[psk[bs]], writes=[("mean", c0)])
            P.op("dve", lambda e, c0=c0, c1=c1: e.tensor_tensor(out=rstd_t[:, c0:c1], in0=mean_t[:, c0:c1], in1=mean_t[:, c0:c1], op=ALU.mult),
                 reads=[("mean", c0)], writes=[("rstd", c0)])
            P.op("dve", lambda e, bq=bq, c0=c0, c1=c1, n=n: e.scalar_tensor_tensor(out=rstd_t[:, c0:c1], in0=ps[bq][:, 0:n], scalar=1.0 / D, in1=rstd_t[:, c0:c1],
                                                                                  op0=ALU.mult, op1=ALU.subtract),
                 reads=[psk[bq], ("rstd", c0)], writes=[("rstd", c0)])
            P.op("act", lambda e, c0=c0, c1=c1: e.activation(out=rstd_t[:, c0:c1], in_=rstd_t[:, c0:c1], func=AF.Sqrt, bias=EPS, scale=1.0),
                 reads=[("rstd", c0)], writes=[("rstd", c0)])
            P.op("dve", lambda e, c0=c0, c1=c1: e.reciprocal(out=rstd_t[:, c0:c1], in_=rstd_t[:, c0:c1]), reads=[("rstd", c0)], writes=[("rstd", c0)])
            for kt in range(16):
                P.op("dve", lambda e, kt=kt, c0=c0, c1=c1: e.tensor_tensor(out=r1[:, kt, c0:c1], in0=r1[:, kt, c0:c1], in1=mean_t[:, c0:c1], op=ALU.subtract),
                     reads=[(key, kt, c0), ("mean", c0)], writes=[(key, kt, c0)])
                P.op("dve", lambda e, kt=kt, c0=c0, c1=c1: e.tensor_tensor(out=r1[:, kt, c0:c1], in0=r1[:, kt, c0:c1], in1=rstd_t[:, c0:c1], op=ALU.mult),
                     reads=[(key, kt, c0), ("rstd", c0)], writes=[(key, kt, c0)])
                P.op("dve", lambda e, kt=kt, c0=c0, c1=c1: e.tensor_scalar(out=r1[:, kt, c0:c1], in0=r1[:, kt, c0:c1], scalar1=gcols[:, kt:kt + 1], scalar2=bcols[:, kt:kt + 1],
                                                                          op0=ALU.mult, op1=ALU.add),
                     reads=[(key, kt, c0), "vec"], writes=[(key, kt, c0)])

    def wtile(st, name, n):
        return [sb(st, f"{name}{i}", [128, n, 128], BF16) for i in range(2)]

    with ExitStack() as s4:
      with ExitStack() as s4g:
        Wg = sb(s4g, "Wg", [128, 8, 1024], BF16)
        for kt in range(8):
            P.dma(f"wg{kt % 4}", Wg[:, kt, :], w_glu[kt * 128:(kt + 1) * 128, :], writes=[("Wg", kt)], q="pool")
        sg = [sb(s4g, f"sg{i}", [128, 512], F32) for i in range(2)]
        nsg = 0
        for mt in range(8):
            for (c0, c1) in QB:
                n = c1 - c0
                b = nb()
                for kt in range(8):
                    P.op("pe", lambda e, kt=kt, b=b, mt=mt, c0=c0, c1=c1, n=n: e.matmul(ps[b][:, 0:n], Wg[:, kt, mt * 128:(mt + 1) * 128], ygT[:, kt, c0:c1],
                                                                                         start=(kt == 0), stop=(kt == 7)),
                         reads=[("Wg", kt), ("ygT", kt)], writes=[psk[b]])
                si = nsg % 2
                nsg += 1
                P.op("act", lambda e, b=b, si=si, mt=mt, n=n: e.activation(out=sg[si][:, 0:n], in_=ps[b][:, 0:n], func=AF.Sigmoid, bias=MISC[:, mt:mt + 1], scale=1.0),
                     reads=[psk[b], "vec"], writes=[("sg", si)])
                P.op("dve", lambda e, si=si, mt=mt, c0=c0, c1=c1, n=n: e.tensor_tensor(out=catT[:, 8 + mt, c0:c1], in0=ygT[:, mt, c0:c1], in1=sg[si][:, 0:n], op=ALU.mult),
                     reads=[("sg", si), ("ygT", mt)], writes=[("catT", 8 + mt, c0)])
        P.barrier()
        Wo = [sb(s4g, f"Wo{i}", [128, 16, 512], BF16) for i in range(2)]
        h0m = [sb(s4g, f"h0m{i}", [128, NQ], F32) for i in range(2)]
        wo_v = w_o.rearrange("(kt p) m -> p kt m", p=128)
        for mt in range(16):
            wg_, mi = (mt // 4) % 2, mt % 4
            wi = mt % 2
            if mi == 0:
                for kq in range(4):
                    P.dma(f"wo{wg_}", Wo[wg_][:, kq * 4:(kq + 1) * 4, :], wo_v[:, kq * 4:(kq + 1) * 4, (mt // 4) * 512:(mt // 4 + 1) * 512],
                          writes=[("Wo", wg_)], q="pool")
            P.dma(f"h0m{wi}", h0m[wi][:, HO:NQ], H0_scr[mt, :, HO:NQ], writes=[("h0m", wi)])
            P.op("dve", lambda e, wi=wi, mt=mt: e.tensor_copy(out=h0m[wi][:, 0:HO], in_=L["halo_f"][:, mt, :]), reads=[("h0m", wi)], writes=[("h0m", wi)])
            for (c0, c1) in QB:
                n = c1 - c0
                b = nb()
                for kt in range(16):
                    P.op("pe", lambda e, kt=kt, b=b, wg_=wg_, mi=mi, c0=c0, c1=c1, n=n: e.matmul(ps[b][:, 0:n], Wo[wg_][:, kt, mi * 128:(mi + 1) * 128], catT[:, kt, c0:c1], start=(kt == 0), stop=(kt == 15)),
                         reads=[("Wo", wg_), ("catT", kt, c0)], writes=[psk[b]])
                P.op("dve", lambda e, b=b, wi=wi, mt=mt, c0=c0, c1=c1, n=n: e.scalar_tensor_tensor(out=r1[:, mt, c0:c1], in0=h0m[wi][:, c0:c1], scalar=ALPHA, in1=ps[b][:, 0:n],
                                                                                                  op0=ALU.mult, op1=ALU.add),
                     reads=[psk[b], ("h0m", wi)], writes=[("r1", mt, c0)])
        layer_norm_fm(QB, G1, B1, "r1")
        for kt in range(16):
            P.op("act", lambda e, kt=kt: e.activation(out=h1b[:, kt, :], in_=r1[:, kt, :], func=AF.Identity),
                 reads=[("r1", kt, c0) for (c0, c1) in QB], writes=[("h1b", kt)])
            P.op("dve", lambda e, kt=kt: e.tensor_scalar(out=h1b[:, kt, HO - 2:HO], in0=h1b[:, kt, HO - 2:HO], scalar1=valid[:, 16:17], scalar2=None, op0=ALU.mult),
                 reads=[("h1b", kt), "valid"], writes=[("h1b", kt)])
        P.barrier()

    with ExitStack() as s5:
        pTb = sb(s5, "pTb", [128, 2, OWN], BF16)
        pt = [sb(s5, f"pt{i}", [128, 256], F32) for i in range(2)]
        for tt in range(8):
            pi_ = tt % 2
            P.dma(f"pt{pi_}", pt[pi_][:], pc[tt * 128:(tt + 1) * 128, :], writes=[("pt", pi_)])
            b = nb()
            for k2 in range(2):
                P.op("pe", lambda e, k2=k2, b=b, pi_=pi_: e.transpose(out=ps[b][:, k2 * 128:(k2 + 1) * 128], in_=pt[pi_][:, k2 * 128:(k2 + 1) * 128], identity=ident[:]),
                     reads=[("pt", pi_), "ident"], writes=[psk[b]])
            for k2 in range(2):
                P.op("dve", lambda e, k2=k2, b=b, tt=tt: e.tensor_copy(out=pTb[:, k2, tt * 128:(tt + 1) * 128], in_=ps[b][:, k2 * 128:(k2 + 1) * 128]),
                     reads=[psk[b]], writes=[("pTb", tt)])
        Wpg = [sb(s5, f"Wpg{i}", [128, 16, 512], BF16) for i in range(2)]
        Wpl = [sb(s5, f"Wpl{i}", [128, 2, 512], BF16) for i in range(2)]
        sg = [sb(s5, f"sgp{i}", [128, 512], F32) for i in range(2)]
        wpg_v = w_pg.rearrange("(kt p) m -> p kt m", p=128)
        wpl_v = w_ple.rearrange("(kt p) m -> p kt m", p=128)
        nsg = 0
        for mt in range(16):
            wi, mi = (mt // 4) % 2, mt % 4
            if mi == 0:
                for kq in range(4):
                    P.dma(f"wpg{wi}", Wpg[wi][:, kq * 4:(kq + 1) * 4, :], wpg_v[:, kq * 4:(kq + 1) * 4, (mt // 4) * 512:(mt // 4 + 1) * 512],
                          writes=[("Wpg", wi)], q="pool")
                P.dma(f"wpl{wi}", Wpl[wi][:], wpl_v[:, :, (mt // 4) * 512:(mt // 4 + 1) * 512], writes=[("Wpl", wi)], q="pool")
            for (c0, c1) in OB:
                bg, bp = nb(), nb()
                for kt in range(16):
                    P.op("pe", lambda e, kt=kt, bg=bg, wi=wi, mi=mi, c0=c0, c1=c1: e.matmul(ps[bg][:], Wpg[wi][:, kt, mi * 128:(mi + 1) * 128], h1b[:, kt, c0:c1], start=(kt == 0), stop=(kt == 15)),
                         reads=[("Wpg", wi), ("h1b", kt)], writes=[psk[bg]])
                for k2 in range(2):
                    P.op("pe", lambda e, k2=k2, bp=bp, wi=wi, mi=mi, c0=c0, c1=c1: e.matmul(ps[bp][:], Wpl[wi][:, k2, mi * 128:(mi + 1) * 128], pTb[:, k2, c0 - HO:c1 - HO], start=(k2 == 0), stop=(k2 == 1)),
                         reads=[("Wpl", wi)] + [("pTb", tt) for tt in range(8)], writes=[psk[bp]])
                si = nsg % 2
                nsg += 1
                P.op("act", lambda e, bg=bg, si=si, mt=mt: e.activation(out=sg[si][:], in_=ps[bg][:], func=AF.Sigmoid, bias=BPG[:, mt:mt + 1], scale=1.0),
                     reads=[psk[bg], "vec"], writes=[("sgp", si)])
                P.op("dve", lambda e, bp=bp, si=si: e.tensor_tensor(out=sg[si][:], in0=ps[bp][:], in1=sg[si][:], op=ALU.mult),
                     reads=[psk[bp], ("sgp", si)], writes=[("sgp", si)])
                P.op("dve", lambda e, si=si, mt=mt, c0=c0, c1=c1: e.scalar_tensor_tensor(out=r1[:, mt, c0:c1], in0=r1[:, mt, c0:c1], scalar=ALPHA, in1=sg[si][:],
                                                                                        op0=ALU.mult, op1=ALU.add),
                     reads=[("sgp", si), ("r1", mt, c0)], writes=[("r1", mt, c0)])
        P.barrier()

    with ExitStack() as s6:
        cvp = sb(s6, "cvp", [128, 4, 88], F32)
        P.dma("cvp", cvp[:].rearrange("p a b -> p (a b)"), convp[:, :], writes=["cvp"])
        Wu = [sb(s6, f"Wu{i}", [128, 16, 512], BF16) for i in range(2)]
        Wd = sb(s6, "Wd", [128, 4, D], BF16)
        actb = sb(s6, "actb", [128, 4, OWN], BF16)
        hv = [sb(s6, f"hv{i}", [128, NQ], F32) for i in range(2)]
        cv = [sb(s6, f"cv{i}", [128, OWN], F32) for i in range(2)]
        wu_v = w_up.rearrange("(kt p) m -> p kt m", p=128)
        for jg in range(11):
            for jj in range(4):
                P.dma("wd", Wd[:, jj, :], w_down[(jg * 4 + jj) * 128:(jg * 4 + jj + 1) * 128, :], writes=[("Wd", jj)], q="pool")
            for vg in range(2):
                for kq in range(4):
                    P.dma(f"wu{vg}", Wu[vg][:, kq * 4:(kq + 1) * 4, :], wu_v[:, kq * 4:(kq + 1) * 4, vg * DFF + jg * 512:vg * DFF + (jg + 1) * 512],
                          writes=[("Wu", vg)], q="pool")
            for jj in range(4):
                j = jg * 4 + jj
                for vg in range(2):
                    ft = vg * 44 + j
                    for (c0, c1) in QB:
                        n = c1 - c0
                        b = nb()
                        for kt in range(16):
                            P.op("pe", lambda e, kt=kt, b=b, vg=vg, jj=jj, c0=c0, c1=c1, n=n: e.matmul(ps[b][:, 0:n], Wu[vg][:, kt, jj * 128:(jj + 1) * 128], h1b[:, kt, c0:c1], start=(kt == 0), stop=(kt == 15)),
                                 reads=[("Wu", vg), ("h1b", kt)], writes=[psk[b]])
                        affine_evac(evac_eng(), hv[vg][:, c0:c1], ps[b][:, 0:n], reads=[psk[b]], writes=[("hv", vg)])
                    P.op("dve", lambda e, vg=vg, ft=ft: e.tensor_scalar(out=cv[vg][:], in0=hv[vg][:, HO - 2:HO - 2 + OWN], scalar1=cvp[:, 0, ft:ft + 1], scalar2=cvp[:, 3, ft:ft + 1],
                                                                       op0=ALU.mult, op1=ALU.add),
                         reads=[("hv", vg), "cvp"], writes=[("cv", vg)])
                    for tap in (1, 2):
                        P.op("dve", lambda e, vg=vg, ft=ft, tap=tap: e.scalar_tensor_tensor(out=cv[vg][:], in0=hv[vg][:, HO - 2 + tap:HO - 2 + tap + OWN], scalar=cvp[:, tap, ft:ft + 1], in1=cv[vg][:],
                                                                                             op0=ALU.mult, op1=ALU.add),
                             reads=[("hv", vg), ("cv", vg), "cvp"], writes=[("cv", vg)])
                P.op("act", lambda e: e.activation(out=cv[1][:], in_=cv[1][:], func=AF.Gelu_apprx_tanh), reads=[("cv", 1)], writes=[("cv", 1)])
                P.op("dve", lambda e, jj=jj: e.tensor_tensor(out=actb[:, jj, :], in0=cv[0][:], in1=cv[1][:], op=ALU.mult),
                     reads=[("cv", 0), ("cv", 1)], writes=[("actb", jj)])
            for mt in range(16):
                for (c0, c1) in OB:
                    b = nb()
                    for jj in range(4):
                        P.op("pe", lambda e, jj=jj, b=b, mt=mt, c0=c0, c1=c1: e.matmul(ps[b][:], Wd[:, jj, mt * 128:(mt + 1) * 128], actb[:, jj, c0 - HO:c1 - HO], start=(jj == 0), stop=(jj == 3)),
                             reads=[("Wd", jj), ("actb", jj)], writes=[psk[b]])
                    P.op("dve", lambda e, b=b, mt=mt, c0=c0, c1=c1: e.tensor_tensor(out=r1[:, mt, c0:c1], in0=r1[:, mt, c0:c1], in1=ps[b][:], op=ALU.add),
                         reads=[psk[b], ("r1", mt, c0)], writes=[("r1", mt, c0)])
        P.barrier()

    layer_norm_fm(OB, G2, B2, "r1")
    with ExitStack() as s7:
        ot = [sb(s7, f"ot{i}", [128, D], F32) for i in range(2)]
        for tt in range(8):
            oi = tt % 2
            for k4 in range(4):
                b = nb()
                for j in range(4):
                    kt = k4 * 4 + j
                    P.op("pe", lambda e, kt=kt, j=j, b=b, tt=tt: e.transpose(out=ps[b][:, j * 128:(j + 1) * 128], in_=r1[:, kt, HO + tt * 128:HO + (tt + 1) * 128], identity=ident[:]),
                         reads=[("r1", kt, c0) for (c0, c1) in OB] + ["ident"], writes=[psk[b]])
                affine_evac(evac_eng(), ot[oi][:, k4 * 512:(k4 + 1) * 512], ps[b][:], reads=[psk[b]], writes=[("ot", oi)])
            P.dma(f"ot{oi}", out_d[tt * 128:(tt + 1) * 128, :], ot[oi][:], reads=[("ot", oi)], writes=[("out", tt)])
        P.barrier()


_CACHE = {}


def _cols(v, n):
    return np.ascontiguousarray(np.asarray(v, np.float32).reshape(n, 128).T)


def kernel(x, p, ln_in_g, ln_in_b, w_in, lambda_q1, lambda_k1, lambda_q2, lambda_k2, g_subln, a_re, a_im, log_dt,
           b_re, b_im, c_re, c_im, d_skip, w_glu, b_glu, w_o, ln1_g, ln1_b, w_up, conv_w, conv_b, w_down, w_ple,
           w_pg, b_pg, ln2_g, ln2_b):
    f = lambda a: np.asarray(a, np.float32)
    x = f(x)[0]
    pp = f(p)[0, 0]
    if "nc" not in _CACHE:
        _CACHE["nc"] = build_program()
    nc = _CACHE["nc"]
    vecs = np.zeros((128, 144), np.float32)
    for i, v in enumerate([ln_in_g, ln_in_b, f(ln1_g)[0], f(ln1_b)[0], f(ln2_g)[0], f(ln2_b)[0], f(b_pg)[0]]):
        vecs[:, i * 16:(i + 1) * 16] = _cols(v, 16)
    vecs[:, 112:120] = _cols(f(b_glu)[0], 8)
    vecs[:, 120] = f(g_subln)[0]
    vecs[:, 128:136] = _cols(f(d_skip)[0], 8)
    convp = np.zeros((128, 4, 88), np.float32)
    for t in range(3):
        convp[:, t, :] = _cols(f(conv_w)[0, t], 88)
    convp[:, 3, :] = _cols(f(conv_b)[0], 88)
    lamv = np.stack([np.broadcast_to(f(v)[0], (128, 64)) for v in (lambda_q1, lambda_k1, lambda_q2, lambda_k2)], axis=1)
    ar, ai, ldt = f(a_re)[0], f(a_im)[0], f(log_dt)[0]
    br, bi, cr, ci = f(b_re)[0], f(b_im)[0], f(c_re)[0], f(c_im)[0]
    def clay(a_gn):
        a = a_gn.reshape(8, 8, 1, 64)
        a = np.broadcast_to(a, (8, 8, 16, 64))
        return a.transpose(1, 2, 0, 3).reshape(128, 512)
    def clay_b(b_gnc):
        a = b_gnc.reshape(8, 8, 64, 16).transpose(1, 3, 0, 2)
        return a.reshape(128, 512)
    ssmC = np.stack([clay(ar), clay(ai), clay_b(br), clay_b(bi), clay(np.broadcast_to(ldt[:, None], (64, 64)))], axis=1)
    def slay(a_gn):
        a = a_gn.reshape(32, 2, 64, 1)
        a = np.broadcast_to(a, (32, 2, 64, 16))
        return a.transpose(1, 2, 0, 3).reshape(128, 512)
    def slay_c(c_gcn):
        a = c_gcn.reshape(32, 2, 16, 64).transpose(1, 3, 0, 2)
        return a.reshape(128, 512)
    def slay_b(b_gnc):
        a = b_gnc.reshape(32, 2, 64, 16).transpose(1, 2, 0, 3)
        return a.reshape(128, 512)
    ssmS = np.stack([slay(ar), slay(ai), slay(np.broadcast_to(ldt[:, None], (64, 64))), slay_c(cr), slay_c(ci), slay_b(br), slay_b(bi)], axis=1)
    ident = np.eye(128, dtype=np.float32)
    slopes = 2.0 ** (-np.arange(1, 9, dtype=np.float64))
    kk = np.arange(128)[:, None]
    qq = np.arange(128)[None, :]
    dtile = np.zeros((128, 8, 128), np.float32)
    for h in range(8):
        dmat = 8.0 * (-slopes[h] * np.abs(qq - kk) + slopes[h] * (qq - 127))
        dmat = np.where((kk // 64) <= (qq // 64), dmat, NEG)
        dtile[:, h, :] = dmat
    shared = {
        "w_in": f(w_in)[0], "w_glu": f(w_glu)[0], "w_o": f(w_o)[0], "w_up": f(w_up)[0], "w_down": f(w_down)[0],
        "w_ple": f(w_ple)[0], "w_pg": f(w_pg)[0], "vecs": vecs, "convp": convp.reshape(128, -1),
        "lamv": np.ascontiguousarray(lamv.reshape(128, -1)), "ident": ident, "dtile": dtile.reshape(128, -1),
        "ssmC": np.ascontiguousarray(ssmC.reshape(128, -1)), "ssmS": np.ascontiguousarray(ssmS.reshape(128, -1)),
    }
    in_maps = []
    origins = [7167] + [7168 + 128 * i + 127 for i in range(8)]
    diagt = [55] + [56 + i for i in range(8)]
    for c in range(NCORES):
        start = S - OWN * (c + 1)
        xcx = np.zeros((S, D), np.float32)
        xcx[start:] = x[:OWN * (c + 1)]
        kb = np.zeros((128, 8, 64, 9), np.float32)
        kpos = (np.arange(64)[None, :] * 128 + np.arange(128)[:, None]).astype(np.float64)
        vmask = kpos >= start
        for h in range(8):
            per = GW[h] // 128
            for s_ in range(9):
                if s_ == 0:
                    og = origins[0]
                else:
                    qt = s_ - 1
                    og = origins[1 + (qt // per) * per + per - 1]
                val = slopes[h] * (kpos - og)
                val[:, diagt[s_]] = -slopes[h] * (og - origins[s_])
                val = np.where(vmask, np.minimum(val, 0.0), NEG)
                kb[:, h, :, s_] = val
        vd = np.zeros((128, 17), np.float32)
        for tb in range(16):
            vd[:, tb] = 1.0 if tb * 512 >= start else 0.0
        vd[:, 16] = 1.0 if c > 0 else 0.0
        m = dict(shared)
        m.update({"xc": xcx, "pc": np.ascontiguousarray(pp[c * OWN:(c + 1) * OWN]), "kbias": kb.reshape(128, -1), "valid": vd})
        in_maps.append(m)
    res = run_bass_kernel_spmd(nc, in_maps, core_ids=list(range(NCORES)))
    out = np.concatenate([np.asarray(res.results[c]["out"], np.float32) for c in range(NCORES)], axis=0)
    return out[None].astype(np.float32)
```

```python
import math
from contextlib import ExitStack

import numpy as np
import concourse.bass as bass
import concourse.mybir as mybir
from concourse.bass_utils import run_bass_kernel_spmd

F32 = mybir.dt.float32
BF16 = mybir.dt.bfloat16
AF = mybir.ActivationFunctionType
ALU = mybir.AluOpType

NCORES = 8
S = 8192
D = 2048
OWN = 1024
NQ = 1040
HO = 16
QB = [(14, 16), (16, 528), (528, 1040)]
DFF = 5632
ALPHA = 2.0 ** 0.25
EPS = 1e-5
LAM_INIT = 0.8 - 0.6 * math.exp(0.0)
NEG = -1.0e30
GW = [128, 256, 512, 512, 512, 512, 512, 512]
DEBUG = False
PH1_BLOCKS = 16
NO_STORE = False
NO_HALO = False
STOP_AFTER = None


class Prog:
    ENG = ("pe", "act", "dve", "pool", "sp")

    def __init__(self, nc, stack, same_engine_sync=True):
        self.nc = nc
        self.stack = stack
        self.e = {"pe": nc.tensor, "act": nc.scalar, "dve": nc.vector, "pool": nc.gpsimd, "sp": nc.sync}
        self.sem = {k: stack.enter_context(nc.semaphore("s_" + k)) for k in ("pe", "act", "dve", "pool")}
        self.cnt = {k: 0 for k in self.sem}
        self.dsem = {}
        self.dcnt = {}
        self.seen = {}
        self.lastw = {}
        self.readers = {}
        self.ses = same_engine_sync
        self.nops = 0

    def _semof(self, tok):
        kind, src, val = tok
        return (self.sem[src] if kind == "eng" else self.dsem[src]), val

    def _wait(self, eng, tok):
        kind, src, val = tok
        if kind == "eng" and src == eng and (eng == "pe" or not self.ses):
            return
        key = (eng, kind, src)
        if self.seen.get(key, 0) >= val:
            return
        self.seen[key] = val
        s, v = self._semof(tok)
        self.e[eng].wait_ge(s, v)

    def _deps(self, eng, reads, writes):
        for b in reads:
            t = self.lastw.get(b)
            if t is not None:
                self._wait(eng, t)
        for b in writes:
            t = self.lastw.get(b)
            if t is not None:
                self._wait(eng, t)
            for t in self.readers.get(b, ()):
                self._wait(eng, t)

    def _commit(self, tok, reads, writes):
        for b in writes:
            self.lastw[b] = tok
            self.readers[b] = []
        for b in reads:
            r = self.readers.setdefault(b, [])
            r[:] = [t for t in r if (t[0], t[1]) != (tok[0], tok[1])]
            r.append(tok)

    def op(self, eng, fn, reads=(), writes=()):
        for k in reads:
            if isinstance(k, str) and k[:2] == "ps" and k[2:].isdigit():
                for t in self.readers.get(k, ()):
                    if t[1] != eng:
                        self._wait(eng, t)
        self._deps(eng, reads, writes)
        self.cnt[eng] += 1
        tok = ("eng", eng, self.cnt[eng])
        fn(self.e[eng]).then_inc(self.sem[eng], 1)
        self._commit(tok, reads, writes)
        self.nops += 1
        return tok

    def dma(self, slot, out, in_, reads=(), writes=(), q="sp", **kw):
        if slot not in self.dsem:
            self.dsem[slot] = self.stack.enter_context(self.nc.semaphore("d_" + slot))
            self.dcnt[slot] = 0
        self._deps(q, reads, writes)
        self.dcnt[slot] += 16
        tok = ("dma", slot, self.dcnt[slot])
        self.e[q].dma_start(out=out, in_=in_, **kw).then_inc(self.dsem[slot], 16)
        self._commit(tok, reads, writes)
        self.nops += 1
        return tok

    def barrier(self):
        toks = [("eng", k, self.cnt[k]) for k in self.sem if self.cnt[k] > 0]
        toks += [("dma", s, c) for s, c in self.dcnt.items()]
        for eng in self.ENG:
            for t in toks:
                self._wait(eng, t)
        self.lastw.clear()
        self.readers.clear()

    def finish(self):
        for k in self.sem:
            if self.cnt[k]:
                self._wait("sp", ("eng", k, self.cnt[k]))
        for s, c in self.dcnt.items():
            self._wait("sp", ("dma", s, c))


def build_program():
    nc = bass.Bass("TRN2", target_bir_lowering=False)

    def din(name, shape, dt=F32):
        return nc.dram_tensor(name, list(shape), dt, kind="ExternalInput").ap()

    def dscr(name, shape, dt):
        return nc.dram_tensor(name, list(shape), dt, kind="Internal").ap()

    xc = din("xc", [S, D])
    pc = din("pc", [OWN, 256])
    w_in = din("w_in", [D, 4096])
    w_glu = din("w_glu", [1024, 1024])
    w_o = din("w_o", [D, D])
    w_up = din("w_up", [D, 2 * DFF])
    w_down = din("w_down", [DFF, D])
    w_ple = din("w_ple", [256, D])
    w_pg = din("w_pg", [D, D])
    vecs = din("vecs", [128, 16 * 9])
    convp = din("convp", [128, 4 * 88])
    lamv = din("lamv", [128, 4 * 64])
    ident_d = din("ident", [128, 128])
    kbias_d = din("kbias", [128, 8 * 64 * 9])
    dt_d = din("dtile", [128, 8 * 128])
    valid_d = din("valid", [128, 17])
    ssmC = din("ssmC", [128, 5 * 512])
    ssmS = din("ssmS", [128, 7 * 512])
    out_d = nc.dram_tensor("out", [OWN, D], F32, kind="ExternalOutput").ap()
    if DEBUG:
        dbg_cat = nc.dram_tensor("dbg_cat", [16, 128, NQ], F32, kind="ExternalOutput").ap()

    KT_scr = dscr("KT_scr", [8, 128, S], BF16)
    V_scr = dscr("V_scr", [64, 128, 1024], BF16)
    UT_scr = dscr("UT_scr", [8, 128, S], BF16)
    H0_scr = dscr("H0_scr", [16, 128, NQ], F32)
    XT_scr = dscr("XT_scr", [16, 128, NQ], BF16)

    with ExitStack() as gs:
        P = Prog(nc, gs)

        def sb(st, name, shape, dt):
            return st.enter_context(nc.sbuf_tensor("sb_" + name, list(shape), dt))

        ps = [gs.enter_context(nc.psum_tensor(f"ps{i}", [128, 512], F32)) for i in range(8)]
        psk = [f"ps{i}" for i in range(8)]

        ident = sb(gs, "ident", [128, 128], F32)
        identb = sb(gs, "identb", [128, 128], BF16)
        onesb = sb(gs, "onesb", [128, 128], BF16)
        onesf = sb(gs, "onesf", [128, 128], F32)
        vec = sb(gs, "vec", [128, 144], F32)
        valid = sb(gs, "valid", [128, 17], F32)
        lamcol = sb(gs, "lamcol", [128, 4], F32)
        halo_f = sb(gs, "halo_f", [128, 16, 16], F32)
        halo_b = sb(gs, "halo_b", [128, 16, 16], BF16)
        P.dma("c0", ident[:], ident_d[:, :], writes=["ident"])
        P.dma("c1", vec[:], vecs[:, :], writes=["vec"])
        P.dma("c2", valid[:], valid_d[:, :], writes=["valid"])
        P.op("dve", lambda e: e.tensor_copy(out=identb[:], in_=ident[:]), reads=["ident"], writes=["identb"])
        P.op("dve", lambda e: e.memset(onesb[:], 1.0), writes=["onesb"])
        P.op("dve", lambda e: e.memset(onesf[:], 1.0), writes=["onesf"])
        G_IN, B_IN, G1, B1, G2, B2, BPG, MISC = [vec[:, i * 16:(i + 1) * 16] for i in range(8)]

        with ExitStack() as s0:
            lv = sb(s0, "lv", [128, 4, 64], F32)
            lt = sb(s0, "lt", [128, 64], F32)
            ld = sb(s0, "ld", [128, 2], F32)
            P.dma("c3", lv[:].rearrange("p a b -> p (a b)"), lamv[:, :], writes=["lv"])
            for i in range(2):
                P.op("dve", lambda e, i=i: e.tensor_tensor(out=lt[:], in0=lv[:, 2 * i, :], in1=lv[:, 2 * i + 1, :], op=ALU.mult),
                     reads=["lv"], writes=["lt"])
                P.op("dve", lambda e, i=i: e.tensor_reduce(out=ld[:, i:i + 1], in_=lt[:], axis=mybir.AxisListType.X, op=ALU.add),
                     reads=["lt"], writes=[("ld", i)])
            P.op("act", lambda e: e.activation(out=ld[:], in_=ld[:], func=AF.Exp), reads=[("ld", 0), ("ld", 1)], writes=["ld"])
            P.op("dve", lambda e: e.tensor_tensor(out=lamcol[:, 0:1], in0=ld[:, 1:2], in1=ld[:, 0:1], op=ALU.subtract),
                 reads=["ld"], writes=["lamcol"])
            P.op("dve", lambda e: e.tensor_scalar(out=lamcol[:, 0:1], in0=lamcol[:, 0:1], scalar1=-LAM_INIT, scalar2=None, op0=ALU.add),
                 reads=["lamcol"], writes=["lamcol"])
            P.barrier()

        rr = {"ps": 0, "ev": 0}

        def evac_eng():
            rr["ev"] += 1
            return "dve" if rr["ev"] % 2 else "act"

        def affine_evac(eng, out, in_, scale=None, bias=None, reads=(), writes=()):
            if eng == "act":
                kw = {}
                if scale is not None:
                    kw["scale"] = scale
                if bias is not None:
                    kw["bias"] = bias
                P.op("act", lambda e: e.activation(out=out, in_=in_, func=AF.Identity, **kw), reads=reads, writes=writes)
            else:
                if scale is None and bias is None:
                    P.op("dve", lambda e: e.tensor_copy(out=out, in_=in_), reads=reads, writes=writes)
                elif bias is None:
                    P.op("dve", lambda e: e.tensor_scalar(out=out, in0=in_, scalar1=scale, scalar2=None, op0=ALU.mult), reads=reads, writes=writes)
                elif scale is None:
                    P.op("dve", lambda e: e.tensor_scalar(out=out, in0=in_, scalar1=bias, scalar2=None, op0=ALU.add), reads=reads, writes=writes)
                else:
                    P.op("dve", lambda e: e.tensor_scalar(out=out, in0=in_, scalar1=scale, scalar2=bias, op0=ALU.mult, op1=ALU.add), reads=reads, writes=writes)

        with ExitStack() as s1:
            W = sb(s1, "W1", [128, 16, 3072], BF16)
            for kt in range(16):
                for cb in range(3):
                    P.dma(f"w1_{kt}", W[:, kt, cb * 1024:(cb + 1) * 1024],
                          w_in[kt * 128:(kt + 1) * 128, 1024 + cb * 1024:1024 + (cb + 1) * 1024],
                          writes=[("W1", kt)], q="pool")
            xq = [sb(s1, f"xq{i}", [128, D], F32) for i in range(2)]
            xh = [sb(s1, f"xh{i}", [128, D], F32) for i in range(4)]
            h0T = [sb(s1, "h0T0", [128, 16, 512], BF16)]
            stats = [sb(s1, f"stats{i}", [128, 4, 6], F32) for i in range(2)]
            mv = [sb(s1, f"mv{i}", [128, 2], F32) for i in range(2)]
            rstd = [sb(s1, f"rstd{i}", [128, 1], F32) for i in range(2)]
            nmr = [sb(s1, f"nmr{i}", [128, 1], F32) for i in range(2)]
            stg = [sb(s1, f"stg{i}", [128, 512], BF16) for i in range(6)]
            h0f = [sb(s1, f"h0f{i}", [128, 512], F32) for i in range(2)]
            nstg = 0
            nx = 0
            for tb in range(16 - PH1_BLOCKS, 16):
                hb = 0
                r0 = tb * 512
                for tt in range(4):
                    xi = nx % 2
                    nx += 1
                    P.dma(f"x{xi}", xq[xi][:], xc[r0 + tt * 128:r0 + (tt + 1) * 128, :], writes=[("xq", xi)])
                    for c in range(4):
                        P.op("dve", lambda e, c=c, xi=xi: e.bn_stats(out=stats[xi][:, c, :], in_=xq[xi][:, c * 512:(c + 1) * 512]),
                             reads=[("xq", xi)], writes=[("stats", xi, c)])
                    P.op("dve", lambda e, xi=xi: e.bn_aggr(out=mv[xi][:], in_=stats[xi][:].rearrange("p a b -> p (a b)")),
                         reads=[("stats", xi, c) for c in range(4)], writes=[("mv", xi)])
                    P.op("act", lambda e, xi=xi: e.activation(out=rstd[xi][:], in_=mv[xi][:, 1:2], func=AF.Sqrt, bias=EPS, scale=1.0),
                         reads=[("mv", xi)], writes=[("rstd", xi)])
                    P.op("dve", lambda e, xi=xi: e.reciprocal(out=rstd[xi][:], in_=rstd[xi][:]), reads=[("rstd", xi)], writes=[("rstd", xi)])
                    P.op("dve", lambda e, xi=xi: e.scalar_tensor_tensor(out=nmr[xi][:], in0=mv[xi][:, 0:1], scalar=-1.0, in1=rstd[xi][:],
                                                                        op0=ALU.mult, op1=ALU.mult),
                         reads=[("mv", xi), ("rstd", xi)], writes=[("nmr", xi)])
                    P.op("act", lambda e, xi=xi, tt=tt: e.activation(out=xh[tt][:], in_=xq[xi][:], func=AF.Identity, bias=nmr[xi][:], scale=rstd[xi][:]),
                         reads=[("xq", xi), ("nmr", xi), ("rstd", xi)], writes=[("xh", tt)])
                for kt in range(16):
                    b = kt % 4
                    for tt in range(4):
                        P.op("pe", lambda e, kt=kt, tt=tt, b=b: e.transpose(out=ps[b][:, tt * 128:(tt + 1) * 128],
                                                                          in_=xh[tt][:, kt * 128:(kt + 1) * 128], identity=ident[:]),
                             reads=[("xh", tt), "ident"], writes=[psk[b]])
                    affine_evac(evac_eng(), h0T[hb][:, kt, :], ps[b][:], scale=G_IN[:, kt:kt + 1], bias=B_IN[:, kt:kt + 1],
                                reads=[psk[b], "vec"], writes=[("h0T", hb, kt)])
                    if tb == 13:
                        affine_evac("dve", halo_f[:, kt, :], ps[b][:, 496:512], scale=G_IN[:, kt:kt + 1], bias=B_IN[:, kt:kt + 1],
                                    reads=[psk[b], "vec"], writes=[("halo_f", kt)])
                        P.op("dve", lambda e, kt=kt: e.tensor_copy(out=halo_b[:, kt, :], in_=h0T[hb][:, kt, 496:512]),
                             reads=[("h0T", hb, kt)], writes=[("halo_b", kt)])
                    if tb >= 14:
                        fi = kt % 2
                        affine_evac("dve", h0f[fi][:], ps[b][:], scale=G_IN[:, kt:kt + 1], bias=B_IN[:, kt:kt + 1],
                                    reads=[psk[b], "vec"], writes=[("h0f", fi)])
                        dc = HO + (tb - 14) * 512
                        P.dma(f"h0s{fi}", H0_scr[kt, :, dc:dc + 512], h0f[fi][:], reads=[("h0f", fi)], writes=[("H0", kt, tb)])
                        P.dma(f"xts{fi}", XT_scr[kt, :, dc:dc + 512], h0T[hb][:, kt, :], reads=[("h0T", hb, kt)], writes=[("XT", kt, tb)])
                hreads = [("h0T", hb, kt) for kt in range(16)]

                def proj_fm(col0, dst, dkey, scale=None):
                    nonlocal nstg
                    b = 4 + rr["ps"] % 4
                    rr["ps"] += 1
                    for kt in range(16):
                        P.op("pe", lambda e, kt=kt, b=b: e.matmul(ps[b][:], W[:, kt, col0:col0 + 128], h0T[hb][:, kt, :],
                                                                  start=(kt == 0), stop=(kt == 15)),
                             reads=[("W1", kt), ("h0T", hb, kt)], writes=[psk[b]])
                    si = nstg % 6
                    nstg += 1
                    affine_evac(evac_eng(), stg[si][:], ps[b][:], scale=scale, reads=[psk[b], "valid"], writes=[("stg", si)])
                    if not NO_STORE:
                        P.dma(f"stg{si}", dst, stg[si][:], reads=[("stg", si)], writes=[dkey])

                for h in range(8):
                    proj_fm(h * 128, KT_scr[h, :, r0:r0 + 512], ("KT", h, tb))
                for t in range(8):
                    proj_fm(2048 + t * 128, UT_scr[t, :, r0:r0 + 512], ("UT", t, tb), scale=valid[:, tb:tb + 1])
                for tt in range(4):
                    for fh in range(2):
                        b = 4 + rr["ps"] % 4
                        rr["ps"] += 1
                        for kt in range(16):
                            P.op("pe", lambda e, kt=kt, b=b, tt=tt, fh=fh: e.matmul(
                                ps[b][:], h0T[hb][:, kt, tt * 128:(tt + 1) * 128], W[:, kt, 1024 + fh * 512:1024 + (fh + 1) * 512],
                                start=(kt == 0), stop=(kt == 15)),
                                reads=[("W1", kt), ("h0T", hb, kt)], writes=[psk[b]])
                        si = nstg % 6
                        nstg += 1
                        affine_evac(evac_eng(), stg[si][:], ps[b][:], reads=[psk[b]], writes=[("stg", si)])
                        if not NO_STORE:
                            P.dma(f"stg{si}", V_scr[tb * 4 + tt, :, fh * 512:(fh + 1) * 512], stg[si][:], reads=[("stg", si)],
                                  writes=[("V", tb * 4 + tt, fh)])
            P.barrier()
        if STOP_AFTER == 1:
            P.finish()
            return nc

        catT = sb(gs, "catT", [128, 16, NQ], BF16)
        with ExitStack() as s2:
            qT = sb(s2, "qT", [128, 8, NQ], BF16)
            dtile = sb(s2, "dtile", [128, 8, 128], BF16)
            P.dma("dt", dtile[:].rearrange("p a b -> p (a b)"), dt_d[:, :], writes=["dtile"], q="pool")
            with ExitStack() as s2a:
                xT = sb(s2a, "xT", [128, 16, NQ], BF16)
                Wq = sb(s2a, "Wq", [128, 16, 1024], BF16)
                for kt in range(16):
                    P.dma(f"xt{kt % 4}", xT[:, kt, HO:NQ], XT_scr[kt, :, HO:NQ], writes=[("xT", kt)])
                    P.op("dve", lambda e, kt=kt: e.tensor_copy(out=xT[:, kt, 0:HO], in_=halo_b[:, kt, :]), reads=[("xT", kt)], writes=[("xT", kt)])
                    P.dma(f"wq{kt % 4}", Wq[:, kt, :], w_in[kt * 128:(kt + 1) * 128, 0:1024], writes=[("Wq", kt)], q="pool")
                for h in range(8):
                    for (c0, c1) in QB:
                        b = rr["ps"] % 8
                        rr["ps"] += 1
                        for kt in range(16):
                            P.op("pe", lambda e, kt=kt, b=b, h=h, c0=c0, c1=c1: e.matmul(
                                ps[b][:, 0:c1 - c0], Wq[:, kt, h * 128:(h + 1) * 128], xT[:, kt, c0:c1], start=(kt == 0), stop=(kt == 15)),
                                reads=[("Wq", kt), ("xT", kt)], writes=[psk[b]])
                        affine_evac(evac_eng(), qT[:, h, c0:c1], ps[b][:, 0:c1 - c0], reads=[psk[b]], writes=[("qT", h, c0)])
                P.barrier()

            kT = [sb(s2, f"kT{i}", [128, S], BF16) for i in range(2)]
            vh = [sb(s2, f"vh{i}", [128, 64, 128], BF16) for i in range(2)]
            kb = [sb(s2, f"kb{i}", [128, 64, 9], F32) for i in range(2)]
            pT = [[sb(s2, f"pT{m}_{i}", [128, 512], BF16) for i in range(3)] for m in range(2)]
            rl = [sb(s2, f"rl{m}", [128, 512], F32) for m in range(2)]
            oo = sb(s2, "oo", [128, 512], F32)
            o2 = sb(s2, "o2", [128, 512], F32)
            sq = sb(s2, "sq", [128, 512], F32)
            gs8 = sb(s2, "gs8", [128, 1], F32)
            P.op("dve", lambda e: e.tensor_scalar(out=gs8[:], in0=MISC[:, 8:9], scalar1=1.0 - LAM_INIT, scalar2=None, op0=ALU.mult),
                 reads=["vec"], writes=["gs8"])
            npt = 0
            for h in range(8):
                hb = h % 2
                for j in range(4):
                    P.dma(f"kt{hb}", kT[hb][:, j * 2048:(j + 1) * 2048], KT_scr[h, :, j * 2048:(j + 1) * 2048], writes=[("kT", hb)])
                for j in range(4):
                    P.dma(f"vh{hb}", vh[hb][:, j * 16:(j + 1) * 16, :],
                          V_scr[j * 16:(j + 1) * 16, :, h * 128:(h + 1) * 128].rearrange("k p d -> p k d"), writes=[("vh", hb)])
                P.dma(f"kb{hb}", kb[hb][:].rearrange("p a b -> p (a b)"), kbias_d[:, h * 576:(h + 1) * 576], writes=[("kb", hb)])
                for qi, (c0, c1) in enumerate(QB):
                    n = c1 - c0
                    if qi == 0:
                        subs = [(0, 2, 0, 55)]
                        klast = 55
                    else:
                        subs = [(i * 128, 128, 1 + (qi - 1) * 4 + i, 56 + (qi - 1) * 4 + i) for i in range(4)]
                        klast = 56 + (qi - 1) * 4 + 3
                    acc = [4, 5, 6, 7]
                    per = GW[h] // 128
                    steps = []
                    for kt in range(klast + 1):
                        act_subs = [s_ for s_ in subs if s_[3] >= kt]
                        groups = []
                        if qi == 0:
                            groups = [(sl, sw, sid) for (sl, sw, sid, dk) in act_subs]
                        else:
                            for g0 in range(0, 4, per):
                                grp = [s_ for s_ in subs[g0:g0 + per] if s_[3] >= kt]
                                if len(grp) == per and all(s_[3] > kt for s_ in grp):
                                    groups.append((grp[0][0], sum(x[1] for x in grp), grp[-1][2]))
                                else:
                                    groups += [(sl, sw, sid) for (sl, sw, sid, dk) in grp]
                        steps.append((kt, act_subs, groups))

                    def emit_S(i):
                        kt, act_subs, groups = steps[i]
                        lo = act_subs[0][0]
                        hi = act_subs[-1][0] + act_subs[-1][1]
                        pi = i % 3
                        for m in range(2):
                            b = (i % 2) * 2 + m
                            P.op("pe", lambda e, m=m, b=b, kt=kt, lo=lo, hi=hi: e.matmul(
                                ps[b][:, lo:hi], kT[hb][m * 64:(m + 1) * 64, kt * 128:(kt + 1) * 128],
                                qT[m * 64:(m + 1) * 64, h, c0 + lo:c0 + hi], start=True, stop=True),
                                reads=[("kT", hb)] + [("qT", h, c0)], writes=[psk[b]])
                            for (sl, sw, sid, dk) in act_subs:
                                if dk == kt:
                                    dsl = dtile[:, h, 128 - sw:128] if sw < 128 else dtile[:, h, :]
                                    P.op("pe", lambda e, b=b, sl=sl, sw=sw, dsl=dsl: e.matmul(
                                        ps[b][:, sl:sl + sw], identb[:], dsl, start=False, stop=True, skip_group_check=True),
                                        reads=["identb", "dtile"], writes=[psk[b]])
                            for (sl, sw, sid) in groups:
                                P.op("act", lambda e, m=m, b=b, sl=sl, sw=sw, sid=sid, kt=kt, pi=pi: e.activation(
                                    out=pT[m][pi][:, sl:sl + sw], in_=ps[b][:, sl:sl + sw], func=AF.Exp,
                                    bias=kb[hb][:, kt, sid:sid + 1], scale=0.125),
                                    reads=[psk[b], ("kb", hb)], writes=[("pT", m, pi, sl)])

                    def emit_PV(i):
                        kt, act_subs, groups = steps[i]
                        lo = act_subs[0][0]
                        hi = act_subs[-1][0] + act_subs[-1][1]
                        pi = i % 3
                        first = (i == 0)
                        for m in range(2):
                            P.op("pe", lambda e, m=m, kt=kt, lo=lo, hi=hi, pi=pi, first=first: e.matmul(
                                ps[acc[m]][:, lo:hi], vh[hb][:, kt, :], pT[m][pi][:, lo:hi], start=first, stop=(kt == klast),
                                skip_group_check=True),
                                reads=[("vh", hb)] + [("pT", m, pi, g_[0]) for g_ in groups], writes=[psk[acc[m]]])
                            P.op("pe", lambda e, m=m, kt=kt, lo=lo, hi=hi, pi=pi, first=first: e.matmul(
                                ps[acc[2 + m]][:, lo:hi], onesb[:], pT[m][pi][:, lo:hi], start=first, stop=(kt == klast),
                                skip_group_check=True),
                                reads=["onesb"] + [("pT", m, pi, g_[0]) for g_ in groups], writes=[psk[acc[2 + m]]])

                    emit_S(0)
                    for i in range(len(steps)):
                        if i + 1 < len(steps):
                            emit_S(i + 1)
                        emit_PV(i)
                    for m in range(2):
                        P.op("dve", lambda e, m=m: e.tensor_scalar(out=rl[m][:, 0:n], in0=ps[acc[2 + m]][:, 0:n], scalar1=1e-37, scalar2=None, op0=ALU.add),
                             reads=[psk[acc[2 + m]]], writes=[("rl", m)])
                        P.op("dve", lambda e, m=m: e.reciprocal(out=rl[m][:, 0:n], in_=rl[m][:, 0:n]), reads=[("rl", m)], writes=[("rl", m)])
                    P.op("dve", lambda e: e.tensor_tensor(out=oo[:, 0:n], in0=ps[acc[0]][:, 0:n], in1=rl[0][:, 0:n], op=ALU.mult),
                         reads=[psk[acc[0]], ("rl", 0)], writes=["oo"])
                    P.op("dve", lambda e: e.tensor_tensor(out=o2[:, 0:n], in0=ps[acc[1]][:, 0:n], in1=rl[1][:, 0:n], op=ALU.mult),
                         reads=[psk[acc[1]], ("rl", 1)], writes=["o2"])
                    P.op("dve", lambda e: e.scalar_tensor_tensor(out=oo[:, 0:n], in0=o2[:, 0:n], scalar=lamcol[:, 0:1], in1=oo[:, 0:n],
                                                                 op0=ALU.mult, op1=ALU.add),
                         reads=["o2", "oo", "lamcol"], writes=["oo"])
                    P.op("act", lambda e: e.activation(out=sq[:, 0:n], in_=oo[:, 0:n], func=AF.Square), reads=["oo"], writes=["sq"])
                    P.op("pe", lambda e: e.matmul(ps[0][:, 0:n], onesf[:], sq[:, 0:n], start=True, stop=True),
                         reads=["onesf", "sq"], writes=[psk[0]])
                    P.op("act", lambda e: e.activation(out=sq[:, 0:n], in_=ps[0][:, 0:n], func=AF.Sqrt, bias=EPS, scale=1.0 / 128.0),
                         reads=[psk[0]], writes=["sq"])
                    P.op("dve", lambda e: e.reciprocal(out=sq[:, 0:n], in_=sq[:, 0:n]), reads=["sq"], writes=["sq"])
                    P.op("dve", lambda e: e.tensor_tensor(out=oo[:, 0:n], in0=oo[:, 0:n], in1=sq[:, 0:n], op=ALU.mult),
                         reads=["oo", "sq"], writes=["oo"])
                    P.op("dve", lambda e, h=h, c0=c0, c1=c1: e.tensor_scalar(out=catT[:, h, c0:c1], in0=oo[:, 0:n], scalar1=gs8[:], scalar2=None, op0=ALU.mult),
                         reads=["oo", "gs8"], writes=[("catT", h, c0)])
            P.barrier()
        if STOP_AFTER == 2:
            P.finish()
            return nc

        build_rest(nc, P, gs, sb, ps, psk, rr, affine_evac, evac_eng, locals())
        P.finish()
    return nc


def build_rest(nc, P, gs, sb, ps, psk, rr, affine_evac, evac_eng, L):
    catT, vec, valid, ident, identb, onesb, onesf = (L[k] for k in ("catT", "vec", "valid", "ident", "identb", "onesb", "onesf"))
    G_IN, B_IN, G1, B1, G2, B2, BPG, MISC = (L[k] for k in ("G_IN", "B_IN", "G1", "B1", "G2", "B2", "BPG", "MISC"))
    UT_scr, H0_scr, ssmC, ssmS = (L[k] for k in ("UT_scr", "H0_scr", "ssmC", "ssmS"))
    w_glu, w_o, w_up, w_down, w_ple, w_pg, pc, convp, out_d = (L[k] for k in ("w_glu", "w_o", "w_up", "w_down", "w_ple", "w_pg", "pc", "convp", "out_d"))
    PI = math.pi
    LC = 8
    NCH = S // LC
    OWNCH = 130
    CTX = NCH - OWNCH

    def TT(o, a, b, op, key):
        P.op("dve", lambda e: e.tensor_tensor(out=o, in0=a, in1=b, op=op), reads=[key], writes=[key])

    def TS(o, a, s1, s2, op0, op1, key):
        if s2 is None:
            P.op("dve", lambda e: e.tensor_scalar(out=o, in0=a, scalar1=s1, scalar2=None, op0=op0), reads=[key], writes=[key])
        else:
            P.op("dve", lambda e: e.tensor_scalar(out=o, in0=a, scalar1=s1, scalar2=s2, op0=op0, op1=op1), reads=[key], writes=[key])

    def STT(o, a, sc, b, op0, op1, key, extra_r=()):
        P.op("dve", lambda e: e.scalar_tensor_tensor(out=o, in0=a, scalar=sc, in1=b, op0=op0, op1=op1), reads=[key] + list(extra_r), writes=[key])

    def ACT(o, a, func, key, **kw):
        P.op("act", lambda e: e.activation(out=o, in_=a, func=func, **kw), reads=[key], writes=[key])

    ygT = sb(gs, "ygT", [128, 8, NQ], BF16)

    with ExitStack() as s3:
        GFb = sb(s3, "GFb", [128, LC, 2, 512], BF16)
        HSb = sb(s3, "HSb", [128, LC + 1, 2, 512], BF16)
        BSb = sb(s3, "BSb", [128, 2, 512], BF16)
        DL = sb(s3, "DL", [128, 11, 2, 32], F32)
        LAMS = sb(s3, "LAMS", [128, 2, 32], F32)
        K = "ssm"

        def lam_calc(st, F, ar, ai, ldt, pre):
            t = {n: sb(st, pre + n, [128, F], F32) for n in ("dt", "mag", "ang", "m", "s", "c", "lr", "li", "den", "cr", "ci", "t1")}
            ACT(t["dt"][:], ldt, AF.Exp, K)
            TT(t["mag"][:], t["dt"][:], ar, ALU.mult, K)
            ACT(t["mag"][:], t["mag"][:], AF.Exp, K)
            TT(t["ang"][:], t["dt"][:], ai, ALU.mult, K)
            for j in range(7):
                TS(t["m"][:], t["ang"][:], (2 * j + 1) * PI, -2 * PI, ALU.is_ge, ALU.mult, K)
                if j == 0:
                    TT(t["s"][:], t["ang"][:], t["m"][:], ALU.add, K)
                else:
                    TT(t["s"][:], t["s"][:], t["m"][:], ALU.add, K)
            TS(t["c"][:], t["s"][:], PI / 2, None, ALU.add, None, K)
            TS(t["m"][:], t["c"][:], PI, -2 * PI, ALU.is_ge, ALU.mult, K)
            TT(t["c"][:], t["c"][:], t["m"][:], ALU.add, K)
            ACT(t["s"][:], t["s"][:], AF.Sin, K)
            ACT(t["c"][:], t["c"][:], AF.Sin, K)
            TT(t["lr"][:], t["mag"][:], t["c"][:], ALU.mult, K)
            TT(t["li"][:], t["mag"][:], t["s"][:], ALU.mult, K)
            TT(t["den"][:], ar, ar, ALU.mult, K)
            TT(t["t1"][:], ai, ai, ALU.mult, K)
            TT(t["den"][:], t["den"][:], t["t1"][:], ALU.add, K)
            P.op("dve", lambda e: e.reciprocal(out=t["den"][:], in_=t["den"][:]), reads=[K], writes=[K])
            TS(t["mag"][:], t["lr"][:], -1.0, None, ALU.add, None, K)
            TT(t["cr"][:], t["mag"][:], ar, ALU.mult, K)
            TT(t["t1"][:], t["li"][:], ai, ALU.mult, K)
            TT(t["cr"][:], t["cr"][:], t["t1"][:], ALU.add, K)
            TT(t["cr"][:], t["cr"][:], t["den"][:], ALU.mult, K)
            TT(t["ci"][:], t["li"][:], ar, ALU.mult, K)
            TT(t["t1"][:], t["mag"][:], ai, ALU.mult, K)
            TT(t["ci"][:], t["ci"][:], t["t1"][:], ALU.subtract, K)
            TT(t["ci"][:], t["ci"][:], t["den"][:], ALU.mult, K)
            return t["lr"], t["li"], t["cr"], t["ci"]

        def cmul(orr, oi, ar_, ai_, br_, bi_, t1, t2):
            TT(t1, ar_, br_, ALU.mult, K)
            TT(t2, ai_, bi_, ALU.mult, K)
            TT(t2, t1, t2, ALU.subtract, K)
            TT(t1, ar_, bi_, ALU.mult, K)
            TT(oi, ai_, br_, ALU.mult, K)
            TT(oi, oi, t1, ALU.add, K)
            P.op("dve", lambda e: e.tensor_copy(out=orr, in_=t2), reads=[K], writes=[K])

        with ExitStack() as sc:
            cs = sb(sc, "cs", [128, 5, 512], F32)
            P.dma("ssmc", cs[:].rearrange("p a b -> p (a b)"), ssmC[:, :], writes=[K])
            lr, li, cr, ci = lam_calc(sc, 512, cs[:, 0, :], cs[:, 1, :], cs[:, 4, :], "C_")
            g = [[sb(sc, f"g{i}{j}", [128, 512], F32) for j in range(2)] for i in range(2)]
            t1 = sb(sc, "ct1", [128, 512], F32)
            t2 = sb(sc, "ct2", [128, 512], F32)
            cmul(g[0][0][:], g[0][1][:], cr[:], ci[:], cs[:, 2, :], cs[:, 3, :], t1[:], t2[:])
            for e_ in range(LC):
                cur, nxt = g[e_ % 2], g[(e_ + 1) % 2]
                for part in range(2):
                    P.op("dve", lambda e, e_=e_, part=part, cur=cur: e.tensor_copy(out=GFb[:, e_, part, :], in_=cur[part][:]), reads=[K], writes=[K])
                if e_ < LC - 1:
                    cmul(nxt[0][:], nxt[1][:], lr[:], li[:], cur[0][:], cur[1][:], t1[:], t2[:])
            ss = sb(sc, "ss", [128, 7, 512], F32)
            P.dma("ssms", ss[:].rearrange("p a b -> p (a b)"), ssmS[:, :], writes=[K])
            lrs, lis, crs, cis = lam_calc(sc, 512, ss[:, 0, :], ss[:, 1, :], ss[:, 2, :], "S_")
            cmul(g[0][0][:], g[0][1][:], crs[:], cis[:], ss[:, 5, :], ss[:, 6, :], t1[:], t2[:])
            for part in range(2):
                P.op("dve", lambda e, part=part: e.tensor_copy(out=BSb[:, part, :], in_=g[0][part][:]), reads=[K], writes=[K])
            pw = [[sb(sc, f"pw{i}{j}", [128, 512], F32) for j in range(2)] for i in range(2)]
            P.op("dve", lambda e: e.memset(pw[0][0][:], 1.0), reads=[K], writes=[K])
            P.op("dve", lambda e: e.memset(pw[0][1][:], 0.0), reads=[K], writes=[K])
            for e_ in range(LC + 1):
                cur, nxt = pw[e_ % 2], pw[(e_ + 1) % 2]
                cmul(g[1][0][:], g[1][1][:], ss[:, 3, :], ss[:, 4, :], cur[0][:], cur[1][:], t1[:], t2[:])
                P.op("dve", lambda e, e_=e_: e.tensor_copy(out=HSb[:, e_, 0, :], in_=g[1][0][:]), reads=[K], writes=[K])
                TS(HSb[:, e_, 1, :], g[1][1][:], -1.0, None, ALU.mult, None, K)
                if e_ < LC:
                    cmul(nxt[0][:], nxt[1][:], lrs[:], lis[:], cur[0][:], cur[1][:], t1[:], t2[:])
            lamL = pw[LC % 2]
            v32 = lambda tl: tl[:].rearrange("p (q c) -> p q c", c=16)[:, :, 0]
            for part in range(2):
                P.op("dve", lambda e, part=part: e.tensor_copy(out=LAMS[:, part, :], in_=v32(lamL[part])), reads=[K], writes=[K])
                P.op("dve", lambda e, part=part: e.tensor_copy(out=DL[:, 0, part, :], in_=v32(lamL[part])), reads=[K], writes=[K])
            d1 = sb(sc, "d1", [128, 32], F32)
            d2 = sb(sc, "d2", [128, 32], F32)
            for k in range(10):
                cmul(DL[:, k + 1, 0, :], DL[:, k + 1, 1, :], DL[:, k, 0, :], DL[:, k, 1, :], DL[:, k, 0, :], DL[:, k, 1, :], d1[:], d2[:])
            P.barrier()

        HZ = sb(s3, "HZ", [128, 4, 2, LC + 1, 128], BF16)
        BZ = sb(s3, "BZ", [128, 4, 2, 128], BF16)
        KTb = sb(s3, "KTb", [128, LC, 128], BF16)
        uown = sb(s3, "uown", [128, OWNCH * LC], BF16)
        xpv = sb(s3, "xpv", [128, 4, 2, OWNCH], BF16)
        dm = sb(s3, "dm", [128, 128], F32)
        UZ = [sb(s3, f"UZ{i}", [128, S], BF16) for i in range(2)]
        Wr = [sb(s3, f"Wr{i}", [128, 2, CTX], F32) for i in range(2)]
        junk = [sb(s3, f"junk{i}", [128, CTX], F32) for i in range(2)]
        acc8 = [sb(s3, f"acc8{i}", [128, 8], F32) for i in range(2)]
        tcol2 = [sb(s3, f"tcol2{i}", [128, 130], F32) for i in range(2)]
        xs = [sb(s3, f"xs{i}", [128, 2], F32) for i in range(2)]
        scn = [[sb(s3, f"scn{c}{i}", [128, 2, OWNCH], F32) for i in range(2)] for c in range(2)]
        tcol = [sb(s3, f"tcol{i}", [128, 512], F32) for i in range(2)]
        TC = "tilec"
        HZK = lambda ql, part: [("HZ", ql, gl, part) for gl in range(2)]
        BZK = lambda ql, part: [("BZ", ql, gl, part) for gl in range(2)]
        P.op("pool", lambda e: e.memset(HZ[:].rearrange("p a b c d -> p (a b c d)"), 0.0), writes=[k_ for ql in range(4) for part in range(2) for k_ in HZK(ql, part)])
        P.op("pool", lambda e: e.memset(BZ[:].rearrange("p a b d -> p (a b d)"), 0.0), writes=[k_ for ql in range(4) for part in range(2) for k_ in BZK(ql, part)])

        def pair_chain(t, ql, c):
            pr = 4 * t + ql
            vb = 4 if c == 0 else 0
            Wr_, junk_, acc_, xs_, scn_, tcol_, UZ_ = Wr[c], junk[c], acc8[c], xs[c], scn[c], tcol[c], UZ[c]
            WRr, WRi = ("wr", c, 0), ("wr", c, 1)
            TCk = [("tc", c, i) for i in range(4)]
            tcs = [tcol_[:, i * 130:(i + 1) * 130] for i in range(3)] + [tcol_[:, 390:512]]

            def dve(fn, r, w):
                P.op("dve", fn, reads=r, writes=w)

            def act_mul(o, a_, col, r, w):
                P.op("act", lambda e: e.activation(out=o, in_=a_, func=AF.Identity, scale=col), reads=r, writes=w)

            dve(lambda e: e.memset(Wr_[:, 0, CTX - 1:CTX], 1.0), [], [WRr])
            yield
            dve(lambda e: e.memset(Wr_[:, 1, CTX - 1:CTX], 0.0), [], [WRi])
            yield
            have = 1
            k = 0
            while have < CTX:
                nn = min(have, CTX - have)
                src_lo = CTX - nn
                dst_lo = CTX - have - nn
                pr_c = DL[:, k, 0, pr:pr + 1]
                pi_c = DL[:, k, 1, pr:pr + 1]
                sr = Wr_[:, 0, src_lo:src_lo + nn]
                si = Wr_[:, 1, src_lo:src_lo + nn]
                t0 = junk_[:, 0:nn] if nn > 122 else tcs[0][:, 0:nn]
                t1 = junk_[:, 512:512 + nn] if nn > 122 else tcs[1][:, 0:nn]
                k0 = ("jk", c, 0) if nn > 122 else TCk[0]
                k1 = ("jk", c, 1) if nn > 122 else TCk[1]
                act_mul(t0, si, pi_c, [WRi], [k0])
                yield
                act_mul(t1, si, pr_c, [WRi], [k1])
                yield
                dve(lambda e, dst_lo=dst_lo, nn=nn, sr=sr, pr_c=pr_c, t0=t0: e.scalar_tensor_tensor(
                    out=Wr_[:, 0, dst_lo:dst_lo + nn], in0=sr, scalar=pr_c, in1=t0, op0=ALU.mult, op1=ALU.subtract), [WRr, k0], [WRr])
                yield
                dve(lambda e, dst_lo=dst_lo, nn=nn, sr=sr, pi_c=pi_c, t1=t1: e.scalar_tensor_tensor(
                    out=Wr_[:, 1, dst_lo:dst_lo + nn], in0=sr, scalar=pi_c, in1=t1, op0=ALU.mult, op1=ALU.add), [WRr, k1], [WRi])
                yield
                have += nn
                k += 1
            for gl in range(2):
                gp = 2 * ql + gl
                P.op("pool", lambda e: e.memset(UZ_[:], 0.0), reads=[("UZ", c)], writes=[("UZ", c)])
                P.dma(f"uz{c}", UZ_[gp * 16:(gp + 1) * 16, :], UT_scr[t, gp * 16:(gp + 1) * 16, :], reads=[], writes=[("UZ", c)])
                uzv = UZ_[:].rearrange("p (m l) -> p m l", l=LC)
                for part in range(2):
                    for half in range(2):
                        b = vb + part * 2 + half
                        for e_ in range(LC):
                            P.op("pe", lambda e, part=part, gl=gl, half=half, b=b, e_=e_, uzv=uzv: e.matmul(
                                ps[b][gl * 64:(gl + 1) * 64, :], GFb[:, e_, part, t * 64:(t + 1) * 64],
                                uzv[:, half * 512:(half + 1) * 512, LC - 1 - e_], start=(e_ == 0), stop=(e_ == LC - 1), skip_group_check=True),
                                reads=[("UZ", c)], writes=[psk[b]])
                yield
            vbanks = psk[vb:vb + 4]

            def vsl(part, lo, hi):
                return ps[vb + part * 2 + lo // 512][:, lo % 512:(hi - 1) % 512 + 1]
            segs = [(0, 512, 0), (512, CTX, 512)]
            WRk = [WRr, WRi]
            for idx, (wp, vp) in enumerate([(0, 0), (1, 1), (0, 1), (1, 0)]):
                for si_, (a_, bnd, joff) in enumerate(segs):
                    jk = ("jk", c, si_)
                    dve(lambda e, a_=a_, bnd=bnd, wp=wp, vp=vp, joff=joff: e.tensor_tensor(
                        out=junk_[:, joff:joff + bnd - a_], in0=Wr_[:, wp, a_:bnd], in1=vsl(vp, a_, bnd), op=ALU.mult),
                        [WRk[wp]] + vbanks, [jk])
                    yield
                    dve(lambda e, a_=a_, bnd=bnd, joff=joff, idx=idx, si_=si_: e.tensor_reduce(
                        out=acc_[:, 4 * si_ + idx:4 * si_ + idx + 1], in_=junk_[:, joff:joff + bnd - a_], axis=mybir.AxisListType.X, op=ALU.add),
                        [jk], [("acc", c, si_, idx)])
                    yield
            akeys = [("acc", c, si_, idx) for si_ in range(2) for idx in range(4)]
            XS = ("xs", c)
            dve(lambda e: e.tensor_tensor(out=acc_[:, 0:4], in0=acc_[:, 0:4], in1=acc_[:, 4:8], op=ALU.add), akeys, [("acc", c, 0, 0)])
            yield
            dve(lambda e: e.tensor_tensor(out=xs_[:, 0:1], in0=acc_[:, 0:1], in1=acc_[:, 1:2], op=ALU.subtract), [("acc", c, 0, 0)], [XS])
            yield
            dve(lambda e: e.tensor_tensor(out=xs_[:, 1:2], in0=acc_[:, 2:3], in1=acc_[:, 3:4], op=ALU.add), [("acc", c, 0, 0), XS], [XS])
            yield
            SK = lambda bi, part: ("scn", c, bi, part)
            for part in range(2):
                dve(lambda e, part=part: e.tensor_copy(out=scn_[0][:, part, 0:512 - (CTX - 512)], in_=ps[vb + 1 + part * 2][:, CTX - 512:512]),
                    vbanks, [SK(0, part)])
                yield
            lr_c = LAMS[:, 0, pr:pr + 1]
            li_c = LAMS[:, 1, pr:pr + 1]
            I0, I1 = ("ini", c, 0), ("ini", c, 1)
            ini = tcs[3]
            act_mul(ini[:, 0:1], xs_[:, 1:2], li_c, [XS], [I0])
            yield
            act_mul(ini[:, 2:3], xs_[:, 1:2], lr_c, [XS], [I1])
            yield
            dve(lambda e: e.scalar_tensor_tensor(out=ini[:, 1:2], in0=xs_[:, 0:1], scalar=lr_c, in1=ini[:, 0:1], op0=ALU.mult, op1=ALU.subtract), [XS, I0], [I0])
            yield
            dve(lambda e: e.scalar_tensor_tensor(out=ini[:, 3:4], in0=xs_[:, 0:1], scalar=li_c, in1=ini[:, 2:3], op0=ALU.mult, op1=ALU.add), [XS, I1], [I1])
            yield
            dve(lambda e: e.tensor_tensor(out=scn_[0][:, 0, 0:1], in0=scn_[0][:, 0, 0:1], in1=ini[:, 1:2], op=ALU.add), [I0, SK(0, 0)], [SK(0, 0)])
            yield
            dve(lambda e: e.tensor_tensor(out=scn_[0][:, 1, 0:1], in0=scn_[0][:, 1, 0:1], in1=ini[:, 3:4], op=ALU.add), [I1, SK(0, 1)], [SK(0, 1)])
            yield
            cur = 0
            sh = 1
            k = 0
            while sh < OWNCH:
                A, B = scn_[cur], scn_[1 - cur]
                ai, bi = cur, 1 - cur
                nn = OWNCH - sh
                pr_c = DL[:, k, 0, pr:pr + 1]
                pi_c = DL[:, k, 1, pr:pr + 1]
                for part in range(2):
                    P.op("act", lambda e, part=part, A=A, B=B, sh=sh: e.activation(out=B[:, part, 0:sh], in_=A[:, part, 0:sh], func=AF.Identity),
                         reads=[SK(ai, part)], writes=[SK(bi, part)])
                    yield
                act_mul(tcs[1][:, 0:nn], A[:, 1, 0:nn], pi_c, [SK(ai, 1)], [TCk[1]])
                yield
                act_mul(tcs[2][:, 0:nn], A[:, 0, 0:nn], pi_c, [SK(ai, 0)], [TCk[2]])
                yield
                dve(lambda e, A=A, nn=nn, sh=sh, pr_c=pr_c: e.scalar_tensor_tensor(out=tcs[0][:, 0:nn], in0=A[:, 0, 0:nn], scalar=pr_c, in1=A[:, 0, sh:OWNCH],
                                                                                   op0=ALU.mult, op1=ALU.add), [SK(ai, 0)], [TCk[0]])
                yield
                dve(lambda e, A=A, nn=nn, sh=sh, pr_c=pr_c: e.scalar_tensor_tensor(out=tcol2[c][:, 0:nn], in0=A[:, 1, 0:nn], scalar=pr_c, in1=A[:, 1, sh:OWNCH],
                                                                                   op0=ALU.mult, op1=ALU.add), [SK(ai, 1)], [TCk[3]])
                yield
                dve(lambda e, B=B, nn=nn, sh=sh: e.tensor_tensor(out=B[:, 0, sh:OWNCH], in0=tcs[0][:, 0:nn], in1=tcs[1][:, 0:nn], op=ALU.subtract),
                    [TCk[0], TCk[1]], [SK(bi, 0)])
                yield
                dve(lambda e, B=B, nn=nn, sh=sh: e.tensor_tensor(out=B[:, 1, sh:OWNCH], in0=tcol2[c][:, 0:nn], in1=tcs[2][:, 0:nn], op=ALU.add),
                    [TCk[3], TCk[2]], [SK(bi, 1)])
                yield
                cur = 1 - cur
                sh *= 2
                k += 1
            fin = scn_[cur]
            for part in range(2):
                P.op("act", lambda e, part=part: e.activation(out=xpv[:, ql, part, 0:1], in_=xs_[:, part:part + 1], func=AF.Identity),
                     reads=[XS], writes=[("xpv", ql, part, 0)])
                yield
                P.op("act", lambda e, part=part: e.activation(out=xpv[:, ql, part, 1:OWNCH], in_=fin[:, part, 0:OWNCH - 1], func=AF.Identity),
                     reads=[SK(cur, part)], writes=[("xpv", ql, part, 1)])
                yield

        for t in range(8):
            for ql in range(4):
                pr = 4 * t + ql
                for gl in range(2):
                    r0, r1 = gl * 64, (gl + 1) * 64
                    c0 = 32 * ql + 16 * gl
                    for part in range(2):
                        P.op("dve", lambda e, ql=ql, part=part, r0=r0, r1=r1, c0=c0, pr=pr: e.tensor_copy(
                            out=HZ[r0:r1, ql, part, :, c0:c0 + 16], in_=HSb[r0:r1, :, part, pr * 16:(pr + 1) * 16]), reads=[], writes=[("HZ", ql, gl, part)])
                        P.op("dve", lambda e, ql=ql, part=part, r0=r0, r1=r1, c0=c0, pr=pr: e.tensor_copy(
                            out=BZ[r0:r1, ql, part, c0:c0 + 16], in_=BSb[r0:r1, part, pr * 16:(pr + 1) * 16]), reads=[], writes=[("BZ", ql, gl, part)])
            P.op("dve", lambda e, t=t: e.tensor_scalar(out=dm[:], in0=ident[:], scalar1=vec[:, 128 + t:129 + t], scalar2=None, op0=ALU.mult),
                 reads=["vec", "ident"], writes=["dm"])
            for d in range(LC):
                b = d % 2
                n = 0
                for ql in range(4):
                    for part in range(2):
                        P.op("pe", lambda e, ql=ql, part=part, d=d, b=b, n=n: e.matmul(
                            ps[b][:, 0:128], BZ[:, ql, part, :], HZ[:, ql, part, d, :], start=(n == 0), stop=(n == 7)),
                            reads=HZK(ql, part) + BZK(ql, part), writes=[psk[b]])
                        n += 1
                if d == 0:
                    P.op("dve", lambda e, b=b: e.tensor_tensor(out=KTb[:, 0, :], in0=ps[b][:, 0:128], in1=dm[:], op=ALU.add), reads=["dm", psk[b]], writes=[("KTb", 0)])
                else:
                    P.op("dve", lambda e, b=b, d=d: e.tensor_copy(out=KTb[:, d, :], in_=ps[b][:, 0:128]), reads=[psk[b]], writes=[("KTb", d)])
            P.dma("uown", uown[:], UT_scr[t, :, S - OWNCH * LC:S], writes=["uown"])
            for q0 in (0, 2):
                gens = [pair_chain(t, q0, 0), pair_chain(t, q0 + 1, 1)]
                while gens:
                    for g_ in list(gens):
                        try:
                            next(g_)
                        except StopIteration:
                            gens.remove(g_)
            uov = uown[:].rearrange("p (m l) -> p m l", l=LC)
            xkeys = [("xpv", ql, part, i) for ql in range(4) for part in range(2) for i in range(2)]
            for j in range(LC):
                b = j % 4
                nmm = 8 + j + 1
                n = 0
                for ql in range(4):
                    for part in range(2):
                        P.op("pe", lambda e, ql=ql, part=part, j=j, b=b, n=n, nmm=nmm: e.matmul(
                            ps[b][:, 0:OWNCH], HZ[:, ql, part, j + 1, :], xpv[:, ql, part, :], start=(n == 0), stop=(n == nmm - 1)),
                            reads=HZK(ql, part) + xkeys, writes=[psk[b]])
                        n += 1
                for d in range(j + 1):
                    P.op("pe", lambda e, d=d, j=j, b=b, n=n, nmm=nmm: e.matmul(
                        ps[b][:, 0:OWNCH], KTb[:, d, :], uov[:, :, j - d], start=(n == 0), stop=(n == nmm - 1)),
                        reads=[("KTb", d), "uown"], writes=[psk[b]])
                    n += 1
                P.op("act", lambda e, b=b, t=t, j=j: e.activation(
                    out=ygT[:, t, j:j + (OWNCH - 1) * LC + 1:LC], in_=ps[b][:, 0:OWNCH], func=AF.Gelu_apprx_tanh),
                    reads=[psk[b]], writes=[("ygT", t)])
        P.barrier()
    if STOP_AFTER == 3:
        return
    build_phase4(nc, P, gs, sb, ps, psk, rr, affine_evac, evac_eng, L, ygT)


def build_phase4(nc, P, gs, sb, ps, psk, rr, affine_evac, evac_eng, L, ygT):
    catT, vec, valid, ident, identb, onesb, onesf = (L[k] for k in ("catT", "vec", "valid", "ident", "identb", "onesb", "onesf"))
    G_IN, B_IN, G1, B1, G2, B2, BPG, MISC = (L[k] for k in ("G_IN", "B_IN", "G1", "B1", "G2", "B2", "BPG", "MISC"))
    H0_scr = L["H0_scr"]
    w_glu, w_o, w_up, w_down, w_ple, w_pg, pc, convp, out_d = (L[k] for k in ("w_glu", "w_o", "w_up", "w_down", "w_ple", "w_pg", "pc", "convp", "out_d"))
    OB = QB[1:]

    def nb():
        b = rr["ps"] % 8
        rr["ps"] += 1
        return b

    r1 = sb(gs, "r1", [128, 16, NQ], F32)
    h1b = catT
    mean_t = sb(gs, "mean_t", [128, NQ], F32)
    rstd_t = sb(gs, "rstd_t", [128, NQ], F32)
    sqt = [sb(gs, f"sqt{i}", [128, 512], F32) for i in range(2)]

    def layer_norm_fm(blocks, gcols, bcols, key):
        for (c0, c1) in blocks:
            n = c1 - c0
            bs, bq = nb(), nb()
            for kt in range(16):
                P.op("pe", lambda e, kt=kt, bs=bs, c0=c0, c1=c1, n=n: e.matmul(ps[bs][:, 0:n], onesf[:], r1[:, kt, c0:c1], start=(kt == 0), stop=(kt == 15)),
                     reads=[(key, kt, c0), "onesf"], writes=[psk[bs]])
            for kt in range(16):
                si = kt % 2
                P.op("act", lambda e, kt=kt, si=si, c0=c0, c1=c1, n=n: e.activation(out=sqt[si][:, 0:n], in_=r1[:, kt, c0:c1], func=AF.Square),
                     reads=[(key, kt, c0)], writes=[("sqt", si)])
                P.op("pe", lambda e, kt=kt, si=si, bq=bq, n=n: e.matmul(ps[bq][:, 0:n], onesf[:], sqt[si][:, 0:n], start=(kt == 0), stop=(kt == 15)),
                     reads=[("sqt", si), "onesf"], writes=[psk[bq]])
            P.op("dve", lambda e, bs=bs, c0=c0, c1=c1, n=n: e.tensor_scalar(out=mean_t[:, c0:c1], in0=ps[bs][:, 0:n], scalar1=1.0 / D, scalar2=None, op0=ALU.mult),
                 reads=[psk[bs]], writes=[("mean", c0)])
            P.op("dve", lambda e, c0=c0, c1=c1: e.tensor_tensor(out=rstd_t[:, c0:c1], in0=mean_t[:, c0:c1], in1=mean_t[:, c0:c1], op=ALU.mult),
                 reads=[("mean", c0)], writes=[("rstd", c0)])
            P.op("dve", lambda e, bq=bq, c0=c0, c1=c1, n=n: e.scalar_tensor_tensor(out=rstd_t[:, c0:c1], in0=ps[bq][:, 0:n], scalar=1.0 / D, in1=rstd_t[:, c0:c1],
                                                                                  op0=ALU.mult, op1=ALU.subtract),
                 reads=[psk[bq], ("rstd", c0)], writes=[("rstd", c0)])
            P.op("act", lambda e, c0=c0, c1=c1: e.activation(out=rstd_t[:, c0:c1], in_=rstd_t[:, c0:c1], func=AF.Sqrt, bias=EPS, scale=1.0),
                 reads=[("rstd", c0)], writes=[("rstd", c0)])
            P.op("dve", lambda e, c0=c0, c1=c1: e.reciprocal(out=rstd_t[:, c0:c1], in_=rstd_t[:, c0:c1]), reads=[("rstd", c0)], writes=[("rstd", c0)])
            for kt in range(16):
                P.op("dve", lambda e, kt=kt, c0=c0, c1=c1: e.tensor_tensor(out=r1[:, kt, c0:c1], in0=r1[:, kt, c0:c1], in1=mean_t[:, c0:c1], op=ALU.subtract),
                     reads=[(key, kt, c0), ("mean", c0)], writes=[(key, kt, c0)])
                P.op("dve", lambda e, kt=kt, c0=c0, c1=c1: e.tensor_tensor(out=r1[:, kt, c0:c1], in0=r1[:, kt, c0:c1], in1=rstd_t[:, c0:c1], op=ALU.mult),
                     reads=[(key, kt, c0), ("rstd", c0)], writes=[(key, kt, c0)])
                P.op("dve", lambda e, kt=kt, c0=c0, c1=c1: e.tensor_scalar(out=r1[:, kt, c0:c1], in0=r1[:, kt, c0:c1], scalar1=gcols[:, kt:kt + 1], scalar2=bcols[:, kt:kt + 1],
                                                                          op0=ALU.mult, op1=ALU.add),
                     reads=[(key, kt, c0), "vec"], writes=[(key, kt, c0)])

    def wtile(st, name, n):
        return [sb(st, f"{name}{i}", [128, n, 128], BF16) for i in range(2)]

    with ExitStack() as s4:
      with ExitStack() as s4g:
        Wg = sb(s4g, "Wg", [128, 8, 1024], BF16)
        for kt in range(8):
            P.dma(f"wg{kt % 4}", Wg[:, kt, :], w_glu[kt * 128:(kt + 1) * 128, :], writes=[("Wg", kt)], q="pool")
        sg = [sb(s4g, f"sg{i}", [128, 512], F32) for i in range(2)]
        nsg = 0
        for mt in range(8):
            for (c0, c1) in QB:
                n = c1 - c0
                b = nb()
                for kt in range(8):
                    P.op("pe", lambda e, kt=kt, b=b, mt=mt, c0=c0, c1=c1, n=n: e.matmul(ps[b][:, 0:n], Wg[:, kt, mt * 128:(mt + 1) * 128], ygT[:, kt, c0:c1],
                                                                                         start=(kt == 0), stop=(kt == 7)),
                         reads=[("Wg", kt), ("ygT", kt)], writes=[psk[b]])
                si = nsg % 2
                nsg += 1
                P.op("act", lambda e, b=b, si=si, mt=mt, n=n: e.activation(out=sg[si][:, 0:n], in_=ps[b][:, 0:n], func=AF.Sigmoid, bias=MISC[:, mt:mt + 1], scale=1.0),
                     reads=[psk[b], "vec"], writes=[("sg", si)])
                P.op("dve", lambda e, si=si, mt=mt, c0=c0, c1=c1, n=n: e.tensor_tensor(out=catT[:, 8 + mt, c0:c1], in0=ygT[:, mt, c0:c1], in1=sg[si][:, 0:n], op=ALU.mult),
                     reads=[("sg", si), ("ygT", mt)], writes=[("catT", 8 + mt, c0)])
        P.barrier()
        Wo = [sb(s4g, f"Wo{i}", [128, 16, 512], BF16) for i in range(2)]
        h0m = [sb(s4g, f"h0m{i}", [128, NQ], F32) for i in range(2)]
        wo_v = w_o.rearrange("(kt p) m -> p kt m", p=128)
        for mt in range(16):
            wg_, mi = (mt // 4) % 2, mt % 4
            wi = mt % 2
            if mi == 0:
                for kq in range(4):
                    P.dma(f"wo{wg_}", Wo[wg_][:, kq * 4:(kq + 1) * 4, :], wo_v[:, kq * 4:(kq + 1) * 4, (mt // 4) * 512:(mt // 4 + 1) * 512],
                          writes=[("Wo", wg_)], q="pool")
            P.dma(f"h0m{wi}", h0m[wi][:, HO:NQ], H0_scr[mt, :, HO:NQ], writes=[("h0m", wi)])
            P.op("dve", lambda e, wi=wi, mt=mt: e.tensor_copy(out=h0m[wi][:, 0:HO], in_=L["halo_f"][:, mt, :]), reads=[("h0m", wi)], writes=[("h0m", wi)])
            for (c0, c1) in QB:
                n = c1 - c0
                b = nb()
                for kt in range(16):
                    P.op("pe", lambda e, kt=kt, b=b, wg_=wg_, mi=mi, c0=c0, c1=c1, n=n: e.matmul(ps[b][:, 0:n], Wo[wg_][:, kt, mi * 128:(mi + 1) * 128], catT[:, kt, c0:c1], start=(kt == 0), stop=(kt == 15)),
                         reads=[("Wo", wg_), ("catT", kt, c0)], writes=[psk[b]])
                P.op("dve", lambda e, b=b, wi=wi, mt=mt, c0=c0, c1=c1, n=n: e.scalar_tensor_tensor(out=r1[:, mt, c0:c1], in0=h0m[wi][:, c0:c1], scalar=ALPHA, in1=ps[b][:, 0:n],
                                                                                                  op0=ALU.mult, op1=ALU.add),
                     reads=[psk[b], ("h0m", wi)], writes=[("r1", mt, c0)])
        layer_norm_fm(QB, G1, B1, "r1")
        for kt in range(16):
            P.op("act", lambda e, kt=kt: e.activation(out=h1b[:, kt, :], in_=r1[:, kt, :], func=AF.Identity),
                 reads=[("r1", kt, c0) for (c0, c1) in QB], writes=[("h1b", kt)])
            P.op("dve", lambda e, kt=kt: e.tensor_scalar(out=h1b[:, kt, HO - 2:HO], in0=h1b[:, kt, HO - 2:HO], scalar1=valid[:, 16:17], scalar2=None, op0=ALU.mult),
                 reads=[("h1b", kt), "valid"], writes=[("h1b", kt)])
        P.barrier()

    with ExitStack() as s5:
        pTb = sb(s5, "pTb", [128, 2, OWN], BF16)
        pt = [sb(s5, f"pt{i}", [128, 256], F32) for i in range(2)]
        for tt in range(8):
            pi_ = tt % 2
            P.dma(f"pt{pi_}", pt[pi_][:], pc[tt * 128:(tt + 1) * 128, :], writes=[("pt", pi_)])
            b = nb()
            for k2 in range(2):
                P.op("pe", lambda e, k2=k2, b=b, pi_=pi_: e.transpose(out=ps[b][:, k2 * 128:(k2 + 1) * 128], in_=pt[pi_][:, k2 * 128:(k2 + 1) * 128], identity=ident[:]),
                     reads=[("pt", pi_), "ident"], writes=[psk[b]])
            for k2 in range(2):
                P.op("dve", lambda e, k2=k2, b=b, tt=tt: e.tensor_copy(out=pTb[:, k2, tt * 128:(tt + 1) * 128], in_=ps[b][:, k2 * 128:(k2 + 1) * 128]),
                     reads=[psk[b]], writes=[("pTb", tt)])
        Wpg = [sb(s5, f"Wpg{i}", [128, 16, 512], BF16) for i in range(2)]
        Wpl = [sb(s5, f"Wpl{i}", [128, 2, 512], BF16) for i in range(2)]
        sg = [sb(s5, f"sgp{i}", [128, 512], F32) for i in range(2)]
        wpg_v = w_pg.rearrange("(kt p) m -> p kt m", p=128)
        wpl_v = w_ple.rearrange("(kt p) m -> p kt m", p=128)
        nsg = 0
        for mt in range(16):
            wi, mi = (mt // 4) % 2, mt % 4
            if mi == 0:
                for kq in range(4):
                    P.dma(f"wpg{wi}", Wpg[wi][:, kq * 4:(kq + 1) * 4, :], wpg_v[:, kq * 4:(kq + 1) * 4, (mt // 4) * 512:(mt // 4 + 1) * 512],
                          writes=[("Wpg", wi)], q="pool")
                P.dma(f"wpl{wi}", Wpl[wi][:], wpl_v[:, :, (mt // 4) * 512:(mt // 4 + 1) * 512], writes=[("Wpl", wi)], q="pool")
            for (c0, c1) in OB:
                bg, bp = nb(), nb()
                for kt in range(16):
                    P.op("pe", lambda e, kt=kt, bg=bg, wi=wi, mi=mi, c0=c0, c1=c1: e.matmul(ps[bg][:], Wpg[wi][:, kt, mi * 128:(mi + 1) * 128], h1b[:, kt, c0:c1], start=(kt == 0), stop=(kt == 15)),
                         reads=[("Wpg", wi), ("h1b", kt)], writes=[psk[bg]])
                for k2 in range(2):
                    P.op("pe", lambda e, k2=k2, bp=bp, wi=wi, mi=mi, c0=c0, c1=c1: e.matmul(ps[bp][:], Wpl[wi][:, k2, mi * 128:(mi + 1) * 128], pTb[:, k2, c0 - HO:c1 - HO], start=(k2 == 0), stop=(k2 == 1)),
                         reads=[("Wpl", wi)] + [("pTb", tt) for tt in range(8)], writes=[psk[bp]])
                si = nsg % 2
                nsg += 1
                P.op("act", lambda e, bg=bg, si=si, mt=mt: e.activation(out=sg[si][:], in_=ps[bg][:], func=AF.Sigmoid, bias=BPG[:, mt:mt + 1], scale=1.0),
                     reads=[psk[bg], "vec"], writes=[("sgp", si)])
                P.op("dve", lambda e, bp=bp, si=si: e.tensor_tensor(out=sg[si][:], in0=ps[bp][:], in1=sg[si][:], op=ALU.mult),
                     reads=[psk[bp], ("sgp", si)], writes=[("sgp", si)])
                P.op("dve", lambda e, si=si, mt=mt, c0=c0, c1=c1: e.scalar_tensor_tensor(out=r1[:, mt, c0:c1], in0=r1[:, mt, c0:c1], scalar=ALPHA, in1=sg[si][:],
                                                                                        op0=ALU.mult, op1=ALU.add),
                     reads=[("sgp", si), ("r1", mt, c0)], writes=[("r1", mt, c0)])
        P.barrier()

    with ExitStack() as s6:
        cvp = sb(s6, "cvp", [128, 4, 88], F32)
        P.dma("cvp", cvp[:].rearrange("p a b -> p (a b)"), convp[:, :], writes=["cvp"])
        Wu = [sb(s6, f"Wu{i}", [128, 16, 512], BF16) for i in range(2)]
        Wd = sb(s6, "Wd", [128, 4, D], BF16)
        actb = sb(s6, "actb", [128, 4, OWN], BF16)
        hv = [sb(s6, f"hv{i}", [128, NQ], F32) for i in range(2)]
        cv = [sb(s6, f"cv{i}", [128, OWN], F32) for i in range(2)]
        wu_v = w_up.rearrange("(kt p) m -> p kt m", p=128)
        for jg in range(11):
            for jj in range(4):
                P.dma("wd", Wd[:, jj, :], w_down[(jg * 4 + jj) * 128:(jg * 4 + jj + 1) * 128, :], writes=[("Wd", jj)], q="pool")
            for vg in range(2):
                for kq in range(4):
                    P.dma(f"wu{vg}", Wu[vg][:, kq * 4:(kq + 1) * 4, :], wu_v[:, kq * 4:(kq + 1) * 4, vg * DFF + jg * 512:vg * DFF + (jg + 1) * 512],
                          writes=[("Wu", vg)], q="pool")
            for jj in range(4):
                j = jg * 4 + jj
                for vg in range(2):
                    ft = vg * 44 + j
                    for (c0, c1) in QB:
                        n = c1 - c0
                        b = nb()
                        for kt in range(16):
                            P.op("pe", lambda e, kt=kt, b=b, vg=vg, jj=jj, c0=c0, c1=c1, n=n: e.matmul(ps[b][:, 0:n], Wu[vg][:, kt, jj * 128:(jj + 1) * 128], h1b[:, kt, c0:c1], start=(kt == 0), stop=(kt == 15)),
                                 reads=[("Wu", vg), ("h1b", kt)], writes=[psk[b]])
                        affine_evac(evac_eng(), hv[vg][:, c0:c1], ps[b][:, 0:n], reads=[psk[b]], writes=[("hv", vg)])
                    P.op("dve", lambda e, vg=vg, ft=ft: e.tensor_scalar(out=cv[vg][:], in0=hv[vg][:, HO - 2:HO - 2 + OWN], scalar1=cvp[:, 0, ft:ft + 1], scalar2=cvp[:, 3, ft:ft + 1],
                                                                       op0=ALU.mult, op1=ALU.add),
                         reads=[("hv", vg), "cvp"], writes=[("cv", vg)])
                    for tap in (1, 2):
                        P.op("dve", lambda e, vg=vg, ft=ft, tap=tap: e.scalar_tensor_tensor(out=cv[vg][:], in0=hv[vg][:, HO - 2 + tap:HO - 2 + tap + OWN], scalar=cvp[:, tap, ft:ft + 1], in1=cv[vg][:],
                                                                                             op0=ALU.mult, op1=ALU.add),
                             reads=[("hv", vg), ("cv", vg), "cvp"], writes=[("cv", vg)])
                P.op("act", lambda e: e.activation(out=cv[1][:], in_=cv[1][:], func=AF.Gelu_apprx_tanh), reads=[("cv", 1)], writes=[("cv", 1)])
                P.op("dve", lambda e, jj=jj: e.tensor_tensor(out=actb[:, jj, :], in0=cv[0][:], in1=cv[1][:], op=ALU.mult),
                     reads=[("cv", 0), ("cv", 1)], writes=[("actb", jj)])
            for mt in range(16):
                for (c0, c1) in OB:
                    b = nb()
                    for jj in range(4):
                        P.op("pe", lambda e, jj=jj, b=b, mt=mt, c0=c0, c1=c1: e.matmul(ps[b][:], Wd[:, jj, mt * 128:(mt + 1) * 128], actb[:, jj, c0 - HO:c1 - HO], start=(jj == 0), stop=(jj == 3)),
                             reads=[("Wd", jj), ("actb", jj)], writes=[psk[b]])
                    P.op("dve", lambda e, b=b, mt=mt, c0=c0, c1=c1: e.tensor_tensor(out=r1[:, mt, c0:c1], in0=r1[:, mt, c0:c1], in1=ps[b][:], op=ALU.add),
                         reads=[psk[b], ("r1", mt, c0)], writes=[("r1", mt, c0)])
        P.barrier()

    layer_norm_fm(OB, G2, B2, "r1")
    with ExitStack() as s7:
        ot = [sb(s7, f"ot{i}", [128, D], F32) for i in range(2)]
        for tt in range(8):
            oi = tt % 2
            for k4 in range(4):
                b = nb()
                for j in range(4):
                    kt = k4 * 4 + j
                    P.op("pe", lambda e, kt=kt, j=j, b=b, tt=tt: e.transpose(out=ps[b][:, j * 128:(j + 1) * 128], in_=r1[:, kt, HO + tt * 128:HO + (tt + 1) * 128], identity=ident[:]),
                         reads=[("r1", kt, c0) for (c0, c1) in OB] + ["ident"], writes=[psk[b]])
                affine_evac(evac_eng(), ot[oi][:, k4 * 512:(k4 + 1) * 512], ps[b][:], reads=[psk[b]], writes=[("ot", oi)])
            P.dma(f"ot{oi}", out_d[tt * 128:(tt + 1) * 128, :], ot[oi][:], reads=[("ot", oi)], writes=[("out", tt)])
        P.barrier()


_CACHE = {}


def _cols(v, n):
    return np.ascontiguousarray(np.asarray(v, np.float32).reshape(n, 128).T)


def kernel(x, p, ln_in_g, ln_in_b, w_in, lambda_q1, lambda_k1, lambda_q2, lambda_k2, g_subln, a_re, a_im, log_dt,
           b_re, b_im, c_re, c_im, d_skip, w_glu, b_glu, w_o, ln1_g, ln1_b, w_up, conv_w, conv_b, w_down, w_ple,
           w_pg, b_pg, ln2_g, ln2_b):
    f = lambda a: np.asarray(a, np.float32)
    x = f(x)[0]
    pp = f(p)[0, 0]
    if "nc" not in _CACHE:
        _CACHE["nc"] = build_program()
    nc = _CACHE["nc"]
    vecs = np.zeros((128, 144), np.float32)
    for i, v in enumerate([ln_in_g, ln_in_b, f(ln1_g)[0], f(ln1_b)[0], f(ln2_g)[0], f(ln2_b)[0], f(b_pg)[0]]):
        vecs[:, i * 16:(i + 1) * 16] = _cols(v, 16)
    vecs[:, 112:120] = _cols(f(b_glu)[0], 8)
    vecs[:, 120] = f(g_subln)[0]
    vecs[:, 128:136] = _cols(f(d_skip)[0], 8)
    convp = np.zeros((128, 4, 88), np.float32)
    for t in range(3):
        convp[:, t, :] = _cols(f(conv_w)[0, t], 88)
    convp[:, 3, :] = _cols(f(conv_b)[0], 88)
    lamv = np.stack([np.broadcast_to(f(v)[0], (128, 64)) for v in (lambda_q1, lambda_k1, lambda_q2, lambda_k2)], axis=1)
    ar, ai, ldt = f(a_re)[0], f(a_im)[0], f(log_dt)[0]
    br, bi, cr, ci = f(b_re)[0], f(b_im)[0], f(c_re)[0], f(c_im)[0]
    def clay(a_gn):
        a = a_gn.reshape(8, 8, 1, 64)
        a = np.broadcast_to(a, (8, 8, 16, 64))
        return a.transpose(1, 2, 0, 3).reshape(128, 512)
    def clay_b(b_gnc):
        a = b_gnc.reshape(8, 8, 64, 16).transpose(1, 3, 0, 2)
        return a.reshape(128, 512)
    ssmC = np.stack([clay(ar), clay(ai), clay_b(br), clay_b(bi), clay(np.broadcast_to(ldt[:, None], (64, 64)))], axis=1)
    def slay(a_gn):
        a = a_gn.reshape(32, 2, 64, 1)
        a = np.broadcast_to(a, (32, 2, 64, 16))
        return a.transpose(1, 2, 0, 3).reshape(128, 512)
    def slay_c(c_gcn):
        a = c_gcn.reshape(32, 2, 16, 64).transpose(1, 3, 0, 2)
        return a.reshape(128, 512)
    def slay_b(b_gnc):
        a = b_gnc.reshape(32, 2, 64, 16).transpose(1, 2, 0, 3)
        return a.reshape(128, 512)
    ssmS = np.stack([slay(ar), slay(ai), slay(np.broadcast_to(ldt[:, None], (64, 64))), slay_c(cr), slay_c(ci), slay_b(br), slay_b(bi)], axis=1)
    ident = np.eye(128, dtype=np.float32)
    slopes = 2.0 ** (-np.arange(1, 9, dtype=np.float64))
    kk = np.arange(128)[:, None]
    qq = np.arange(128)[None, :]
    dtile = np.zeros((128, 8, 128), np.float32)
    for h in range(8):
        dmat = 8.0 * (-slopes[h] * np.abs(qq - kk) + slopes[h] * (qq - 127))
        dmat = np.where((kk // 64) <= (qq // 64), dmat, NEG)
        dtile[:, h, :] = dmat
    shared = {
        "w_in": f(w_in)[0], "w_glu": f(w_glu)[0], "w_o": f(w_o)[0], "w_up": f(w_up)[0], "w_down": f(w_down)[0],
        "w_ple": f(w_ple)[0], "w_pg": f(w_pg)[0], "vecs": vecs, "convp": convp.reshape(128, -1),
        "lamv": np.ascontiguousarray(lamv.reshape(128, -1)), "ident": ident, "dtile": dtile.reshape(128, -1),
        "ssmC": np.ascontiguousarray(ssmC.reshape(128, -1)), "ssmS": np.ascontiguousarray(ssmS.reshape(128, -1)),
    }
    in_maps = []
    origins = [7167] + [7168 + 128 * i + 127 for i in range(8)]
    diagt = [55] + [56 + i for i in range(8)]
    for c in range(NCORES):
        start = S - OWN * (c + 1)
        xcx = np.zeros((S, D), np.float32)
        xcx[start:] = x[:OWN * (c + 1)]
        kb = np.zeros((128, 8, 64, 9), np.float32)
        kpos = (np.arange(64)[None, :] * 128 + np.arange(128)[:, None]).astype(np.float64)
        vmask = kpos >= start
        for h in range(8):
            per = GW[h] // 128
            for s_ in range(9):
                if s_ == 0:
                    og = origins[0]
                else:
                    qt = s_ - 1
                    og = origins[1 + (qt // per) * per + per - 1]
                val = slopes[h] * (kpos - og)
                val[:, diagt[s_]] = -slopes[h] * (og - origins[s_])
                val = np.where(vmask, np.minimum(val, 0.0), NEG)
                kb[:, h, :, s_] = val
        vd = np.zeros((128, 17), np.float32)
        for tb in range(16):
            vd[:, tb] = 1.0 if tb * 512 >= start else 0.0
        vd[:, 16] = 1.0 if c > 0 else 0.0
        m = dict(shared)
        m.update({"xc": xcx, "pc": np.ascontiguousarray(pp[c * OWN:(c + 1) * OWN]), "kbias": kb.reshape(128, -1), "valid": vd})
        in_maps.append(m)
    res = run_bass_kernel_spmd(nc, in_maps, core_ids=list(range(NCORES)))
    out = np.concatenate([np.asarray(res.results[c]["out"], np.float32) for c in range(NCORES)], axis=0)
    return out[None].astype(np.float32)
```
